# Optimizing a Trainium2 kernel written in Bass

```python
import math
import jax, jax.numpy as jnp
from jax import lax
import numpy as np

D_MODEL = 1024
BATCH = 8
SEQ = 4096
DEPTH = 4

NORM_EPS = 1e-6
DN_HEAD_DIM = 128
DN_WIDTH = D_MODEL // 2
DN_HEADS = DN_WIDTH // DN_HEAD_DIM
DN_CHUNK = 64
DN_CONV = 4
GM_GROUP_DIM = 64
GM_WIDTH = D_MODEL // 4
GM_GROUPS = GM_WIDTH // GM_GROUP_DIM
GM_CHUNK = 128
SW_HEAD_DIM = 64
SW_WIDTH = D_MODEL // 4
SW_HEADS = SW_WIDTH // SW_HEAD_DIM
SW_PATTERNS = ((128, 1), (512, 4), (2048, 16))
SW_BLOCK = 128
ROPE_THETA = 500000.0
ROPE_DIM = SW_HEAD_DIM // 4
MIX_WIDTH = DN_WIDTH + GM_WIDTH + SW_WIDTH
IN_SPLITS = (3 * DN_WIDTH, 4 * DN_WIDTH, 4 * DN_WIDTH + DN_HEADS, 4 * DN_WIDTH + 2 * DN_HEADS,
             4 * DN_WIDTH + 2 * DN_HEADS + 2 * GM_WIDTH)
IN_WIDTH = IN_SPLITS[-1] + len(SW_PATTERNS) * 3 * SW_WIDTH
FFN_HIDDEN = -(-8 * D_MODEL // (3 * 256)) * 256

kernel_name = "hybrid_parallel_heads_decoder"


def rms_norm(x, w):
    xf = x.astype(jnp.float32)
    y = xf * lax.rsqrt(jnp.mean(xf * xf, axis=-1, keepdims=True) + NORM_EPS)
    return (y * w.astype(jnp.float32)).astype(x.dtype)


def layer_norm(x, g, b):
    xf = x.astype(jnp.float32)
    mu = jnp.mean(xf, axis=-1, keepdims=True)
    xc = xf - mu
    var = jnp.mean(xc * xc, axis=-1, keepdims=True)
    return (xc * lax.rsqrt(var + NORM_EPS) * g.astype(jnp.float32) + b.astype(jnp.float32)).astype(x.dtype)


def l2_norm(x):
    return x * lax.rsqrt(jnp.sum(x * x, axis=-1, keepdims=True) + NORM_EPS)


def rotary_tables(seq):
    inv = ROPE_THETA ** (-jnp.arange(0, ROPE_DIM, 2, dtype=jnp.float32) / ROPE_DIM)
    ang = jnp.arange(seq, dtype=jnp.float32)[:, None] * inv[None, :]
    return jnp.cos(ang), jnp.sin(ang)


def apply_partial_rotary(x, cos, sin):
    half = ROPE_DIM // 2
    x1, x2, xp = x[..., :half], x[..., half:ROPE_DIM], x[..., ROPE_DIM:]
    c = cos[None, :, None, :]
    s = sin[None, :, None, :]
    return jnp.concatenate([x1 * c - x2 * s, x2 * c + x1 * s, xp], axis=-1)


def causal_dwconv_silu(x, w):
    k, ch = w.shape
    y = lax.conv_general_dilated(x, w[:, None, :].astype(x.dtype), window_strides=(1,),
                                 padding=[(k - 1, 0)], dimension_numbers=('NWC', 'WIO', 'NWC'),
                                 feature_group_count=ch)
    return jax.nn.silu(y)


def gated_delta_rule(q, k, v, g, beta):
    bsz, h, t, dk = q.shape
    dv = v.shape[-1]
    n = t // DN_CHUNK
    q = q.reshape(bsz, h, n, DN_CHUNK, dk) * (dk ** -0.5)
    k = k.reshape(bsz, h, n, DN_CHUNK, dk)
    v = v.reshape(bsz, h, n, DN_CHUNK, dv)
    beta = beta.reshape(bsz, h, n, DN_CHUNK)
    gcum = jnp.cumsum(g.reshape(bsz, h, n, DN_CHUNK), axis=-1)
    idx = jnp.arange(DN_CHUNK)
    causal = idx[:, None] >= idx[None, :]
    strict = idx[:, None] > idx[None, :]
    decay = jnp.exp(jnp.where(causal, gcum[..., :, None] - gcum[..., None, :], -jnp.inf))
    kb = k * beta[..., None]
    a_low = jnp.where(strict, jnp.einsum('bhnid,bhnjd->bhnij', kb, k) * decay, 0.0)
    eye = jnp.eye(DN_CHUNK, dtype=q.dtype)
    rhs = jnp.concatenate([v * beta[..., None], kb * jnp.exp(gcum)[..., None]], axis=-1)
    sol = lax.linalg.triangular_solve(a_low + eye, rhs, left_side=True, lower=True)
    u, w = sol[..., :dv], sol[..., dv:]
    a_qk = jnp.einsum('bhnid,bhnjd->bhnij', q, k) * decay
    q_dec = q * jnp.exp(gcum)[..., None]
    g_last = gcum[..., -1]
    k_dec = k * jnp.exp(g_last[..., None] - gcum)[..., None]

    def step(state, xs):
        u_c, w_c, qd_c, a_c, kd_c, gl_c = xs
        v_new = u_c - jnp.einsum('bhcd,bhde->bhce', w_c, state)
        o = jnp.einsum('bhcd,bhde->bhce', qd_c, state) + jnp.einsum('bhij,bhje->bhie', a_c, v_new)
        state = state * jnp.exp(gl_c)[..., None, None] + jnp.einsum('bhcd,bhce->bhde', kd_c, v_new)
        return state, o

    xs = (jnp.moveaxis(u, 2, 0), jnp.moveaxis(w, 2, 0), jnp.moveaxis(q_dec, 2, 0),
          jnp.moveaxis(a_qk, 2, 0), jnp.moveaxis(k_dec, 2, 0), jnp.moveaxis(g_last, 2, 0))
    s0 = jnp.zeros((bsz, h, dk, dv), q.dtype)
    _, o = lax.scan(step, s0, xs)
    return jnp.moveaxis(o, 0, 2).reshape(bsz, h, t, dv)


def deltanet_mixer(qkv, z, a, b, conv_w, a_log, dt_bias, out_norm_w):
    bsz, t, _ = qkv.shape
    qkv = causal_dwconv_silu(qkv, conv_w).astype(jnp.float32)
    q, k, v = jnp.split(qkv, 3, axis=-1)
    heads = lambda y: y.reshape(bsz, t, DN_HEADS, DN_HEAD_DIM).transpose(0, 2, 1, 3)
    q, k, v = l2_norm(heads(q)), l2_norm(heads(k)), heads(v)
    g = -jnp.exp(a_log.astype(jnp.float32)) * jax.nn.softplus(a.astype(jnp.float32) + dt_bias.astype(jnp.float32))
    beta = jax.nn.sigmoid(b.astype(jnp.float32))
    o = gated_delta_rule(q, k, v, g.transpose(0, 2, 1), beta.transpose(0, 2, 1))
    o = o.transpose(0, 2, 1, 3)
    zg = z.astype(jnp.float32).reshape(bsz, t, DN_HEADS, DN_HEAD_DIM)
    o = rms_norm(o, out_norm_w) * jax.nn.silu(zg)
    return o.reshape(bsz, t, DN_WIDTH)


def spatial_gating_mixer(uv, ln_g, ln_b, w_s, b_s):
    bsz, t, _ = uv.shape
    zz = jax.nn.gelu(uv.astype(jnp.float32), approximate=False)
    u, v = jnp.split(zz, 2, axis=-1)
    v = layer_norm(v, ln_g, ln_b)
    n = t // GM_CHUNK
    v = v.reshape(bsz, n, GM_CHUNK, GM_GROUPS, GM_GROUP_DIM)
    causal = jnp.tril(jnp.ones((GM_CHUNK, GM_CHUNK), dtype=bool))
    ws = jnp.where(causal, w_s.astype(jnp.float32), 0.0)
    sv = jnp.einsum('gij,bnjgc->bnigc', ws, v) + b_s.astype(jnp.float32).T[None, None, :, :, None]
    return u * sv.reshape(bsz, t, GM_WIDTH)


def dilated_window_attention(q, k, v, dilation, span):
    bsz, t, h, d = q.shape
    length = t // dilation
    nb = -(-length // SW_BLOCK)
    lp = nb * SW_BLOCK

    def to_sub(y):
        y = y.reshape(bsz, length, dilation, h, d).transpose(0, 2, 3, 1, 4)
        y = jnp.pad(y, ((0, 0), (0, 0), (0, 0), (0, lp - length), (0, 0)))
        return y.reshape(bsz, dilation, h, nb, SW_BLOCK, d)

    def with_prev(y):
        prev = jnp.pad(y, ((0, 0), (0, 0), (0, 0), (1, 0), (0, 0), (0, 0)))[:, :, :, :-1]
        return jnp.concatenate([prev, y], axis=-2)

    qs = to_sub(q)
    kk, vv = with_prev(to_sub(k)), with_prev(to_sub(v))
    s = jnp.einsum('brhnie,brhnje->brhnij', qs, kk) * (d ** -0.5)
    blk = jnp.arange(nb)[:, None, None] * SW_BLOCK
    qpos = blk + jnp.arange(SW_BLOCK)[None, :, None]
    kpos = blk - SW_BLOCK + jnp.arange(2 * SW_BLOCK)[None, None, :]
    dist = qpos - kpos
    valid = (dist >= 0) & (dist <= span) & (kpos >= 0)
    s = jnp.where(valid, s, -jnp.inf)
    m = jnp.max(s, axis=-1, keepdims=True)
    p = jnp.exp(s - m)
    l = jnp.sum(p, axis=-1, keepdims=True)
    o = jnp.einsum('brhnij,brhnje->brhnie', p, vv) / l
    lse = (m + jnp.log(l))[..., 0]
    o = o.reshape(bsz, dilation, h, lp, d)[:, :, :, :length]
    o = o.transpose(0, 3, 1, 2, 4).reshape(bsz, t, h, d)
    lse = lse.reshape(bsz, dilation, h, lp)[:, :, :, :length]
    lse = lse.transpose(0, 3, 1, 2).reshape(bsz, t, h)
    return o, lse


def dilated_attention_mixer(qkv, q_norm_w, k_norm_w, cos, sin):
    bsz, t, _ = qkv.shape
    parts = qkv.astype(jnp.float32).reshape(bsz, t, len(SW_PATTERNS), 3, SW_HEADS, SW_HEAD_DIM)
    outs, lses = [], []
    for gi, (window, dilation) in enumerate(SW_PATTERNS):
        q = apply_partial_rotary(rms_norm(parts[:, :, gi, 0], q_norm_w), cos, sin)
        k = apply_partial_rotary(rms_norm(parts[:, :, gi, 1], k_norm_w), cos, sin)
        o, lse = dilated_window_attention(q, k, parts[:, :, gi, 2], dilation, window // dilation)
        outs.append(o)
        lses.append(lse)
    o = jnp.stack(outs, axis=0)
    wts = jax.nn.softmax(jnp.stack(lses, axis=0), axis=0)
    return jnp.sum(wts[..., None] * o, axis=0).reshape(bsz, t, SW_WIDTH)


def setup_inputs(seed: int = 0) -> dict:
    key = jax.random.key(seed)
    ks = jax.random.split(key, 20)
    f32 = jnp.float32
    nl = DEPTH

    def nrm(k, shape, scale):
        return jax.random.normal(k, shape, f32) * scale

    dt = jnp.exp(jax.random.uniform(ks[10], (nl, DN_HEADS), f32, math.log(1e-3), math.log(1e-1)))
    return {
        'x': nrm(ks[0], (BATCH, SEQ, D_MODEL), 1.0),
        'c': nrm(ks[1], (BATCH, D_MODEL), 1.0),
        'w_mod': nrm(ks[2], (nl, D_MODEL, 6 * D_MODEL), 0.5 * D_MODEL ** -0.5),
        'b_mod': nrm(ks[3], (nl, 6 * D_MODEL), 0.01),
        'mix_norm_w': 1.0 + nrm(ks[4], (nl, D_MODEL), 0.02),
        'ffn_norm_w': 1.0 + nrm(ks[5], (nl, D_MODEL), 0.02),
        'w_in': nrm(ks[6], (nl, D_MODEL, IN_WIDTH), D_MODEL ** -0.5),
        'w_out': nrm(ks[7], (nl, MIX_WIDTH, D_MODEL), MIX_WIDTH ** -0.5),
        'dn_conv_w': nrm(ks[8], (nl, DN_CONV, 3 * DN_WIDTH), DN_CONV ** -0.5),
        'dn_a_log': jnp.log(jax.random.uniform(ks[9], (nl, DN_HEADS), f32, 1.0, 16.0)),
        'dn_dt_bias': dt + jnp.log(-jnp.expm1(-dt)),
        'dn_out_norm_w': 1.0 + nrm(ks[11], (nl, DN_HEAD_DIM), 0.02),
        'gm_ln_g': 1.0 + nrm(ks[12], (nl, GM_WIDTH), 0.02),
        'gm_ln_b': nrm(ks[13], (nl, GM_WIDTH), 0.02),
        'gm_w_s': nrm(ks[14], (nl, GM_GROUPS, GM_CHUNK, GM_CHUNK), GM_CHUNK ** -0.5),
        'gm_b_s': 1.0 + nrm(ks[15], (nl, GM_GROUPS, GM_CHUNK), 0.01),
        'sw_q_norm_w': 1.0 + nrm(ks[16], (nl, SW_HEAD_DIM), 0.02),
        'sw_k_norm_w': 1.0 + nrm(ks[17], (nl, SW_HEAD_DIM), 0.02),
        'w_ffn_in': nrm(ks[18], (nl, D_MODEL, 2 * FFN_HIDDEN), D_MODEL ** -0.5),
        'w_ffn_out': nrm(ks[19], (nl, FFN_HIDDEN, D_MODEL), FFN_HIDDEN ** -0.5),
    }


def reference(x, c, w_mod, b_mod, mix_norm_w, ffn_norm_w, w_in, w_out, dn_conv_w, dn_a_log,
              dn_dt_bias, dn_out_norm_w, gm_ln_g, gm_ln_b, gm_w_s, gm_b_s, sw_q_norm_w,
              sw_k_norm_w, w_ffn_in, w_ffn_out):
    bsz, t, _ = x.shape
    cos, sin = rotary_tables(t)
    c_act = jax.nn.silu(c)
    for layer in range(DEPTH):
        mod = jnp.einsum('bd,de->be', c_act, w_mod[layer]) + b_mod[layer]
        shift1, scale1, gate1, shift2, scale2, gate2 = [m[:, None, :] for m in jnp.split(mod, 6, axis=-1)]
        h = rms_norm(x, mix_norm_w[layer]) * (1.0 + scale1) + shift1
        proj = jnp.einsum('btd,de->bte', h, w_in[layer])
        dn_qkv, dn_z, dn_a, dn_b, gm_uv, sw_qkv = jnp.split(proj, IN_SPLITS, axis=-1)
        y_a = deltanet_mixer(dn_qkv, dn_z, dn_a, dn_b, dn_conv_w[layer], dn_a_log[layer],
                             dn_dt_bias[layer], dn_out_norm_w[layer])
        y_b = spatial_gating_mixer(gm_uv, gm_ln_g[layer], gm_ln_b[layer], gm_w_s[layer], gm_b_s[layer])
        y_c = dilated_attention_mixer(sw_qkv, sw_q_norm_w[layer], sw_k_norm_w[layer], cos, sin)
        y = jnp.concatenate([y_a, y_b, y_c], axis=-1).astype(x.dtype)
        x = x + gate1 * jnp.einsum('bte,ed->btd', y, w_out[layer])
        h = rms_norm(x, ffn_norm_w[layer]) * (1.0 + scale2) + shift2
        gate, up = jnp.split(jnp.einsum('btd,df->btf', h, w_ffn_in[layer]), 2, axis=-1)
        x = x + gate2 * jnp.einsum('btf,fd->btd', jax.nn.silu(gate) * up, w_ffn_out[layer])
    return x
```

```python
import numpy as np
from contextlib import ExitStack
import concourse.bass as bass
import concourse.mybir as mybir
from concourse.bass_utils import run_bass_kernel_spmd

F32 = mybir.dt.float32
BF16 = mybir.dt.bfloat16
AF = mybir.ActivationFunctionType
ALU = mybir.AluOpType
AX = mybir.AxisListType

D = 1024
KC = 8
DEPTH = 4
SEQ = 4096
NCORES = 8
EPS = 1e-6
INW = 4872
FH = 2816
NFC = 22
SEM_ROLL = 15000
PATTERNS = ((128, 1), (512, 4), (2048, 16))
CUT = 99
ROPE_THETA = 500000.0


class Tile:
    def __init__(self, k, t, name, dram=False):
        self.k, self.t, self.name, self.dram = k, t, name, dram
        self.writes = {}
        self.reads = {}
        self.dsem = None
        self.dcnt = 0

    def __getitem__(self, idx):
        return TV(self, self.t[idx])

    def v(self, ap):
        return TV(self, ap)


class TV:
    def __init__(self, tile, ap, extra=()):
        self.tile, self.ap, self.extra = tile, ap, extra


def _m(d, s):
    for sem, v in s.items():
        if d.get(sem, 0) < v:
            d[sem] = v


class K:
    def __init__(self, nc, es):
        self.nc, self.es = nc, es
        self.engs = {'pe': nc.tensor, 'act': nc.scalar, 'dve': nc.vector, 'pool': nc.gpsimd, 'sp': nc.sync}
        self.cnt = {}
        self.sem = {}
        self.own = {e: set() for e in self.engs}
        self.waited = {e: {} for e in self.engs}
        self.nsem = 0
        for e in self.engs:
            self._newsem(e)
        self.dma_tiles = []
        self.dpool = {'sp': [], 'pool': []}
        self.scope = None
        self.ninst = {e: 0 for e in self.engs}

    def _newsem(self, e):
        s = self.es.enter_context(self.nc.semaphore(f"s_{e}_{self.nsem}"))
        self.nsem += 1
        self.sem[e] = s
        self.cnt[e] = 0
        self.own[e].add(s)

    def sb(self, name, shape, dt, es=None):
        self.nsem += 1
        name = f"{name}_{self.nsem}"
        t = (es or self.es).enter_context(self.nc.sbuf_tensor(name, list(shape), dt))
        tl = Tile(self, t, name)
        if es is not None and self.scope is not None:
            self.scope.append(tl)
        return tl

    def begin_phase(self):
        self.scope = []

    def end_phase(self):
        self.barrier()
        for tl in self.scope:
            if tl.dsem is not None:
                if 16 * tl.dcnt < 2000:
                    self.dpool[tl.dq].append((tl.dsem, tl.dcnt))
                else:
                    self.retired = getattr(self, 'retired', 0) + 1
                self.dma_tiles.remove(tl)
                tl.dsem = None
        self.scope = None

    def ps(self, name, shape, dt, es=None):
        self.nsem += 1
        name = f"{name}_{self.nsem}"
        t = (es or self.es).enter_context(self.nc.psum_tensor(name, list(shape), dt))
        tl = Tile(self, t, name)
        tl.psum = True
        return tl

    def dram(self, ap, name):
        return Tile(self, ap, name, dram=True)

    @staticmethod
    def _tiles(tvs):
        out = []
        for tv in tvs:
            out.append(tv.tile)
            out.extend(tv.extra)
        return out

    def _waits(self, e, reads, writes):
        need = {}
        for t in self._tiles(reads):
            _m(need, t.writes)
            if getattr(t, 'psum', False):
                for sem, v in t.reads.items():
                    if sem not in self.own[e] and need.get(sem, 0) < v:
                        need[sem] = v
        for t in self._tiles(writes):
            _m(need, t.writes)
            _m(need, t.reads)
        eng = self.engs[e]
        w = self.waited[e]
        for sem, val in need.items():
            if e == 'pe' and sem in self.own['pe']:
                continue
            if w.get(sem, 0) < val:
                eng.wait_ge(sem, val)
                w[sem] = val
                self.ninst[e] += 1

    def _done(self, tok, reads, writes):
        sem, val = tok
        for t in self._tiles(reads):
            if t.reads.get(sem, 0) < val:
                t.reads[sem] = val
        for t in self._tiles(writes):
            if t.dram:
                if t.writes.get(sem, 0) < val:
                    t.writes[sem] = val
            else:
                t.writes = {sem: val}
            t.reads = {}

    def op(self, e, emit, reads, writes):
        self._waits(e, reads, writes)
        ins = emit(self.engs[e])
        if self.cnt[e] >= SEM_ROLL:
            self._newsem(e)
        self.cnt[e] += 1
        ins.then_inc(self.sem[e], 1)
        self.ninst[e] += 1
        tok = (self.sem[e], self.cnt[e])
        self._done(tok, reads, writes)
        return tok

    def dma(self, q, out, in_, **kw):
        own = out.tile if not out.tile.dram else in_.tile
        assert not own.dram
        if own.dsem is None:
            own.dq = q
            if self.dpool[q]:
                own.dsem, own.dcnt = self.dpool[q].pop(0)
            else:
                own.dsem = self.es.enter_context(self.nc.semaphore(f"d_{self.nsem}"))
                self.nds = getattr(self, 'nds', 0) + 1
                self.nsem += 1
            self.dma_tiles.append(own)
        assert own.dq == q, (own.name, own.dq, q)
        self._waits(q, [in_], [out])
        ins = self.engs[q].dma_start(out=out.ap, in_=in_.ap, **kw)
        own.dcnt += 1
        ins.then_inc(own.dsem, 16)
        self.ninst[q] += 1
        tok = (own.dsem, 16 * own.dcnt)
        self._done(tok, [in_], [out])
        return tok

    def load_w(self, dst, src2d, nk, cols, RO, q='pool', cstep=4096):
        for kc in range(nk):
            for c0 in range(0, cols, cstep):
                c1 = min(cols, c0 + cstep)
                self.dma(q, dst[:, kc, c0:c1], RO(src2d[kc * 128:(kc + 1) * 128, c0:c1]))

    def barrier(self):
        need = {}
        for e in self.engs:
            if self.cnt[e] > 0:
                need[self.sem[e]] = self.cnt[e]
        for t in self.dma_tiles:
            need[t.dsem] = 16 * t.dcnt
        for e in self.engs:
            w = self.waited[e]
            for sem, val in need.items():
                if e == 'pe' and sem in self.own['pe']:
                    continue
                if w.get(sem, 0) < val:
                    self.engs[e].wait_ge(sem, val)
                    w[sem] = val
                    self.ninst[e] += 1

    def mm(self, out, lhsT, rhs, start=True, stop=True):
        return self.op('pe', lambda g: g.matmul(out.ap, lhsT.ap, rhs.ap, start=start, stop=stop),
                       [lhsT, rhs], [out])

    def tr(self, out, in_, ident):
        return self.op('pe', lambda g: g.transpose(out.ap, in_.ap, ident.ap), [in_, ident], [out])

    def act(self, out, in_, func, bias=None, scale=None, accum=None):
        kw = {}
        rd = [in_]
        wr = [out]
        if bias is not None:
            if isinstance(bias, TV):
                kw['bias'] = bias.ap
                rd.append(bias)
            else:
                kw['bias'] = bias
        if scale is not None:
            if isinstance(scale, TV):
                kw['scale'] = scale.ap
                rd.append(scale)
            else:
                kw['scale'] = scale
        if accum is not None:
            kw['accum_out'] = accum.ap
            wr.append(accum)
        return self.op('act', lambda g: g.activation(out=out.ap, in_=in_.ap, func=func, **kw), rd, wr)

    def tt(self, e, out, in0, in1, op):
        return self.op(e, lambda g: g.tensor_tensor(out=out.ap, in0=in0.ap, in1=in1.ap, op=op), [in0, in1], [out])

    def ts(self, e, out, in0, s1, s2, op0, op1=None):
        rd = [in0]
        a1, a2 = s1, s2
        if isinstance(s1, TV):
            rd.append(s1)
            a1 = s1.ap
        if isinstance(s2, TV):
            rd.append(s2)
            a2 = s2.ap
        if op1 is None:
            return self.op(e, lambda g: g.tensor_scalar(out=out.ap, in0=in0.ap, scalar1=a1, scalar2=None, op0=op0),
                           rd, [out])
        return self.op(e, lambda g: g.tensor_scalar(out=out.ap, in0=in0.ap, scalar1=a1, scalar2=a2, op0=op0, op1=op1),
                       rd, [out])

    def stt(self, e, out, in0, s, in1, op0, op1):
        rd = [in0, in1]
        a = s
        if isinstance(s, TV):
            rd.append(s)
            a = s.ap
        return self.op(e, lambda g: g.scalar_tensor_tensor(out=out.ap, in0=in0.ap, scalar=a, in1=in1.ap,
                                                           op0=op0, op1=op1), rd, [out])

    def copy(self, e, out, in_):
        if e == 'act':
            return self.op(e, lambda g: g.copy(out=out.ap, in_=in_.ap), [in_], [out])
        return self.op(e, lambda g: g.tensor_copy(out=out.ap, in_=in_.ap), [in_], [out])

    def amul(self, out, in_, m):
        return self.op('act', lambda g: g.mul(out=out.ap, in_=in_.ap, mul=m.ap), [in_, m], [out])

    def memset(self, e, out, val):
        return self.op(e, lambda g: g.memset(out.ap, val), [], [out])

    def recip(self, out, in_):
        return self.op('dve', lambda g: g.reciprocal(out=out.ap, in_=in_.ap), [in_], [out])

    def reduce(self, e, out, in_, op):
        return self.op(e, lambda g: g.tensor_reduce(out=out.ap, in_=in_.ap, axis=AX.X, op=op), [in_], [out])


def build(T=SEQ, depth=DEPTH, debug=False, phases='MNABCDF'):
    NT = T // 128
    NB = T // 512
    nc = bass.Bass("TRN2", target_bir_lowering=False)
    es = ExitStack()

    def din(name, shape):
        return nc.dram_tensor(name, list(shape), F32, kind="ExternalInput").ap()

    x_in = din("x", [T, D])
    c_in = din("c", [128, KC])
    w_mod = din("w_mod", [depth, D, 6 * D])
    b_mod = din("b_mod", [depth, 6 * D])
    mixw = din("mix_norm_w", [depth, 128, KC])
    ffnw = din("ffn_norm_w", [depth, 128, KC])
    w_in = din("w_in", [depth, D, INW])
    w_out = din("w_out", [depth, D, D])
    convw = din("dn_conv_w", [depth, 128, 12, 4])
    a_log = din("dn_a_log", [depth, 4])
    dt_b = din("dn_dt_bias", [depth, 4])
    onw_in = din("dn_out_norm_w", [depth, 128])
    lng_in = din("gm_ln_g", [depth, 256])
    lnb_in = din("gm_ln_b", [depth, 256])
    wsT_in = din("gm_w_sT", [depth, 128, 4, 128])
    bs_in = din("gm_b_s", [depth, 128, 4])
    qnw_in = din("sw_q_norm_w", [depth, 64])
    knw_in = din("sw_k_norm_w", [depth, 64])
    wf_in = din("w_ffn_in", [depth, D, 2 * FH])
    wf_out = din("w_ffn_out", [depth, FH, D])
    cst_in = din("consts", [128, 8, 128])
    rot_in = din("rot", [T, 32])
    out_d = nc.dram_tensor("out", [T, D], F32, kind="ExternalOutput").ap()
    skind = "ExternalOutput" if debug else "Internal"
    modrow_d = nc.dram_tensor("modrow", [depth, 6 * D], F32, kind=skind).ap()
    yT_d = nc.dram_tensor("yT", [NT, 128, 768], BF16, kind="Internal").ap()
    nl_d = [nc.dram_tensor(f"nl{p}", [T, 260], F32, kind=skind).ap() for p in range(3)]
    dbg_d = nc.dram_tensor("dbg", [T, D], F32, kind=skind).ap() if debug else None

    with es:
        k = K(nc, es)
        xin_t = [k.dram(x_in[i * 128:(i + 1) * 128, :], f"xin{i}") for i in range(NT)]
        out_t = [k.dram(out_d[i * 128:(i + 1) * 128, :], f"out{i}") for i in range(NT)]
        wts = k.dram(w_mod, "wts")
        modrow_t = k.dram(modrow_d, "modrow")
        yT_t = [k.dram(yT_d[i], f"yT{i}") for i in range(NT)]
        nl_t = [k.dram(nl_d[p], f"nl{p}") for p in range(3)]
        dbg_t = k.dram(dbg_d, "dbg") if debug else None

        def RO(ap):
            return wts.v(ap)

        hT = k.sb("hT", [128, KC, T], BF16)
        hT_b = [Tile(k, hT.t, f"hTb{b}") for b in range(NB)]
        cst = k.sb("cst", [128, 8, 128], F32)
        identb = k.sb("identb", [128, 128], BF16)
        maskCb = k.sb("maskCb", [128, 128], BF16)
        maskPb = k.sb("maskPb", [128, 128], BF16)
        epsc = k.sb("epsc", [128, 1], F32)
        k.dma('sp', cst[:], RO(cst_in))
        ident = cst[:, 0, :]
        Um = cst[:, 1, :]
        SLm = cst[:, 2, :]
        ones = cst[:, 3, :]
        maskS = cst[:, 4, :]
        maskUi = cst[:, 5, :]
        maskP = cst[:, 6, :]
        k.copy('dve', identb[:], ident)
        k.copy('dve', maskCb[:], maskUi)
        k.copy('dve', maskPb[:], maskP)
        k.memset('dve', epsc[:], EPS)

        a1 = k.sb("a1", [128, KC], F32)
        b1 = k.sb("b1", [128, KC], F32)
        a2 = k.sb("a2", [128, KC], F32)
        b2 = k.sb("b2", [128, KC], F32)
        g1bc = k.sb("g1bc", [128, D], F32)
        g2bc = k.sb("g2bc", [128, D], F32)
        nw1 = k.sb("nw1", [128, KC], F32)
        nw2 = k.sb("nw2", [128, KC], F32)
        mcol = k.sb("mcol", [128, 4, KC], F32)

        with ExitStack() as pes:
            k.begin_phase()
            c_sb = k.sb("c_sb", [128, KC], F32, pes)
            cact = k.sb("cact", [128, KC], F32, pes)
            wm = [k.sb(f"wm{i}", [128, KC, 512], F32, pes) for i in range(2)]
            bm = k.sb("bm", [1, 6 * D], F32, pes)
            mr = k.sb("mr", [1, 6 * D], F32, pes)
            pm = [k.ps(f"pm{i}", [128, 512], F32, pes) for i in range(2)]
            k.dma('sp', c_sb[:], RO(c_in))
            k.act(cact[:], c_sb[:], AF.Silu)
            it = 0
            for l in range(depth):
                k.dma('sp', bm[:], RO(b_mod[l:l + 1, :]))
                for ec in range(12):
                    w = wm[it % 2]
                    p = pm[it % 2]
                    q = 'sp' if it % 2 == 0 else 'pool'
                    k.dma(q, w[:], RO(w_mod[l, :, ec * 512:(ec + 1) * 512].rearrange("(kc p) e -> p kc e", p=128)))
                    for kc in range(KC):
                        k.mm(p[0:1, :], cact[:, kc:kc + 1], w[:, kc, :], start=(kc == 0), stop=(kc == KC - 1))
                    k.tt('dve', mr[0:1, ec * 512:(ec + 1) * 512], p[0:1, :], bm[0:1, ec * 512:(ec + 1) * 512], ALU.add)
                    it += 1
                k.dma('sp', modrow_t.v(modrow_d[l:l + 1, :]), mr[:])
            k.end_phase()

        def load_layer_params(l):
            for j, src in enumerate((0, 1, 3, 4)):
                k.dma('sp', mcol[:, j, :],
                      modrow_t.v(modrow_d[l, src * D:(src + 1) * D].rearrange("(kc p) -> p kc", p=128)),
                      allow_slow_non_contiguous=True)
            k.dma('sp', g1bc[:], modrow_t.v(modrow_d[l, 2 * D:3 * D].partition_broadcast(128)))
            k.dma('sp', g2bc[:], modrow_t.v(modrow_d[l, 5 * D:6 * D].partition_broadcast(128)))
            k.dma('sp', nw1[:], RO(mixw[l]))
            k.dma('sp', nw2[:], RO(ffnw[l]))
            k.stt('dve', a1[:], mcol[:, 1, :], 1.0, nw1[:], ALU.add, ALU.mult)
            k.copy('dve', b1[:], mcol[:, 0, :])
            k.stt('dve', a2[:], mcol[:, 3, :], 1.0, nw2[:], ALU.add, ALU.mult)
            k.copy('dve', b2[:], mcol[:, 2, :])

        def norm_to_hT(xt, tt, aa, bb, sq, ss, rs, xn, ptr):
            hb = hT_b[tt // 4]
            k.memset('dve', ss[:, 0:1], 0.0)
            k.act(sq[:], xt[:], AF.Square, accum=ss[:, 0:1])
            k.act(rs[:, 0:1], ss[:, 0:1], AF.Sqrt, bias=epsc[:, 0:1], scale=1.0 / D)
            k.recip(rs[:, 1:2], rs[:, 0:1])
            k.ts('dve', xn[:], xt[:], rs[:, 1:2], None, ALU.mult)
            for half in range(2):
                p = ptr[half]
                for j in range(4):
                    kc = half * 4 + j
                    k.tr(p[:, j, :], xn[:, kc * 128:(kc + 1) * 128], ident)
                for j in range(4):
                    kc = half * 4 + j
                    e = 'dve' if j % 2 == 0 else 'pool'
                    if e == 'pool':
                        e = 'dve'
                    k.ts(e, hb.v(hT.t[:, kc, tt * 128:(tt + 1) * 128]), p[:, j, :], aa[:, kc:kc + 1], bb[:, kc:kc + 1],
                         ALU.mult, ALU.add)

        for l in range(depth):
            load_layer_params(l)
            xsrc = xin_t if l == 0 else out_t

            with ExitStack() as pes:
                k.begin_phase()
                xt = [k.sb(f"n1x{i}", [128, D], F32, pes) for i in range(2)]
                sq = k.sb("n1sq", [128, D], F32, pes)
                xn = [k.sb(f"n1xn{i}", [128, D], F32, pes) for i in range(2)]
                ss = [k.sb(f"n1ss{i}", [128, 1], F32, pes) for i in range(2)]
                rs = [k.sb(f"n1rs{i}", [128, 2], F32, pes) for i in range(2)]
                ptr = [[k.ps(f"n1p{i}{h}", [128, 4, 128], F32, pes) for h in range(2)] for i in range(2)]
                for tt in range(NT):
                    i = tt % 2
                    k.dma('sp', xt[i][:], xsrc[tt][:])
                    norm_to_hT(xt[i], tt, a1, b1, sq, ss[i], rs[i], xn[i], ptr[i])
                k.end_phase()

            if 'A' in phases:
                phase_A(k, l, T, hT, hT_b, RO, w_in, convw, a_log, dt_b, onw_in, cst, identb, epsc, yT_t, yT_d)
            if 'B' in phases:
                phase_B(k, l, T, hT, hT_b, RO, w_in, lng_in, lnb_in, wsT_in, bs_in, cst, identb, epsc, yT_t, yT_d)
            if 'C' in phases:
                phase_C(k, l, T, hT, hT_b, RO, w_in, qnw_in, knw_in, rot_in, identb, maskCb, maskPb, epsc, nl_t, nl_d)

            with ExitStack() as pes:
                if 'D' not in phases:
                    break
                k.begin_phase()
                wo = k.sb("wo", [128, KC, D], BF16, pes)
                k.load_w(wo, w_out[l], KC, D, RO)
                nl = [[k.sb(f"dnl{i}{p}", [128, 4, 65], F32, pes) for p in range(3)] for i in range(2)]
                nsum = k.sb("dnsum", [128, 4, 65], F32, pes)
                rl = k.sb("drl", [128, 4], F32, pes)
                ycb = k.sb("dycb", [128, 4, 64], BF16, pes)
                yT = [k.sb(f"dyT{i}", [128, 8, 128], BF16, pes) for i in range(2)]
                xt = [k.sb(f"dx{i}", [128, D], F32, pes) for i in range(2)]
                xo = [k.sb(f"dxo{i}", [128, D], F32, pes) for i in range(2)]
                tmp = k.sb("dtmp", [128, D], F32, pes)
                sq = k.sb("dsq", [128, D], F32, pes)
                xn = k.sb("dxn", [128, D], F32, pes)
                ss = k.sb("dss", [128, 1], F32, pes)
                rs = k.sb("drs", [128, 2], F32, pes)
                ptc = k.ps("dptc", [128, 8, 128], BF16, pes)
                po = [k.ps(f"dpo{h}", [128, 512], F32, pes) for h in range(2)]
                ptr = [k.ps(f"dptr{h}", [128, 4, 128], F32, pes) for h in range(2)]
                for tt in range(NT):
                    i = tt % 2
                    for p in range(3):
                        k.dma('sp', nl[i][p][:], nl_t[p].v(nl_d[p][tt * 128:(tt + 1) * 128, :].rearrange("p (h e) -> p h e", h=4)))
                    k.dma('sp', yT[i][:, 0:6, :], yT_t[tt].v(yT_d[tt].rearrange("p (c t) -> p c t", c=6)))
                    k.dma('sp', xt[i][:], xsrc[tt][:])
                    k.tt('dve', nsum[:], nl[i][0][:], nl[i][1][:], ALU.add)
                    k.tt('dve', nsum[:], nsum[:], nl[i][2][:], ALU.add)
                    k.recip(rl[:], nsum[:, :, 64])
                    k.tt('dve', ycb[:], nsum[:, :, 0:64], rl.v(rl.t[:, :].unsqueeze(2).to_broadcast([128, 4, 64])), ALU.mult)
                    for j in range(2):
                        k.tr(ptc[:, j, :], ycb.v(ycb.t[:, 2 * j:2 * j + 2, :].rearrange("p h e -> p (h e)")), identb[:])
                    k.copy('act', yT[i][:, 6:8, :], ptc[:, 0:2, :])
                    for half in range(2):
                        for c in range(8):
                            k.mm(po[half][:], yT[i][:, c, :], wo[:, c, half * 512:(half + 1) * 512],
                                 start=(c == 0), stop=(c == 7))
                    for half in range(2):
                        sl = slice(half * 512, (half + 1) * 512)
                        k.tt('dve', tmp[:, sl], po[half][:], g1bc[:, sl], ALU.mult)
                    k.tt('pool', xo[i][:], tmp[:], xt[i][:], ALU.add)
                    k.dma('sp', out_t[tt][:], xo[i][:])
                    norm_to_hT(xo[i], tt, a2, b2, sq, ss, rs, xn, ptr)
                k.end_phase()

            groups = [(0, 6), (6, 6), (12, 5), (17, 5)] if 'F' in phases else []
            for (f0, nf) in groups:
                with ExitStack() as pes:
                    k.begin_phase()
                    wg = k.sb("fwg", [128, KC, nf * 128], BF16, pes)
                    wu = k.sb("fwu", [128, KC, nf * 128], BF16, pes)
                    wo2 = k.sb("fwo", [128, nf, D], BF16, pes)
                    k.load_w(wg, wf_in[l, :, f0 * 128:(f0 + nf) * 128], KC, nf * 128, RO)
                    k.load_w(wu, wf_in[l, :, FH + f0 * 128:FH + (f0 + nf) * 128], KC, nf * 128, RO)
                    k.load_w(wo2, wf_out[l, f0 * 128:(f0 + nf) * 128, :], nf, D, RO)
                    actT = [k.sb(f"fact{i}", [128, nf, 512], BF16, pes) for i in range(2)]
                    sg = [k.sb(f"fsg{i}", [128, 512], F32, pes) for i in range(2)]
                    xt = [k.sb(f"fx{i}", [128, D], F32, pes) for i in range(2)]
                    xo = [k.sb(f"fxo{i}", [128, D], F32, pes) for i in range(2)]
                    tmp = k.sb("ftmp", [128, D], F32, pes)
                    pg = [k.ps(f"fpg{i}", [128, 512], F32, pes) for i in range(2)]
                    pu = [k.ps(f"fpu{i}", [128, 512], F32, pes) for i in range(2)]
                    po = [[k.ps(f"fpo{i}{h}", [128, 512], F32, pes) for h in range(2)] for i in range(2)]
                    it = 0
                    xi = 0
                    for blk in range(NB):
                        hb = hT_b[blk]
                        a = actT[blk % 2]
                        for fi in range(nf):
                            j = it % 2
                            it += 1
                            for kc in range(KC):
                                k.mm(pg[j][:], wg[:, kc, fi * 128:(fi + 1) * 128], hb.v(hT.t[:, kc, blk * 512:(blk + 1) * 512]),
                                     start=(kc == 0), stop=(kc == KC - 1))
                            for kc in range(KC):
                                k.mm(pu[j][:], wu[:, kc, fi * 128:(fi + 1) * 128], hb.v(hT.t[:, kc, blk * 512:(blk + 1) * 512]),
                                     start=(kc == 0), stop=(kc == KC - 1))
                            k.act(sg[j][:], pg[j][:], AF.Silu)
                            k.tt('dve', a[:, fi, :], sg[j][:], pu[j][:], ALU.mult)
                        for t4 in range(4):
                            tt = blk * 4 + t4
                            i = xi % 2
                            xi += 1
                            k.dma('sp', xt[i][:], out_t[tt][:])
                            for half in range(2):
                                for fi in range(nf):
                                    k.mm(po[i][half][:], a[:, fi, t4 * 128:(t4 + 1) * 128], wo2[:, fi, half * 512:(half + 1) * 512],
                                         start=(fi == 0), stop=(fi == nf - 1))
                            for half in range(2):
                                sl = slice(half * 512, (half + 1) * 512)
                                k.tt('dve', tmp[:, sl], po[i][half][:], g2bc[:, sl], ALU.mult)
                            k.tt('pool', xo[i][:], tmp[:], xt[i][:], ALU.add)
                            k.dma('sp', out_t[tt][:], xo[i][:])
                    k.end_phase()
        k.barrier()
        build.ninst = dict(k.ninst)
        build.retired = (getattr(k, 'retired', 0), getattr(k, 'nds', 0), len(k.own['pe']) + len(k.own['act']) + len(k.own['dve']) + len(k.own['pool']) + len(k.own['sp']))
        build.maxd = (max([c for _, c in k.dpool['sp']] + [0]) * 16, max([c for _, c in k.dpool['pool']] + [0]) * 16, k.nsem)
    return nc


def phase_B(k, l, T, hT, hT_b, RO, w_in, lng_in, lnb_in, wsT_in, bs_in, cst, identb, epsc, yT_t, yT_d):
    NT = T // 128
    c0 = 2056
    maskUi = cst[:, 5, :]
    with ExitStack() as pes:
        k.begin_phase()
        wb = k.sb("bw", [128, KC, 512], BF16, pes)
        k.load_w(wb, w_in[l, :, c0:c0 + 512], KC, 512, RO)
        lng = k.sb("blng", [128, 256], F32, pes)
        lnb = k.sb("blnb", [128, 256], F32, pes)
        wsf = k.sb("bwsf", [128, 4, 128], F32, pes)
        wsb = k.sb("bwsb", [128, 4, 128], BF16, pes)
        bs = k.sb("bbs", [128, 4], F32, pes)
        k.dma('sp', lng[:], RO(lng_in[l].partition_broadcast(128)))
        k.dma('sp', lnb[:], RO(lnb_in[l].partition_broadcast(128)))
        k.dma('sp', wsf[:], RO(wsT_in[l]))
        k.dma('sp', bs[:], RO(bs_in[l]))
        k.tt('dve', wsb[:], wsf[:], maskUi.tile.v(cst.t[:, 5:6, :].to_broadcast([128, 4, 128])), ALU.mult)
        gl = [k.sb(f"bgl{i}", [128, 512], F32, pes) for i in range(2)]
        st = k.sb("bst", [128, 6], F32, pes)
        mv = k.sb("bmv", [128, 2], F32, pes)
        rs = k.sb("brs", [128, 2], F32, pes)
        vn = k.sb("bvn", [128, 256], F32, pes)
        vn2 = k.sb("bvn2", [128, 256], F32, pes)
        vnb = [k.sb(f"bvnb{i}", [128, 256], BF16, pes) for i in range(2)]
        yb = [k.sb(f"byb{i}", [128, 256], BF16, pes) for i in range(2)]
        ybT = [k.sb(f"bybT{i}", [128, 2, 128], BF16, pes) for i in range(2)]
        pp = [k.ps(f"bpp{i}", [128, 512], F32, pes) for i in range(2)]
        psv = [k.ps(f"bpsv{i}", [128, 512], F32, pes) for i in range(2)]
        ptr = [k.ps(f"bptr{i}", [128, 8, 128], BF16, pes) for i in range(2)]
        for tt in range(NT):
            i = tt % 2
            hb = hT_b[tt // 4]
            for kc in range(KC):
                k.mm(pp[i][:], hb.v(hT.t[:, kc, tt * 128:(tt + 1) * 128]), wb[:, kc, :], start=(kc == 0), stop=(kc == KC - 1))
            k.act(gl[i][:], pp[i][:], AF.Gelu)
            k.op('dve', lambda g: g.bn_stats(out=st.t[:, 0:6], in_=gl[i].t[:, 256:512]), [gl[i][:]], [st[:]])
            k.op('dve', lambda g: g.bn_aggr(out=mv.t[:, 0:2], in_=st.t[:, 0:6]), [st[:]], [mv[:]])
            k.act(rs[:, 0:1], mv[:, 1:2], AF.Sqrt, bias=epsc[:, 0:1], scale=1.0)
            k.recip(rs[:, 1:2], rs[:, 0:1])
            k.ts('dve', vn[:], gl[i][:, 256:512], mv[:, 0:1], rs[:, 1:2], ALU.subtract, ALU.mult)
            k.tt('pool', vn2[:], vn[:], lng[:], ALU.mult)
            k.tt('pool', vnb[i][:], vn2[:], lnb[:], ALU.add)
            for g in range(4):
                k.mm(psv[i][:, g * 64:(g + 1) * 64], wsb[:, g, :], vnb[i][:, g * 64:(g + 1) * 64])
            for g in range(4):
                k.stt('dve', yb[i][:, g * 64:(g + 1) * 64], psv[i][:, g * 64:(g + 1) * 64], bs[:, g:g + 1],
                      gl[i][:, g * 64:(g + 1) * 64], ALU.add, ALU.mult)
            for j in range(2):
                k.tr(ptr[i][:, j, :], yb[i][:, j * 128:(j + 1) * 128], identb[:])
            k.copy('act', ybT[i][:], ptr[i][:, 0:2, :])
            k.dma('sp', yT_t[tt].v(yT_d[tt][:, 512:768].rearrange("p (c t) -> p c t", c=2)), ybT[i][:])
        k.end_phase()


def phase_C(k, l, T, hT, hT_b, RO, w_in, qnw_in, knw_in, rot_in, identb, maskCb, maskPb, epsc, nl_t, nl_d):
    c0 = 2568
    with ExitStack() as pes:
        k.begin_phase()
        wc = k.sb("cw", [128, KC, 2304], BF16, pes)
        k.load_w(wc, w_in[l, :, c0:c0 + 2304], KC, 2304, RO)
        nwq = k.sb("cnwq", [128, 64], F32, pes)
        nwk = k.sb("cnwk", [128, 64], F32, pes)
        nw = k.sb("cnw", [128, 8, 64], F32, pes)
        k.dma('sp', nwq[:], RO(qnw_in[l].partition_broadcast(128)))
        k.dma('sp', nwk[:], RO(knw_in[l].partition_broadcast(128)))
        for h in range(4):
            k.copy('dve', nw[:, h, :], nwq[:])
            k.copy('dve', nw[:, 4 + h, :], nwk[:])
        rot = [k.sb(f"crot{i}", [128, 32], F32, pes) for i in range(2)]
        sqt = k.sb("csq", [128, 512], F32, pes)
        ss = k.sb("css", [128, 8], F32, pes)
        rs = k.sb("crs", [128, 8], F32, pes)
        ri = k.sb("cri", [128, 8], F32, pes)
        qkf = k.sb("cqkf", [128, 8, 64], F32, pes)
        qkg = k.sb("cqkg", [128, 8, 64], F32, pes)
        ra = k.sb("cra", [128, 8, 16], F32, pes)
        rb = k.sb("crb", [128, 8, 16], F32, pes)
        qkb = k.sb("cqkb", [128, 8, 64], BF16, pes)
        v1 = [k.sb(f"cv1{i}", [128, 4, 80], BF16, pes) for i in range(2)]
        qT = k.sb("cqT", [64, 4, 128], BF16, pes)
        kT = [k.sb(f"ckT{i}", [64, 4, 128], BF16, pes) for i in range(2)]
        P0 = k.sb("cP0", [128, 4, 128], BF16, pes)
        P1 = k.sb("cP1", [128, 4, 128], BF16, pes)
        P0m = k.sb("cP0m", [128, 4, 128], BF16, pes)
        P1m = k.sb("cP1m", [128, 4, 128], BF16, pes)
        nlo = [k.sb(f"cnlo{i}", [128, 4, 65], F32, pes) for i in range(2)]
        pc0 = k.ps("cpc0", [128, 512], F32, pes)
        pc1 = k.ps("cpc1", [128, 512], F32, pes)
        ptr = k.ps("cptr", [128, 8, 128], BF16, pes)
        pS0 = k.ps("cpS0", [128, 4, 128], F32, pes)
        pS1 = k.ps("cpS1", [128, 4, 128], F32, pes)
        pO = k.ps("cpO", [128, 4, 128], F32, pes)
        for i in range(2):
            k.memset('pool', v1[i][:], 1.0)
        mC = maskCb.v(maskCb.t[:, :].unsqueeze(1).to_broadcast([128, 4, 128]))
        mP = maskPb.v(maskPb.t[:, :].unsqueeze(1).to_broadcast([128, 4, 128]))
        it = 0
        for p, (window, d) in enumerate(PATTERNS):
            L = T // d
            nb = L // 128
            for r in range(d):
                for n in range(nb):
                    start = 128 * n * d + r
                    stop_ = start + 127 * d + 1
                    cur = it % 2
                    prev = 1 - cur
                    it += 1
                    b0 = start // 512
                    b1 = (stop_ - 1) // 512
                    hbs = [hT_b[b] for b in range(b0, b1 + 1)]
                    k.dma('sp', rot[cur][:], RO(rot_in[start:stop_:d, :]))
                    for kc in range(KC):
                        lhs = TV(hbs[0], hT.t[:, kc, start:stop_:d], tuple(hbs[1:]))
                        k.mm(pc0[:], lhs, wc[:, kc, p * 768:p * 768 + 512], start=(kc == 0), stop=(kc == KC - 1))
                    for kc in range(KC):
                        lhs = TV(hbs[0], hT.t[:, kc, start:stop_:d], tuple(hbs[1:]))
                        k.mm(pc1[:, 0:256], lhs, wc[:, kc, p * 768 + 512:p * 768 + 768], start=(kc == 0), stop=(kc == KC - 1))
                    if CUT < 2:
                        continue
                    k.act(sqt[:], pc0[:], AF.Square)
                    k.reduce('dve', ss[:], sqt.v(sqt.t[:, :].rearrange("p (g e) -> p g e", g=8)), ALU.add)
                    k.act(rs[:], ss[:], AF.Sqrt, bias=epsc[:, 0:1], scale=1.0 / 64)
                    k.recip(ri[:], rs[:])
                    k.tt('dve', qkf[:], pc0.v(pc0.t[:, :].rearrange("p (g e) -> p g e", g=8)),
                         ri.v(ri.t[:, :].unsqueeze(2).to_broadcast([128, 8, 64])), ALU.mult)
                    k.tt('pool', qkg[:], qkf[:], nw[:], ALU.mult)
                    C2 = rot[cur].v(rot[cur].t[:, 0:16].unsqueeze(1).to_broadcast([128, 8, 16]))
                    Sa = rot[cur].v(rot[cur].t[:, 16:24].unsqueeze(1).to_broadcast([128, 8, 8]))
                    Sb = rot[cur].v(rot[cur].t[:, 24:32].unsqueeze(1).to_broadcast([128, 8, 8]))
                    k.tt('dve', ra[:], qkg[:, :, 0:16], C2, ALU.mult)
                    k.tt('pool', rb[:, :, 0:8], qkg[:, :, 8:16], Sa, ALU.mult)
                    k.tt('pool', rb[:, :, 8:16], qkg[:, :, 0:8], Sb, ALU.mult)
                    k.copy('pool', qkb[:, :, 16:64], qkg[:, :, 16:64])
                    k.tt('dve', qkb[:, :, 0:16], ra[:], rb[:], ALU.add)
                    if CUT < 3:
                        continue
                    k.copy('act', v1[cur][:, :, 0:64], pc1.v(pc1.t[:, 0:256].rearrange("p (h e) -> p h e", h=4)))
                    for j in range(8):
                        k.tr(ptr[0:64, j, :], qkb[:, j, :], identb[:])
                    k.copy('act', qT[:], ptr[0:64, 0:4, :])
                    k.copy('act', kT[cur][:], ptr[0:64, 4:8, :])
                    if CUT < 4:
                        continue
                    for h in range(4):
                        k.mm(pS0[:, h, :], kT[cur][:, h, :], qT[:, h, :])
                    k.act(P0[:], pS0[:], AF.Exp, scale=0.125)
                    k.tt('pool', P0m[:], P0[:], mC, ALU.mult)
                    if n > 0:
                        for h in range(4):
                            k.mm(pS1[:, h, :], kT[prev][:, h, :], qT[:, h, :])
                        k.act(P1[:], pS1[:], AF.Exp, scale=0.125)
                        k.tt('dve', P1m[:], P1[:], mP, ALU.mult)
                    if CUT < 5:
                        continue
                    for h in range(4):
                        if n > 0:
                            k.mm(pO[:, h, 0:65], P1m[:, h, :], v1[prev][:, h, 0:65], start=True, stop=False)
                        k.mm(pO[:, h, 0:65], P0m[:, h, :], v1[cur][:, h, 0:65], start=(n == 0), stop=True)
                    k.copy('act', nlo[cur][:], pO[:, :, 0:65])
                    k.dma('sp', nl_t[p].v(nl_d[p][start:stop_:d, :].rearrange("p (h e) -> p h e", h=4)), nlo[cur][:])
        k.end_phase()


def phase_A(k, l, T, hT, hT_b, RO, w_in, convw_in, a_log, dt_b, onw_in, cst, identb, epsc, yT_t, yT_d):
    NT = T // 128
    NB = T // 512
    ident = cst[:, 0, :]
    Um = cst[:, 1, :]
    SLm = cst[:, 2, :]
    ones = cst[:, 3, :]
    maskS = cst[:, 4, :]
    maskUi = cst[:, 5, :]
    SCALE = 128.0 ** -0.5
    with ExitStack() as pes:
        k.begin_phase()
        wa = k.sb("aw", [128, KC, 2056], BF16, pes)
        k.load_w(wa, w_in[l, :, 0:2056], KC, 2056, RO)
        cw = k.sb("acw", [128, 12, 4], F32, pes)
        k.dma('sp', cw[:], RO(convw_in[l]))
        alog = k.sb("aalog", [128, 4], F32, pes)
        dtb = k.sb("adtb", [128, 4], F32, pes)
        nea = k.sb("anea", [128, 4], F32, pes)
        onw = k.sb("aonw", [128, 128], F32, pes)
        k.dma('sp', alog[:], RO(a_log[l].partition_broadcast(128)))
        k.dma('sp', dtb[:], RO(dt_b[l].partition_broadcast(128)))
        k.dma('sp', onw[:], RO(onw_in[l].partition_broadcast(128)))
        k.act(nea[:], alog[:], AF.Exp)
        k.ts('dve', nea[:], nea[:], -1.0, None, ALU.mult)
        halo = k.sb("ahalo", [128, 12, 3], F32, pes)
        k.memset('dve', halo[:], 0.0)
        raw = [k.sb(f"araw{i}", [128, 515], F32, pes) for i in range(2)]
        acc = [k.sb(f"aacc{i}", [128, 512], F32, pes) for i in range(2)]
        acc2 = [k.sb(f"aacc2{i}", [128, 512], F32, pes) for i in range(2)]
        qkv = k.sb("aqkv", [128, 12, 512], F32, pes)
        sqt, rsq, rinv = acc[0], acc[1], acc2[0]
        S = [[k.sb(f"aS{h}{i}", [128, 128], F32, pes) for i in range(2)] for h in range(4)]
        for h in range(4):
            k.memset('dve', S[h][0][:], 0.0)
        sz = k.sb("asz", [128, 512], F32, pes)
        sm = k.sb("asm", [128, 64], F32, pes)
        oall = k.sb("aoall", [128, 4, 128], F32, pes)
        osq = k.sb("aosq", [128, 128], F32, pes)
        oss = k.sb("aoss", [128, 4], F32, pes)
        ors = k.sb("aors", [128, 8], F32, pes)
        ot = k.sb("aot", [128, 128], F32, pes)
        ya = k.sb("aya", [128, 4, 128], F32, pes)
        yaT = [k.sb(f"ayaT{i}", [128, 4, 128], BF16, pes) for i in range(2)]
        names = ["kbg", "kdec", "vb", "GU", "eD", "eDT", "tA", "N", "NT", "tQ", "aqk", "TTa", "TTb",
                 "Pa", "PTa", "Pb", "PTb", "u", "wT", "vnew", "o2"]
        W = [{n: k.sb(f"a{n}{s}", [128, 128], F32, pes) for n in names} for s in range(4)]
        BK = [k.ps(f"abk{i}", [128, 512], F32, pes) for i in range(8)]
        pq = [BK[0], BK[1]]
        pz = BK[2]
        pm = BK[3]
        pty = BK[4]

        def sc(j):
            return sm[:, j:j + 1]

        pwi = [0]

        def nxt():
            t = pw[pwi[0] % 3]
            s = (pwi[0] // 3) % 4
            pwi[0] += 1
            return t[:, s, :]

        it = 0
        for blk in range(NB):
            hb = hT_b[blk]
            tok = slice(blk * 512, (blk + 1) * 512)
            for fc in range(12):
                j = it % 2
                it += 1
                for kc in range(KC):
                    k.mm(pq[j][:], wa[:, kc, fc * 128:(fc + 1) * 128], hb.v(hT.t[:, kc, tok]), start=(kc == 0), stop=(kc == KC - 1))
                k.copy('pool', raw[j][:, 0:3], halo[:, fc, :])
                k.copy('act', raw[j][:, 3:515], pq[j][:])
                k.copy('pool', halo[:, fc, :], raw[j][:, 512:515])
                k.ts('dve', acc[j][:], raw[j][:, 0:512], cw[:, fc, 0:1], None, ALU.mult)
                k.stt('dve', acc2[j][:], raw[j][:, 1:513], cw[:, fc, 1:2], acc[j][:], ALU.mult, ALU.add)
                k.stt('dve', acc[j][:], raw[j][:, 2:514], cw[:, fc, 2:3], acc2[j][:], ALU.mult, ALU.add)
                k.stt('dve', acc2[j][:], raw[j][:, 3:515], cw[:, fc, 3:4], acc[j][:], ALU.mult, ALU.add)
                k.act(qkv[:, fc, :], acc2[j][:], AF.Silu)
            for fc in range(8):
                j = it % 2
                it += 1
                k.act(sqt[:], qkv[:, fc, :], AF.Square)
                k.mm(pq[j][:], ones, sqt[:])
                k.act(rsq[:], pq[j][:], AF.Sqrt, bias=epsc[:, 0:1], scale=1.0)
                k.recip(rinv[:], rsq[:])
                k.tt('dve', qkv[:, fc, :], qkv[:, fc, :], rinv[:], ALU.mult)
            for t4 in range(4):
                tt = blk * 4 + t4
                cs = slice(t4 * 128, (t4 + 1) * 128)
                hts = slice(tt * 128, (tt + 1) * 128)
                for kc in range(KC):
                    k.mm(pz[:], hb.v(hT.t[:, kc, hts]), wa[:, kc, 1536:2048], start=(kc == 0), stop=(kc == KC - 1))
                for kc in range(KC):
                    k.mm(pm[:, 0:8], hb.v(hT.t[:, kc, hts]), wa[:, kc, 2048:2056], start=(kc == 0), stop=(kc == KC - 1))
                k.act(sz[:], pz[:], AF.Silu)
                k.tt('dve', sm[:, 0:4], pm[:, 0:4], dtb[:], ALU.add)
                k.act(sm[:, 4:8], sm[:, 0:4], AF.Exp)
                k.act(sm[:, 8:12], sm[:, 4:8], AF.Ln, bias=1.0)
                k.tt('dve', sm[:, 12:16], sm[:, 8:12], nea[:], ALU.mult)
                k.act(sm[:, 16:20], pm[:, 4:8], AF.Exp, scale=-1.0)
                k.ts('dve', sm[:, 16:20], sm[:, 16:20], 1.0, None, ALU.add)
                k.recip(sm[:, 20:24], sm[:, 16:20])
                k.ts('dve', sm[:, 24:28], sm[:, 20:24], -1.0, None, ALU.mult)
                k.mm(pm[:, 8:12], Um, sm[:, 12:16])
                k.mm(pm[:, 12:16], ones, sm[:, 12:16])
                k.copy('dve', sm[:, 28:32], pm[:, 8:12])
                k.copy('dve', sm[:, 36:40], pm[:, 12:16])
                k.act(sm[:, 32:36], sm[:, 28:32], AF.Exp)
                k.act(sm[:, 40:44], sm[:, 36:40], AF.Exp)
                k.tt('dve', sm[:, 44:48], sm[:, 36:40], sm[:, 28:32], ALU.subtract)
                k.act(sm[:, 48:52], sm[:, 44:48], AF.Exp)
                k.tt('dve', sm[:, 52:56], sm[:, 20:24], sm[:, 32:36], ALU.mult)
                k.ts('dve', sm[:, 56:60], sm[:, 32:36], SCALE, None, ALU.mult)
                def head_gen(h):
                    w = W[h]
                    qT_ = qkv[:, h, cs]
                    kT_ = qkv[:, 4 + h, cs]
                    vT_ = qkv[:, 8 + h, cs]
                    Sold = S[h][tt % 2]
                    Snew = S[h][(tt + 1) % 2]
                    cnt = [0]

                    def nx():
                        c = cnt[0]
                        cnt[0] += 1
                        bk = BK[2 * h + (c % 2)]
                        o = ((c // 2) % 4) * 128
                        return bk[:, o:o + 128]
                    p = nx()
                    k.tr(p, kT_, ident)
                    k.ts('dve', w["kbg"][:], p, sc(52 + h), None, ALU.mult)
                    k.amul(w["kdec"][:], p, sc(48 + h))
                    yield
                    p = nx()
                    k.tr(p, vT_, ident)
                    k.amul(w["vb"][:], p, sc(20 + h))
                    k.ts('dve', w["GU"][:], Um, sc(12 + h), None, ALU.mult)
                    yield
                    p = nx()
                    k.mm(p, w["GU"][:], SLm)
                    k.act(w["eD"][:], p, AF.Exp)
                    k.tt('pool', w["tA"][:], w["eD"][:], maskS, ALU.mult)
                    yield
                    p = nx()
                    k.mm(p, SLm, w["GU"][:])
                    k.act(w["eDT"][:], p, AF.Exp)
                    k.tt('pool', w["tQ"][:], w["eDT"][:], maskUi, ALU.mult)
                    yield
                    p = nx()
                    k.mm(p, kT_, kT_)
                    k.stt('dve', w["N"][:], p, sc(24 + h), w["tA"][:], ALU.mult, ALU.mult)
                    yield
                    p = nx()
                    k.tr(p, w["N"][:], ident)
                    k.copy('act', w["NT"][:], p)
                    k.tt('pool', w["TTa"][:], w["NT"][:], ident, ALU.add)
                    yield
                    p = nx()
                    k.mm(p, kT_, qT_)
                    k.stt('dve', w["aqk"][:], p, SCALE, w["tQ"][:], ALU.mult, ALU.mult)
                    yield
                    TT, TTo = w["TTa"], w["TTb"]
                    P_, PT_ = w["N"], w["NT"]
                    Pn, PTn = w["Pa"], w["PTa"]
                    for s_ in range(6):
                        p = nx()
                        k.mm(p, PT_[:], P_[:])
                        k.copy('act', Pn[:], p)
                        yield
                        if s_ < 5:
                            p2 = nx()
                            k.mm(p2, P_[:], PT_[:])
                            k.copy('dve' if s_ % 2 == 0 else 'act', PTn[:], p2)
                            yield
                        p3 = nx()
                        k.mm(p3, Pn[:], TT[:])
                        k.tt('dve', TTo[:], p3, TT[:], ALU.add)
                        yield
                        TT, TTo = TTo, TT
                        P_, PT_ = Pn, PTn
                        if s_ % 2 == 0:
                            Pn, PTn = w["Pb"], w["PTb"]
                        else:
                            Pn, PTn = w["Pa"], w["PTa"]
                    p = nx()
                    k.mm(p, TT[:], w["vb"][:])
                    k.copy('act', w["u"][:], p)
                    yield
                    p = nx()
                    k.mm(p, w["kbg"][:], TT[:])
                    k.copy('dve', w["wT"][:], p)
                    yield
                    p = nx()
                    k.mm(p, w["wT"][:], Sold[:])
                    k.stt('dve', w["vnew"][:], p, -1.0, w["u"][:], ALU.mult, ALU.add)
                    yield
                    p1 = nx()
                    k.mm(p1, qT_, Sold[:])
                    yield
                    p2 = nx()
                    k.mm(p2, w["aqk"][:], w["vnew"][:])
                    k.copy('act', w["o2"][:], p2)
                    k.stt('dve', oall[:, h, :], p1, sc(56 + h), w["o2"][:], ALU.mult, ALU.add)
                    yield
                    p = nx()
                    k.mm(p, w["kdec"][:], w["vnew"][:])
                    k.stt('dve', Snew[:], Sold[:], sc(40 + h), p, ALU.mult, ALU.add)
                    yield

                gens = [head_gen(h) for h in range(4)]
                while gens:
                    for g in list(gens):
                        try:
                            next(g)
                        except StopIteration:
                            gens.remove(g)
                k.memset('dve', oss[:], 0.0)
                for h in range(4):
                    k.act(osq[:], oall[:, h, :], AF.Square, accum=oss[:, h:h + 1])
                k.act(ors[:, 0:4], oss[:], AF.Sqrt, bias=epsc[:, 0:1], scale=1.0 / 128)
                k.recip(ors[:, 4:8], ors[:, 0:4])
                for h in range(4):
                    k.stt('dve', ot[:], oall[:, h, :], ors[:, 4 + h:5 + h], onw[:], ALU.mult, ALU.mult)
                    k.tt('dve', ya[:, h, :], ot[:], sz[:, h * 128:(h + 1) * 128], ALU.mult)
                for h in range(4):
                    k.tr(pty[:, h * 128:(h + 1) * 128], ya[:, h, :], ident)
                i = tt % 2
                k.copy('act', yaT[i][:], pty.v(pty.t[:, :].rearrange("p (h t) -> p h t", h=4)))
                k.dma('sp', yT_t[tt].v(yT_d[tt][:, 0:512].rearrange("p (c t) -> p c t", c=4)), yaT[i][:])
        k.end_phase()


def _consts():
    i = np.arange(128)
    c = np.zeros((128, 8, 128), np.float32)
    c[:, 0, :] = np.eye(128)
    c[:, 1, :] = (i[:, None] <= i[None, :])
    c[:, 2, :] = (i[:, None] > i[None, :])
    c[:, 3, :] = 1.0
    c[:, 4, :] = (i[:, None] > i[None, :])
    c[:, 5, :] = (i[:, None] <= i[None, :])
    c[:, 6, :] = (i[:, None] >= i[None, :])
    return c


def _rot(T):
    inv = (np.float32(ROPE_THETA) ** (-np.arange(0, 16, 2, dtype=np.float32) / np.float32(16))).astype(np.float32)
    ang = (np.arange(T, dtype=np.float32)[:, None] * inv[None, :]).astype(np.float32)
    cos, sin = np.cos(ang).astype(np.float32), np.sin(ang).astype(np.float32)
    return np.concatenate([cos, cos, -sin, sin], axis=1).astype(np.float32)


def make_in_maps(inputs, T, depth, ncores):
    f = lambda a: np.ascontiguousarray(np.asarray(a, dtype=np.float32))
    shared = {
        "w_mod": f(inputs["w_mod"]), "b_mod": f(inputs["b_mod"]),
        "mix_norm_w": f(np.asarray(inputs["mix_norm_w"]).reshape(depth, KC, 128).transpose(0, 2, 1)),
        "ffn_norm_w": f(np.asarray(inputs["ffn_norm_w"]).reshape(depth, KC, 128).transpose(0, 2, 1)),
        "w_in": f(inputs["w_in"]), "w_out": f(inputs["w_out"]),
        "dn_conv_w": f(np.asarray(inputs["dn_conv_w"]).transpose(0, 2, 1).reshape(depth, 12, 128, 4).transpose(0, 2, 1, 3)),
        "dn_a_log": f(inputs["dn_a_log"]), "dn_dt_bias": f(inputs["dn_dt_bias"]),
        "dn_out_norm_w": f(inputs["dn_out_norm_w"]),
        "gm_ln_g": f(inputs["gm_ln_g"]), "gm_ln_b": f(inputs["gm_ln_b"]),
        "gm_w_sT": f(np.asarray(inputs["gm_w_s"]).transpose(0, 3, 1, 2)),
        "gm_b_s": f(np.asarray(inputs["gm_b_s"]).transpose(0, 2, 1)),
        "sw_q_norm_w": f(inputs["sw_q_norm_w"]), "sw_k_norm_w": f(inputs["sw_k_norm_w"]),
        "w_ffn_in": f(inputs["w_ffn_in"]), "w_ffn_out": f(inputs["w_ffn_out"]),
        "consts": _consts(), "rot": _rot(T),
    }
    x = np.asarray(inputs["x"], dtype=np.float32)
    c = np.asarray(inputs["c"], dtype=np.float32)
    maps = []
    for b in range(ncores):
        m = dict(shared)
        m["x"] = np.ascontiguousarray(x[b])
        m["c"] = np.ascontiguousarray(c[b].reshape(KC, 128).T)
        maps.append(m)
    return maps


def kernel(**inputs):
    nc = build(SEQ, DEPTH, False)
    maps = make_in_maps(inputs, SEQ, DEPTH, NCORES)
    res = run_bass_kernel_spmd(nc, maps, core_ids=list(range(NCORES)))
    return np.stack([np.asarray(r["out"], dtype=np.float32) for r in res.results], axis=0)
```

```python
import numpy as np
from contextlib import ExitStack
import concourse.bass as bass
import concourse.mybir as mybir
from concourse.bass_utils import run_bass_kernel_spmd

F32 = mybir.dt.float32
BF16 = mybir.dt.bfloat16
AF = mybir.ActivationFunctionType
ALU = mybir.AluOpType
AX = mybir.AxisListType

D = 1024
KC = 8
DEPTH = 4
SEQ = 4096
NCORES = 8
EPS = 1e-6
INW = 4872
FH = 2816
NFC = 22
SEM_ROLL = 15000
PATTERNS = ((128, 1), (512, 4), (2048, 16))
CUT = 99
ROPE_THETA = 500000.0


class Tile:
    def __init__(self, k, t, name, dram=False):
        self.k, self.t, self.name, self.dram = k, t, name, dram
        self.writes = {}
        self.reads = {}
        self.dsem = None
        self.dcnt = 0

    def __getitem__(self, idx):
        return TV(self, self.t[idx])

    def v(self, ap):
        return TV(self, ap)


class TV:
    def __init__(self, tile, ap, extra=()):
        self.tile, self.ap, self.extra = tile, ap, extra


def _m(d, s):
    for sem, v in s.items():
        if d.get(sem, 0) < v:
            d[sem] = v


class K:
    def __init__(self, nc, es):
        self.nc, self.es = nc, es
        self.engs = {'pe': nc.tensor, 'act': nc.scalar, 'dve': nc.vector, 'pool': nc.gpsimd, 'sp': nc.sync}
        self.cnt = {}
        self.sem = {}
        self.own = {e: set() for e in self.engs}
        self.waited = {e: {} for e in self.engs}
        self.nsem = 0
        for e in self.engs:
            self._newsem(e)
        self.dma_tiles = []
        self.dpool = {'sp': [], 'pool': []}
        self.scope = None
        self.ninst = {e: 0 for e in self.engs}

    def _newsem(self, e):
        s = self.es.enter_context(self.nc.semaphore(f"s_{e}_{self.nsem}"))
        self.nsem += 1
        self.sem[e] = s
        self.cnt[e] = 0
        self.own[e].add(s)

    def sb(self, name, shape, dt, es=None):
        self.nsem += 1
        name = f"{name}_{self.nsem}"
        t = (es or self.es).enter_context(self.nc.sbuf_tensor(name, list(shape), dt))
        tl = Tile(self, t, name)
        if es is not None and self.scope is not None:
            self.scope.append(tl)
        return tl

    def begin_phase(self):
        self.scope = []

    def end_phase(self):
        self.barrier()
        for tl in self.scope:
            if tl.dsem is not None:
                if 16 * tl.dcnt < 2000:
                    self.dpool[tl.dq].append((tl.dsem, tl.dcnt))
                else:
                    self.retired = getattr(self, 'retired', 0) + 1
                self.dma_tiles.remove(tl)
                tl.dsem = None
        self.scope = None

    def ps(self, name, shape, dt, es=None):
        self.nsem += 1
        name = f"{name}_{self.nsem}"
        t = (es or self.es).enter_context(self.nc.psum_tensor(name, list(shape), dt))
        tl = Tile(self, t, name)
        tl.psum = True
        return tl

    def dram(self, ap, name):
        return Tile(self, ap, name, dram=True)

    @staticmethod
    def _tiles(tvs):
        out = []
        for tv in tvs:
            out.append(tv.tile)
            out.extend(tv.extra)
        return out

    def _waits(self, e, reads, writes):
        need = {}
        for t in self._tiles(reads):
            _m(need, t.writes)
            if getattr(t, 'psum', False):
                for sem, v in t.reads.items():
                    if sem not in self.own[e] and need.get(sem, 0) < v:
                        need[sem] = v
        for t in self._tiles(writes):
            _m(need, t.writes)
            _m(need, t.reads)
        eng = self.engs[e]
        w = self.waited[e]
        for sem, val in need.items():
            if e == 'pe' and sem in self.own['pe']:
                continue
            if w.get(sem, 0) < val:
                eng.wait_ge(sem, val)
                w[sem] = val
                self.ninst[e] += 1

    def _done(self, tok, reads, writes):
        sem, val = tok
        for t in self._tiles(reads):
            if t.reads.get(sem, 0) < val:
                t.reads[sem] = val
        for t in self._tiles(writes):
            if t.dram:
                if t.writes.get(sem, 0) < val:
                    t.writes[sem] = val
            else:
                t.writes = {sem: val}
            t.reads = {}

    def op(self, e, emit, reads, writes):
        self._waits(e, reads, writes)
        ins = emit(self.engs[e])
        if self.cnt[e] >= SEM_ROLL:
            self._newsem(e)
        self.cnt[e] += 1
        ins.then_inc(self.sem[e], 1)
        self.ninst[e] += 1
        tok = (self.sem[e], self.cnt[e])
        self._done(tok, reads, writes)
        return tok

    def dma(self, q, out, in_, **kw):
        own = out.tile if not out.tile.dram else in_.tile
        assert not own.dram
        if own.dsem is None:
            own.dq = q
            if self.dpool[q]:
                own.dsem, own.dcnt = self.dpool[q].pop(0)
            else:
                own.dsem = self.es.enter_context(self.nc.semaphore(f"d_{self.nsem}"))
                self.nds = getattr(self, 'nds', 0) + 1
                self.nsem += 1
            self.dma_tiles.append(own)
        assert own.dq == q, (own.name, own.dq, q)
        self._waits(q, [in_], [out])
        ins = self.engs[q].dma_start(out=out.ap, in_=in_.ap, **kw)
        own.dcnt += 1
        ins.then_inc(own.dsem, 16)
        self.ninst[q] += 1
        tok = (own.dsem, 16 * own.dcnt)
        self._done(tok, [in_], [out])
        return tok

    def load_w(self, dst, src2d, nk, cols, RO, q='pool', cstep=4096):
        for kc in range(nk):
            for c0 in range(0, cols, cstep):
                c1 = min(cols, c0 + cstep)
                self.dma(q, dst[:, kc, c0:c1], RO(src2d[kc * 128:(kc + 1) * 128, c0:c1]))

    def barrier(self):
        need = {}
        for e in self.engs:
            if self.cnt[e] > 0:
                need[self.sem[e]] = self.cnt[e]
        for t in self.dma_tiles:
            need[t.dsem] = 16 * t.dcnt
        for e in self.engs:
            w = self.waited[e]
            for sem, val in need.items():
                if e == 'pe' and sem in self.own['pe']:
                    continue
                if w.get(sem, 0) < val:
                    self.engs[e].wait_ge(sem, val)
                    w[sem] = val
                    self.ninst[e] += 1

    def mm(self, out, lhsT, rhs, start=True, stop=True):
        return self.op('pe', lambda g: g.matmul(out.ap, lhsT.ap, rhs.ap, start=start, stop=stop),
                       [lhsT, rhs], [out])

    def tr(self, out, in_, ident):
        return self.op('pe', lambda g: g.transpose(out.ap, in_.ap, ident.ap), [in_, ident], [out])

    def act(self, out, in_, func, bias=None, scale=None, accum=None):
        kw = {}
        rd = [in_]
        wr = [out]
        if bias is not None:
            if isinstance(bias, TV):
                kw['bias'] = bias.ap
                rd.append(bias)
            else:
                kw['bias'] = bias
        if scale is not None:
            if isinstance(scale, TV):
                kw['scale'] = scale.ap
                rd.append(scale)
            else:
                kw['scale'] = scale
        if accum is not None:
            kw['accum_out'] = accum.ap
            wr.append(accum)
        return self.op('act', lambda g: g.activation(out=out.ap, in_=in_.ap, func=func, **kw), rd, wr)

    def tt(self, e, out, in0, in1, op):
        return self.op(e, lambda g: g.tensor_tensor(out=out.ap, in0=in0.ap, in1=in1.ap, op=op), [in0, in1], [out])

    def ts(self, e, out, in0, s1, s2, op0, op1=None):
        rd = [in0]
        a1, a2 = s1, s2
        if isinstance(s1, TV):
            rd.append(s1)
            a1 = s1.ap
        if isinstance(s2, TV):
            rd.append(s2)
            a2 = s2.ap
        if op1 is None:
            return self.op(e, lambda g: g.tensor_scalar(out=out.ap, in0=in0.ap, scalar1=a1, scalar2=None, op0=op0),
                           rd, [out])
        return self.op(e, lambda g: g.tensor_scalar(out=out.ap, in0=in0.ap, scalar1=a1, scalar2=a2, op0=op0, op1=op1),
                       rd, [out])

    def stt(self, e, out, in0, s, in1, op0, op1):
        rd = [in0, in1]
        a = s
        if isinstance(s, TV):
            rd.append(s)
            a = s.ap
        return self.op(e, lambda g: g.scalar_tensor_tensor(out=out.ap, in0=in0.ap, scalar=a, in1=in1.ap,
                                                           op0=op0, op1=op1), rd, [out])

    def copy(self, e, out, in_):
        if e == 'act':
            return self.op(e, lambda g: g.copy(out=out.ap, in_=in_.ap), [in_], [out])
        return self.op(e, lambda g: g.tensor_copy(out=out.ap, in_=in_.ap), [in_], [out])

    def amul(self, out, in_, m):
        return self.op('act', lambda g: g.mul(out=out.ap, in_=in_.ap, mul=m.ap), [in_, m], [out])

    def memset(self, e, out, val):
        return self.op(e, lambda g: g.memset(out.ap, val), [], [out])

    def recip(self, out, in_):
        return self.op('dve', lambda g: g.reciprocal(out=out.ap, in_=in_.ap), [in_], [out])

    def reduce(self, e, out, in_, op):
        return self.op(e, lambda g: g.tensor_reduce(out=out.ap, in_=in_.ap, axis=AX.X, op=op), [in_], [out])


def build(T=SEQ, depth=DEPTH, debug=False, phases='MNABCDF'):
    NT = T // 128
    NB = T // 512
    nc = bass.Bass("TRN2", target_bir_lowering=False)
    es = ExitStack()

    def din(name, shape):
        return nc.dram_tensor(name, list(shape), F32, kind="ExternalInput").ap()

    x_in = din("x", [T, D])
    c_in = din("c", [128, KC])
    w_mod = din("w_mod", [depth, D, 6 * D])
    b_mod = din("b_mod", [depth, 6 * D])
    mixw = din("mix_norm_w", [depth, 128, KC])
    ffnw = din("ffn_norm_w", [depth, 128, KC])
    w_in = din("w_in", [depth, D, INW])
    w_out = din("w_out", [depth, D, D])
    convw = din("dn_conv_w", [depth, 128, 12, 4])
    a_log = din("dn_a_log", [depth, 4])
    dt_b = din("dn_dt_bias", [depth, 4])
    onw_in = din("dn_out_norm_w", [depth, 128])
    lng_in = din("gm_ln_g", [depth, 256])
    lnb_in = din("gm_ln_b", [depth, 256])
    wsT_in = din("gm_w_sT", [depth, 128, 4, 128])
    bs_in = din("gm_b_s", [depth, 128, 4])
    qnw_in = din("sw_q_norm_w", [depth, 64])
    knw_in = din("sw_k_norm_w", [depth, 64])
    wf_in = din("w_ffn_in", [depth, D, 2 * FH])
    wf_out = din("w_ffn_out", [depth, FH, D])
    cst_in = din("consts", [128, 8, 128])
    rot_in = din("rot", [T, 32])
    out_d = nc.dram_tensor("out", [T, D], F32, kind="ExternalOutput").ap()
    skind = "ExternalOutput" if debug else "Internal"
    modrow_d = nc.dram_tensor("modrow", [depth, 6 * D], F32, kind=skind).ap()
    yT_d = nc.dram_tensor("yT", [NT, 128, 768], BF16, kind="Internal").ap()
    nl_d = [nc.dram_tensor(f"nl{p}", [T, 260], F32, kind=skind).ap() for p in range(3)]
    dbg_d = nc.dram_tensor("dbg", [T, D], F32, kind=skind).ap() if debug else None

    with es:
        k = K(nc, es)
        xin_t = [k.dram(x_in[i * 128:(i + 1) * 128, :], f"xin{i}") for i in range(NT)]
        out_t = [k.dram(out_d[i * 128:(i + 1) * 128, :], f"out{i}") for i in range(NT)]
        wts = k.dram(w_mod, "wts")
        modrow_t = k.dram(modrow_d, "modrow")
        yT_t = [k.dram(yT_d[i], f"yT{i}") for i in range(NT)]
        nl_t = [k.dram(nl_d[p], f"nl{p}") for p in range(3)]
        dbg_t = k.dram(dbg_d, "dbg") if debug else None

        def RO(ap):
            return wts.v(ap)

        hT = k.sb("hT", [128, KC, T], BF16)
        hT_b = [Tile(k, hT.t, f"hTb{b}") for b in range(NB)]
        cst = k.sb("cst", [128, 8, 128], F32)
        identb = k.sb("identb", [128, 128], BF16)
        maskCb = k.sb("maskCb", [128, 128], BF16)
        maskPb = k.sb("maskPb", [128, 128], BF16)
        epsc = k.sb("epsc", [128, 1], F32)
        k.dma('sp', cst[:], RO(cst_in))
        ident = cst[:, 0, :]
        Um = cst[:, 1, :]
        SLm = cst[:, 2, :]
        ones = cst[:, 3, :]
        maskS = cst[:, 4, :]
        maskUi = cst[:, 5, :]
        maskP = cst[:, 6, :]
        k.copy('dve', identb[:], ident)
        k.copy('dve', maskCb[:], maskUi)
        k.copy('dve', maskPb[:], maskP)
        k.memset('dve', epsc[:], EPS)

        a1 = k.sb("a1", [128, KC], F32)
        b1 = k.sb("b1", [128, KC], F32)
        a2 = k.sb("a2", [128, KC], F32)
        b2 = k.sb("b2", [128, KC], F32)
        g1bc = k.sb("g1bc", [128, D], F32)
        g2bc = k.sb("g2bc", [128, D], F32)
        nw1 = k.sb("nw1", [128, KC], F32)
        nw2 = k.sb("nw2", [128, KC], F32)
        mcol = k.sb("mcol", [128, 4, KC], F32)

        with ExitStack() as pes:
            k.begin_phase()
            c_sb = k.sb("c_sb", [128, KC], F32, pes)
            cact = k.sb("cact", [128, KC], F32, pes)
            wm = [k.sb(f"wm{i}", [128, KC, 512], F32, pes) for i in range(2)]
            bm = k.sb("bm", [1, 6 * D], F32, pes)
            mr = k.sb("mr", [1, 6 * D], F32, pes)
            pm = [k.ps(f"pm{i}", [128, 512], F32, pes) for i in range(2)]
            k.dma('sp', c_sb[:], RO(c_in))
            k.act(cact[:], c_sb[:], AF.Silu)
            it = 0
            for l in range(depth):
                k.dma('sp', bm[:], RO(b_mod[l:l + 1, :]))
                for ec in range(12):
                    w = wm[it % 2]
                    p = pm[it % 2]
                    q = 'sp' if it % 2 == 0 else 'pool'
                    k.dma(q, w[:], RO(w_mod[l, :, ec * 512:(ec + 1) * 512].rearrange("(kc p) e -> p kc e", p=128)))
                    for kc in range(KC):
                        k.mm(p[0:1, :], cact[:, kc:kc + 1], w[:, kc, :], start=(kc == 0), stop=(kc == KC - 1))
                    k.tt('dve', mr[0:1, ec * 512:(ec + 1) * 512], p[0:1, :], bm[0:1, ec * 512:(ec + 1) * 512], ALU.add)
                    it += 1
                k.dma('sp', modrow_t.v(modrow_d[l:l + 1, :]), mr[:])
            k.end_phase()

        def load_layer_params(l):
            for j, src in enumerate((0, 1, 3, 4)):
                k.dma('sp', mcol[:, j, :],
                      modrow_t.v(modrow_d[l, src * D:(src + 1) * D].rearrange("(kc p) -> p kc", p=128)),
                      allow_slow_non_contiguous=True)
            k.dma('sp', g1bc[:], modrow_t.v(modrow_d[l, 2 * D:3 * D].partition_broadcast(128)))
            k.dma('sp', g2bc[:], modrow_t.v(modrow_d[l, 5 * D:6 * D].partition_broadcast(128)))
            k.dma('sp', nw1[:], RO(mixw[l]))
            k.dma('sp', nw2[:], RO(ffnw[l]))
            k.stt('dve', a1[:], mcol[:, 1, :], 1.0, nw1[:], ALU.add, ALU.mult)
            k.copy('dve', b1[:], mcol[:, 0, :])
            k.stt('dve', a2[:], mcol[:, 3, :], 1.0, nw2[:], ALU.add, ALU.mult)
            k.copy('dve', b2[:], mcol[:, 2, :])

        def norm_to_hT(xt, tt, aa, bb, sq, ss, rs, xn, ptr):
            hb = hT_b[tt // 4]
            k.memset('dve', ss[:, 0:1], 0.0)
            k.act(sq[:], xt[:], AF.Square, accum=ss[:, 0:1])
            k.act(rs[:, 0:1], ss[:, 0:1], AF.Sqrt, bias=epsc[:, 0:1], scale=1.0 / D)
            k.recip(rs[:, 1:2], rs[:, 0:1])
            k.ts('dve', xn[:], xt[:], rs[:, 1:2], None, ALU.mult)
            for half in range(2):
                p = ptr[half]
                for j in range(4):
                    kc = half * 4 + j
                    k.tr(p[:, j, :], xn[:, kc * 128:(kc + 1) * 128], ident)
                for j in range(4):
                    kc = half * 4 + j
                    e = 'dve' if j % 2 == 0 else 'pool'
                    if e == 'pool':
                        e = 'dve'
                    k.ts(e, hb.v(hT.t[:, kc, tt * 128:(tt + 1) * 128]), p[:, j, :], aa[:, kc:kc + 1], bb[:, kc:kc + 1],
                         ALU.mult, ALU.add)

        for l in range(depth):
            load_layer_params(l)
            xsrc = xin_t if l == 0 else out_t

            with ExitStack() as pes:
                k.begin_phase()
                xt = [k.sb(f"n1x{i}", [128, D], F32, pes) for i in range(2)]
                sq = k.sb("n1sq", [128, D], F32, pes)
                xn = [k.sb(f"n1xn{i}", [128, D], F32, pes) for i in range(2)]
                ss = [k.sb(f"n1ss{i}", [128, 1], F32, pes) for i in range(2)]
                rs = [k.sb(f"n1rs{i}", [128, 2], F32, pes) for i in range(2)]
                ptr = [[k.ps(f"n1p{i}{h}", [128, 4, 128], F32, pes) for h in range(2)] for i in range(2)]
                for tt in range(NT):
                    i = tt % 2
                    k.dma('sp', xt[i][:], xsrc[tt][:])
                    norm_to_hT(xt[i], tt, a1, b1, sq, ss[i], rs[i], xn[i], ptr[i])
                k.end_phase()

            if 'A' in phases:
                phase_A(k, l, T, hT, hT_b, RO, w_in, convw, a_log, dt_b, onw_in, cst, identb, epsc, yT_t, yT_d)
            if 'B' in phases:
                phase_B(k, l, T, hT, hT_b, RO, w_in, lng_in, lnb_in, wsT_in, bs_in, cst, identb, epsc, yT_t, yT_d)
            if 'C' in phases:
                phase_C(k, l, T, hT, hT_b, RO, w_in, qnw_in, knw_in, rot_in, identb, maskCb, maskPb, epsc, nl_t, nl_d)

            with ExitStack() as pes:
                if 'D' not in phases:
                    break
                k.begin_phase()
                wo = k.sb("wo", [128, KC, D], BF16, pes)
                k.load_w(wo, w_out[l], KC, D, RO)
                nl = [[k.sb(f"dnl{i}{p}", [128, 4, 65], F32, pes) for p in range(3)] for i in range(2)]
                nsum = k.sb("dnsum", [128, 4, 65], F32, pes)
                rl = k.sb("drl", [128, 4], F32, pes)
                ycb = k.sb("dycb", [128, 4, 64], BF16, pes)
                yT = [k.sb(f"dyT{i}", [128, 8, 128], BF16, pes) for i in range(2)]
                xt = [k.sb(f"dx{i}", [128, D], F32, pes) for i in range(2)]
                xo = [k.sb(f"dxo{i}", [128, D], F32, pes) for i in range(2)]
                tmp = k.sb("dtmp", [128, D], F32, pes)
                sq = k.sb("dsq", [128, D], F32, pes)
                xn = k.sb("dxn", [128, D], F32, pes)
                ss = k.sb("dss", [128, 1], F32, pes)
                rs = k.sb("drs", [128, 2], F32, pes)
                ptc = k.ps("dptc", [128, 8, 128], BF16, pes)
                po = [k.ps(f"dpo{h}", [128, 512], F32, pes) for h in range(2)]
                ptr = [k.ps(f"dptr{h}", [128, 4, 128], F32, pes) for h in range(2)]
                for tt in range(NT):
                    i = tt % 2
                    for p in range(3):
                        k.dma('sp', nl[i][p][:], nl_t[p].v(nl_d[p][tt * 128:(tt + 1) * 128, :].rearrange("p (h e) -> p h e", h=4)))
                    k.dma('sp', yT[i][:, 0:6, :], yT_t[tt].v(yT_d[tt].rearrange("p (c t) -> p c t", c=6)))
                    k.dma('sp', xt[i][:], xsrc[tt][:])
                    k.tt('dve', nsum[:], nl[i][0][:], nl[i][1][:], ALU.add)
                    k.tt('dve', nsum[:], nsum[:], nl[i][2][:], ALU.add)
                    k.recip(rl[:], nsum[:, :, 64])
                    k.tt('dve', ycb[:], nsum[:, :, 0:64], rl.v(rl.t[:, :].unsqueeze(2).to_broadcast([128, 4, 64])), ALU.mult)
                    for j in range(2):
                        k.tr(ptc[:, j, :], ycb.v(ycb.t[:, 2 * j:2 * j + 2, :].rearrange("p h e -> p (h e)")), identb[:])
                    k.copy('act', yT[i][:, 6:8, :], ptc[:, 0:2, :])
                    for half in range(2):
                        for c in range(8):
                            k.mm(po[half][:], yT[i][:, c, :], wo[:, c, half * 512:(half + 1) * 512],
                                 start=(c == 0), stop=(c == 7))
                    for half in range(2):
                        sl = slice(half * 512, (half + 1) * 512)
                        k.tt('dve', tmp[:, sl], po[half][:], g1bc[:, sl], ALU.mult)
                    k.tt('pool', xo[i][:], tmp[:], xt[i][:], ALU.add)
                    k.dma('sp', out_t[tt][:], xo[i][:])
                    norm_to_hT(xo[i], tt, a2, b2, sq, ss, rs, xn, ptr)
                k.end_phase()

            groups = [(0, 6), (6, 6), (12, 5), (17, 5)] if 'F' in phases else []
            if groups:
                with ExitStack() as pes:
                    k.begin_phase()
                    wg = [k.sb(f"fwg{i}", [128, KC, 768], BF16, pes) for i in range(2)]
                    wu = [k.sb(f"fwu{i}", [128, KC, 768], BF16, pes) for i in range(2)]
                    wo2 = [k.sb(f"fwo{i}", [128, 6, D], BF16, pes) for i in range(2)]
                    actT = [k.sb(f"fact{i}", [128, 6, 512], BF16, pes) for i in range(2)]
                    sg = [k.sb(f"fsg{i}", [128, 512], F32, pes) for i in range(2)]
                    xt = [k.sb(f"fx{i}", [128, D], F32, pes) for i in range(2)]
                    xo = [k.sb(f"fxo{i}", [128, D], F32, pes) for i in range(2)]
                    tmp = k.sb("ftmp", [128, D], F32, pes)
                    pg = [k.ps(f"fpg{i}", [128, 512], F32, pes) for i in range(2)]
                    pu = [k.ps(f"fpu{i}", [128, 512], F32, pes) for i in range(2)]
                    po = [[k.ps(f"fpo{i}{h}", [128, 512], F32, pes) for h in range(2)] for i in range(2)]

                    def loadw(gi):
                        f0, nf = groups[gi]
                        j = gi % 2
                        for kc in range(KC):
                            k.dma('pool', wg[j][:, kc, 0:nf * 128], RO(wf_in[l, kc * 128:(kc + 1) * 128, f0 * 128:(f0 + nf) * 128]))
                            k.dma('pool', wu[j][:, kc, 0:nf * 128], RO(wf_in[l, kc * 128:(kc + 1) * 128, FH + f0 * 128:FH + (f0 + nf) * 128]))
                        for c in range(nf):
                            k.dma('pool', wo2[j][:, c, :], RO(wf_out[l, (f0 + c) * 128:(f0 + c + 1) * 128, :]))

                    loadw(0)
                    it = 0
                    xi = 0
                    ab = 0
                    for gi, (f0, nf) in enumerate(groups):
                        if gi + 1 < len(groups):
                            loadw(gi + 1)
                        wj = gi % 2
                        for blk in range(NB):
                            hb = hT_b[blk]
                            a = actT[ab % 2]
                            ab += 1
                            for fi in range(nf):
                                j = it % 2
                                it += 1
                                for kc in range(KC):
                                    k.mm(pg[j][:], wg[wj][:, kc, fi * 128:(fi + 1) * 128], hb.v(hT.t[:, kc, blk * 512:(blk + 1) * 512]),
                                         start=(kc == 0), stop=(kc == KC - 1))
                                for kc in range(KC):
                                    k.mm(pu[j][:], wu[wj][:, kc, fi * 128:(fi + 1) * 128], hb.v(hT.t[:, kc, blk * 512:(blk + 1) * 512]),
                                         start=(kc == 0), stop=(kc == KC - 1))
                                k.act(sg[j][:], pg[j][:], AF.Silu)
                                k.tt('dve', a[:, fi, :], sg[j][:], pu[j][:], ALU.mult)
                            for t4 in range(4):
                                tt = blk * 4 + t4
                                i = xi % 2
                                xi += 1
                                k.dma('sp', xt[i][:], out_t[tt][:])
                                for half in range(2):
                                    for fi in range(nf):
                                        k.mm(po[i][half][:], a[:, fi, t4 * 128:(t4 + 1) * 128], wo2[wj][:, fi, half * 512:(half + 1) * 512],
                                             start=(fi == 0), stop=(fi == nf - 1))
                                for half in range(2):
                                    sl = slice(half * 512, (half + 1) * 512)
                                    k.tt('dve', tmp[:, sl], po[i][half][:], g2bc[:, sl], ALU.mult)
                                    k.tt('dve' if half == 0 else 'pool', xo[i][:, sl], tmp[:, sl], xt[i][:, sl], ALU.add)
                                k.dma('sp', out_t[tt][:], xo[i][:])
                    k.end_phase()
        k.barrier()
        build.ninst = dict(k.ninst)
        build.retired = (getattr(k, 'retired', 0), getattr(k, 'nds', 0), len(k.own['pe']) + len(k.own['act']) + len(k.own['dve']) + len(k.own['pool']) + len(k.own['sp']))
        build.maxd = (max([c for _, c in k.dpool['sp']] + [0]) * 16, max([c for _, c in k.dpool['pool']] + [0]) * 16, k.nsem)
    return nc


def phase_B(k, l, T, hT, hT_b, RO, w_in, lng_in, lnb_in, wsT_in, bs_in, cst, identb, epsc, yT_t, yT_d):
    NT = T // 128
    c0 = 2056
    maskUi = cst[:, 5, :]
    with ExitStack() as pes:
        k.begin_phase()
        wb = k.sb("bw", [128, KC, 512], BF16, pes)
        k.load_w(wb, w_in[l, :, c0:c0 + 512], KC, 512, RO)
        lng = k.sb("blng", [128, 256], F32, pes)
        lnb = k.sb("blnb", [128, 256], F32, pes)
        wsf = k.sb("bwsf", [128, 4, 128], F32, pes)
        wsb = k.sb("bwsb", [128, 4, 128], BF16, pes)
        bs = k.sb("bbs", [128, 4], F32, pes)
        k.dma('sp', lng[:], RO(lng_in[l].partition_broadcast(128)))
        k.dma('sp', lnb[:], RO(lnb_in[l].partition_broadcast(128)))
        k.dma('sp', wsf[:], RO(wsT_in[l]))
        k.dma('sp', bs[:], RO(bs_in[l]))
        k.tt('dve', wsb[:], wsf[:], maskUi.tile.v(cst.t[:, 5:6, :].to_broadcast([128, 4, 128])), ALU.mult)
        gl = [k.sb(f"bgl{i}", [128, 512], F32, pes) for i in range(2)]
        st = k.sb("bst", [128, 6], F32, pes)
        mv = k.sb("bmv", [128, 2], F32, pes)
        rs = k.sb("brs", [128, 2], F32, pes)
        vn = k.sb("bvn", [128, 256], F32, pes)
        vn2 = k.sb("bvn2", [128, 256], F32, pes)
        vnb = [k.sb(f"bvnb{i}", [128, 256], BF16, pes) for i in range(2)]
        yb = [k.sb(f"byb{i}", [128, 256], BF16, pes) for i in range(2)]
        ybT = [k.sb(f"bybT{i}", [128, 2, 128], BF16, pes) for i in range(2)]
        pp = [k.ps(f"bpp{i}", [128, 512], F32, pes) for i in range(2)]
        psv = [k.ps(f"bpsv{i}", [128, 512], F32, pes) for i in range(2)]
        ptr = [k.ps(f"bptr{i}", [128, 8, 128], BF16, pes) for i in range(2)]
        for tt in range(NT):
            i = tt % 2
            hb = hT_b[tt // 4]
            for kc in range(KC):
                k.mm(pp[i][:], hb.v(hT.t[:, kc, tt * 128:(tt + 1) * 128]), wb[:, kc, :], start=(kc == 0), stop=(kc == KC - 1))
            k.act(gl[i][:], pp[i][:], AF.Gelu)
            k.op('dve', lambda g: g.bn_stats(out=st.t[:, 0:6], in_=gl[i].t[:, 256:512]), [gl[i][:]], [st[:]])
            k.op('dve', lambda g: g.bn_aggr(out=mv.t[:, 0:2], in_=st.t[:, 0:6]), [st[:]], [mv[:]])
            k.act(rs[:, 0:1], mv[:, 1:2], AF.Sqrt, bias=epsc[:, 0:1], scale=1.0)
            k.recip(rs[:, 1:2], rs[:, 0:1])
            k.ts('dve', vn[:], gl[i][:, 256:512], mv[:, 0:1], rs[:, 1:2], ALU.subtract, ALU.mult)
            k.tt('pool', vn2[:], vn[:], lng[:], ALU.mult)
            k.tt('pool', vnb[i][:], vn2[:], lnb[:], ALU.add)
            for g in range(4):
                k.mm(psv[i][:, g * 64:(g + 1) * 64], wsb[:, g, :], vnb[i][:, g * 64:(g + 1) * 64])
            for g in range(4):
                k.stt('dve', yb[i][:, g * 64:(g + 1) * 64], psv[i][:, g * 64:(g + 1) * 64], bs[:, g:g + 1],
                      gl[i][:, g * 64:(g + 1) * 64], ALU.add, ALU.mult)
            for j in range(2):
                k.tr(ptr[i][:, j, :], yb[i][:, j * 128:(j + 1) * 128], identb[:])
            k.copy('act', ybT[i][:], ptr[i][:, 0:2, :])
            k.dma('sp', yT_t[tt].v(yT_d[tt][:, 512:768].rearrange("p (c t) -> p c t", c=2)), ybT[i][:])
        k.end_phase()


def phase_C(k, l, T, hT, hT_b, RO, w_in, qnw_in, knw_in, rot_in, identb, maskCb, maskPb, epsc, nl_t, nl_d):
    c0 = 2568
    with ExitStack() as pes:
        k.begin_phase()
        wc = k.sb("cw", [128, KC, 2304], BF16, pes)
        k.load_w(wc, w_in[l, :, c0:c0 + 2304], KC, 2304, RO)
        nwq = k.sb("cnwq", [128, 64], F32, pes)
        nwk = k.sb("cnwk", [128, 64], F32, pes)
        nw = k.sb("cnw", [128, 8, 64], F32, pes)
        k.dma('sp', nwq[:], RO(qnw_in[l].partition_broadcast(128)))
        k.dma('sp', nwk[:], RO(knw_in[l].partition_broadcast(128)))
        for h in range(4):
            k.copy('dve', nw[:, h, :], nwq[:])
            k.copy('dve', nw[:, 4 + h, :], nwk[:])
        rot = [k.sb(f"crot{i}", [128, 32], F32, pes) for i in range(2)]
        sqt = k.sb("csq", [128, 512], F32, pes)
        ss = k.sb("css", [128, 8], F32, pes)
        rs = k.sb("crs", [128, 8], F32, pes)
        ri = k.sb("cri", [128, 8], F32, pes)
        qkf = k.sb("cqkf", [128, 8, 64], F32, pes)
        qkg = k.sb("cqkg", [128, 8, 64], F32, pes)
        ra = k.sb("cra", [128, 8, 16], F32, pes)
        rb = k.sb("crb", [128, 8, 16], F32, pes)
        qkb = k.sb("cqkb", [128, 8, 64], BF16, pes)
        v1 = [k.sb(f"cv1{i}", [128, 4, 80], BF16, pes) for i in range(3)]
        qT = [k.sb(f"cqT{i}", [64, 4, 128], BF16, pes) for i in range(2)]
        kT = [k.sb(f"ckT{i}", [64, 4, 128], BF16, pes) for i in range(3)]
        P0 = k.sb("cP0", [128, 4, 128], BF16, pes)
        P1 = k.sb("cP1", [128, 4, 128], BF16, pes)
        P0m = k.sb("cP0m", [128, 4, 128], BF16, pes)
        P1m = k.sb("cP1m", [128, 4, 128], BF16, pes)
        nlo = [k.sb(f"cnlo{i}", [128, 4, 65], F32, pes) for i in range(2)]
        pc0 = k.ps("cpc0", [128, 512], F32, pes)
        pc1 = k.ps("cpc1", [128, 512], F32, pes)
        ptr = k.ps("cptr", [128, 8, 128], BF16, pes)
        pS0 = k.ps("cpS0", [128, 4, 128], F32, pes)
        pS1 = k.ps("cpS1", [128, 4, 128], F32, pes)
        pO = k.ps("cpO", [128, 4, 128], F32, pes)
        for i in range(3):
            k.memset('pool', v1[i][:], 1.0)
        mC = maskCb.v(maskCb.t[:, :].unsqueeze(1).to_broadcast([128, 4, 128]))
        mP = maskPb.v(maskPb.t[:, :].unsqueeze(1).to_broadcast([128, 4, 128]))
        tiles = []
        for p, (window, d) in enumerate(PATTERNS):
            nb = (T // d) // 128
            for r in range(d):
                for n in range(nb):
                    start = 128 * n * d + r
                    tiles.append((p, d, n, start, start + 127 * d + 1))

        def s1(i):
            p, d, n, start, stop_ = tiles[i]
            cur = i % 3
            b0 = start // 512
            b1 = (stop_ - 1) // 512
            hbs = [hT_b[b] for b in range(b0, b1 + 1)]
            rt = rot[i % 2]
            k.dma('sp', rt[:], RO(rot_in[start:stop_:d, :]))
            for kc in range(KC):
                lhs = TV(hbs[0], hT.t[:, kc, start:stop_:d], tuple(hbs[1:]))
                k.mm(pc0[:], lhs, wc[:, kc, p * 768:p * 768 + 512], start=(kc == 0), stop=(kc == KC - 1))
            yield
            for kc in range(KC):
                lhs = TV(hbs[0], hT.t[:, kc, start:stop_:d], tuple(hbs[1:]))
                k.mm(pc1[:, 0:256], lhs, wc[:, kc, p * 768 + 512:p * 768 + 768], start=(kc == 0), stop=(kc == KC - 1))
            k.act(sqt[:], pc0[:], AF.Square)
            yield
            k.reduce('dve', ss[:], sqt.v(sqt.t[:, :].rearrange("p (g e) -> p g e", g=8)), ALU.add)
            k.act(rs[:], ss[:], AF.Ln, bias=epsc[:, 0:1], scale=1.0 / 64)
            k.act(ri[:], rs[:], AF.Exp, scale=-0.5)
            k.copy('act', v1[cur][:, :, 0:64], pc1.v(pc1.t[:, 0:256].rearrange("p (h e) -> p h e", h=4)))
            yield
            k.tt('dve', qkf[:], pc0.v(pc0.t[:, :].rearrange("p (g e) -> p g e", g=8)),
                 ri.v(ri.t[:, :].unsqueeze(2).to_broadcast([128, 8, 64])), ALU.mult)
            yield
            k.tt('dve', qkg[:], qkf[:], nw[:], ALU.mult)
            yield
            C2 = rt.v(rt.t[:, 0:16].unsqueeze(1).to_broadcast([128, 8, 16]))
            Sa = rt.v(rt.t[:, 16:24].unsqueeze(1).to_broadcast([128, 8, 8]))
            Sb = rt.v(rt.t[:, 24:32].unsqueeze(1).to_broadcast([128, 8, 8]))
            k.tt('dve', ra[:], qkg[:, :, 0:16], C2, ALU.mult)
            k.tt('dve', rb[:, :, 0:8], qkg[:, :, 8:16], Sa, ALU.mult)
            k.tt('dve', rb[:, :, 8:16], qkg[:, :, 0:8], Sb, ALU.mult)
            k.copy('pool', qkb[:, :, 16:64], qkg[:, :, 16:64])
            yield
            k.tt('dve', qkb[:, :, 0:16], ra[:], rb[:], ALU.add)
            yield
            for j in range(8):
                k.tr(ptr[0:64, j, :], qkb[:, j, :], identb[:])
            yield
            k.copy('act', qT[i % 2][:], ptr[0:64, 0:4, :])
            k.copy('act', kT[cur][:], ptr[0:64, 4:8, :])
            yield

        def s2(i):
            p, d, n, start, stop_ = tiles[i]
            cur = i % 3
            prev = (i - 1) % 3
            q_ = qT[i % 2]
            for h in range(4):
                k.mm(pS0[:, h, :], kT[cur][:, h, :], q_[:, h, :])
            yield
            k.act(P0[:], pS0[:], AF.Exp, scale=0.125)
            if n > 0:
                for h in range(4):
                    k.mm(pS1[:, h, :], kT[prev][:, h, :], q_[:, h, :])
            yield
            k.tt('dve', P0m[:], P0[:], mC, ALU.mult)
            if n > 0:
                k.act(P1[:], pS1[:], AF.Exp, scale=0.125)
                yield
                k.tt('dve', P1m[:], P1[:], mP, ALU.mult)
            yield
            for h in range(4):
                if n > 0:
                    k.mm(pO[:, h, 0:65], P1m[:, h, :], v1[prev][:, h, 0:65], start=True, stop=False)
                k.mm(pO[:, h, 0:65], P0m[:, h, :], v1[cur][:, h, 0:65], start=(n == 0), stop=True)
            yield
            o_ = nlo[i % 2]
            k.copy('act', o_[:], pO[:, :, 0:65])
            yield
            k.dma('sp', nl_t[p].v(nl_d[p][start:stop_:d, :].rearrange("p (h e) -> p h e", h=4)), o_[:])
            yield

        def rr(gens):
            gens = list(gens)
            while gens:
                for g in list(gens):
                    try:
                        next(g)
                    except StopIteration:
                        gens.remove(g)

        NTL = len(tiles)
        rr([s1(0)])
        for i in range(NTL):
            rr([s2(i)] + ([s1(i + 1)] if i + 1 < NTL else []))
        k.end_phase()


def phase_A(k, l, T, hT, hT_b, RO, w_in, convw_in, a_log, dt_b, onw_in, cst, identb, epsc, yT_t, yT_d):
    NT = T // 128
    NB = T // 512
    ident = cst[:, 0, :]
    Um = cst[:, 1, :]
    SLm = cst[:, 2, :]
    ones = cst[:, 3, :]
    maskS = cst[:, 4, :]
    maskUi = cst[:, 5, :]
    SCALE = 128.0 ** -0.5
    with ExitStack() as pes:
        k.begin_phase()
        wa = k.sb("aw", [128, KC, 2056], BF16, pes)
        k.load_w(wa, w_in[l, :, 0:2056], KC, 2056, RO)
        cw = k.sb("acw", [128, 12, 4], F32, pes)
        k.dma('sp', cw[:], RO(convw_in[l]))
        alog = k.sb("aalog", [128, 4], F32, pes)
        dtb = k.sb("adtb", [128, 4], F32, pes)
        nea = k.sb("anea", [128, 4], F32, pes)
        onw = k.sb("aonw", [128, 128], F32, pes)
        k.dma('sp', alog[:], RO(a_log[l].partition_broadcast(128)))
        k.dma('sp', dtb[:], RO(dt_b[l].partition_broadcast(128)))
        k.dma('sp', onw[:], RO(onw_in[l].partition_broadcast(128)))
        k.act(nea[:], alog[:], AF.Exp)
        k.ts('dve', nea[:], nea[:], -1.0, None, ALU.mult)
        halo = k.sb("ahalo", [128, 12, 3], F32, pes)
        k.memset('dve', halo[:], 0.0)
        raw = [k.sb(f"araw{i}", [128, 515], F32, pes) for i in range(2)]
        acc = [k.sb(f"aacc{i}", [128, 512], F32, pes) for i in range(2)]
        acc2 = [k.sb(f"aacc2{i}", [128, 512], F32, pes) for i in range(2)]
        qkv = k.sb("aqkv", [128, 12, 512], F32, pes)
        sqt, rsq, rinv = acc[0], acc[1], acc2[0]
        S = [[k.sb(f"aS{h}{i}", [128, 128], F32, pes) for i in range(2)] for h in range(4)]
        for h in range(4):
            k.memset('dve', S[h][0][:], 0.0)
        sz = k.sb("asz", [128, 512], F32, pes)
        sm = k.sb("asm", [128, 64], F32, pes)
        oall = k.sb("aoall", [128, 4, 128], F32, pes)
        osq = k.sb("aosq", [128, 128], F32, pes)
        oss = k.sb("aoss", [128, 4], F32, pes)
        ors = k.sb("aors", [128, 8], F32, pes)
        ot = k.sb("aot", [128, 128], F32, pes)
        ya = k.sb("aya", [128, 4, 128], F32, pes)
        yaT = [k.sb(f"ayaT{i}", [128, 4, 128], BF16, pes) for i in range(2)]
        names = ["kbg", "kdec", "vb", "GU", "eD", "eDT", "tA", "N", "NT", "tQ", "aqk", "TTa", "TTb",
                 "Pa", "PTa", "Pb", "PTb", "u", "wT", "vnew", "o2"]
        W = [{n: k.sb(f"a{n}{s}", [128, 128], F32, pes) for n in names} for s in range(4)]
        BK = [k.ps(f"abk{i}", [128, 512], F32, pes) for i in range(8)]
        pq = [BK[0], BK[1]]
        pz = BK[2]
        pm = BK[3]
        pty = BK[4]

        def sc(j):
            return sm[:, j:j + 1]

        pwi = [0]

        def nxt():
            t = pw[pwi[0] % 3]
            s = (pwi[0] // 3) % 4
            pwi[0] += 1
            return t[:, s, :]

        it = 0
        for blk in range(NB):
            hb = hT_b[blk]
            tok = slice(blk * 512, (blk + 1) * 512)
            for fc in range(12):
                j = it % 2
                it += 1
                for kc in range(KC):
                    k.mm(pq[j][:], wa[:, kc, fc * 128:(fc + 1) * 128], hb.v(hT.t[:, kc, tok]), start=(kc == 0), stop=(kc == KC - 1))
                k.copy('pool', raw[j][:, 0:3], halo[:, fc, :])
                k.copy('act', raw[j][:, 3:515], pq[j][:])
                k.copy('pool', halo[:, fc, :], raw[j][:, 512:515])
                k.ts('dve', acc[j][:], raw[j][:, 0:512], cw[:, fc, 0:1], None, ALU.mult)
                k.stt('dve', acc2[j][:], raw[j][:, 1:513], cw[:, fc, 1:2], acc[j][:], ALU.mult, ALU.add)
                k.stt('dve', acc[j][:], raw[j][:, 2:514], cw[:, fc, 2:3], acc2[j][:], ALU.mult, ALU.add)
                k.stt('dve', acc2[j][:], raw[j][:, 3:515], cw[:, fc, 3:4], acc[j][:], ALU.mult, ALU.add)
                k.act(qkv[:, fc, :], acc2[j][:], AF.Silu)
            for fc in range(8):
                j = it % 2
                it += 1
                k.act(sqt[:], qkv[:, fc, :], AF.Square)
                k.mm(pq[j][:], ones, sqt[:])
                k.act(rsq[:], pq[j][:], AF.Sqrt, bias=epsc[:, 0:1], scale=1.0)
                k.recip(rinv[:], rsq[:])
                k.tt('dve', qkv[:, fc, :], qkv[:, fc, :], rinv[:], ALU.mult)
            for t4 in range(4):
                tt = blk * 4 + t4
                cs = slice(t4 * 128, (t4 + 1) * 128)
                hts = slice(tt * 128, (tt + 1) * 128)
                for kc in range(KC):
                    k.mm(pz[:], hb.v(hT.t[:, kc, hts]), wa[:, kc, 1536:2048], start=(kc == 0), stop=(kc == KC - 1))
                for kc in range(KC):
                    k.mm(pm[:, 0:8], hb.v(hT.t[:, kc, hts]), wa[:, kc, 2048:2056], start=(kc == 0), stop=(kc == KC - 1))
                k.act(sz[:], pz[:], AF.Silu)
                k.tt('dve', sm[:, 0:4], pm[:, 0:4], dtb[:], ALU.add)
                k.act(sm[:, 4:8], sm[:, 0:4], AF.Exp)
                k.act(sm[:, 8:12], sm[:, 4:8], AF.Ln, bias=1.0)
                k.tt('dve', sm[:, 12:16], sm[:, 8:12], nea[:], ALU.mult)
                k.act(sm[:, 16:20], pm[:, 4:8], AF.Exp, scale=-1.0)
                k.ts('dve', sm[:, 16:20], sm[:, 16:20], 1.0, None, ALU.add)
                k.recip(sm[:, 20:24], sm[:, 16:20])
                k.ts('dve', sm[:, 24:28], sm[:, 20:24], -1.0, None, ALU.mult)
                k.mm(pm[:, 8:12], Um, sm[:, 12:16])
                k.mm(pm[:, 12:16], ones, sm[:, 12:16])
                k.copy('dve', sm[:, 28:32], pm[:, 8:12])
                k.copy('dve', sm[:, 36:40], pm[:, 12:16])
                k.act(sm[:, 32:36], sm[:, 28:32], AF.Exp)
                k.act(sm[:, 40:44], sm[:, 36:40], AF.Exp)
                k.tt('dve', sm[:, 44:48], sm[:, 36:40], sm[:, 28:32], ALU.subtract)
                k.act(sm[:, 48:52], sm[:, 44:48], AF.Exp)
                k.tt('dve', sm[:, 52:56], sm[:, 20:24], sm[:, 32:36], ALU.mult)
                k.ts('dve', sm[:, 56:60], sm[:, 32:36], SCALE, None, ALU.mult)
                def head_gen(h):
                    w = W[h]
                    qT_ = qkv[:, h, cs]
                    kT_ = qkv[:, 4 + h, cs]
                    vT_ = qkv[:, 8 + h, cs]
                    Sold = S[h][tt % 2]
                    Snew = S[h][(tt + 1) % 2]
                    cnt = [0]

                    def nx():
                        c = cnt[0]
                        cnt[0] += 1
                        bk = BK[2 * h + (c % 2)]
                        o = ((c // 2) % 4) * 128
                        return bk[:, o:o + 128]
                    p = nx()
                    k.tr(p, kT_, ident)
                    k.ts('dve', w["kbg"][:], p, sc(52 + h), None, ALU.mult)
                    k.amul(w["kdec"][:], p, sc(48 + h))
                    yield
                    p = nx()
                    k.tr(p, vT_, ident)
                    k.amul(w["vb"][:], p, sc(20 + h))
                    k.ts('dve', w["GU"][:], Um, sc(12 + h), None, ALU.mult)
                    yield
                    p = nx()
                    k.mm(p, w["GU"][:], SLm)
                    k.act(w["eD"][:], p, AF.Exp)
                    k.tt('pool', w["tA"][:], w["eD"][:], maskS, ALU.mult)
                    yield
                    p = nx()
                    k.mm(p, SLm, w["GU"][:])
                    k.act(w["eDT"][:], p, AF.Exp)
                    k.tt('pool', w["tQ"][:], w["eDT"][:], maskUi, ALU.mult)
                    yield
                    p = nx()
                    k.mm(p, kT_, kT_)
                    k.stt('dve', w["N"][:], p, sc(24 + h), w["tA"][:], ALU.mult, ALU.mult)
                    yield
                    p = nx()
                    k.tr(p, w["N"][:], ident)
                    k.copy('act', w["NT"][:], p)
                    k.tt('pool', w["TTa"][:], w["NT"][:], ident, ALU.add)
                    yield
                    p = nx()
                    k.mm(p, kT_, qT_)
                    k.stt('dve', w["aqk"][:], p, SCALE, w["tQ"][:], ALU.mult, ALU.mult)
                    yield
                    TT, TTo = w["TTa"], w["TTb"]
                    P_, PT_ = w["N"], w["NT"]
                    Pn, PTn = w["Pa"], w["PTa"]
                    for s_ in range(6):
                        p = nx()
                        k.mm(p, PT_[:], P_[:])
                        k.copy('act', Pn[:], p)
                        yield
                        if s_ < 5:
                            p2 = nx()
                            k.mm(p2, P_[:], PT_[:])
                            k.copy('dve' if s_ % 2 == 0 else 'act', PTn[:], p2)
                            yield
                        p3 = nx()
                        k.mm(p3, Pn[:], TT[:])
                        k.tt('dve', TTo[:], p3, TT[:], ALU.add)
                        yield
                        TT, TTo = TTo, TT
                        P_, PT_ = Pn, PTn
                        if s_ % 2 == 0:
                            Pn, PTn = w["Pb"], w["PTb"]
                        else:
                            Pn, PTn = w["Pa"], w["PTa"]
                    p = nx()
                    k.mm(p, TT[:], w["vb"][:])
                    k.copy('act', w["u"][:], p)
                    yield
                    p = nx()
                    k.mm(p, w["kbg"][:], TT[:])
                    k.copy('dve', w["wT"][:], p)
                    yield
                    p = nx()
                    k.mm(p, w["wT"][:], Sold[:])
                    k.stt('dve', w["vnew"][:], p, -1.0, w["u"][:], ALU.mult, ALU.add)
                    yield
                    p1 = nx()
                    k.mm(p1, qT_, Sold[:])
                    yield
                    p2 = nx()
                    k.mm(p2, w["aqk"][:], w["vnew"][:])
                    k.copy('act', w["o2"][:], p2)
                    k.stt('dve', oall[:, h, :], p1, sc(56 + h), w["o2"][:], ALU.mult, ALU.add)
                    yield
                    p = nx()
                    k.mm(p, w["kdec"][:], w["vnew"][:])
                    k.stt('dve', Snew[:], Sold[:], sc(40 + h), p, ALU.mult, ALU.add)
                    yield

                gens = [head_gen(h) for h in range(4)]
                while gens:
                    for g in list(gens):
                        try:
                            next(g)
                        except StopIteration:
                            gens.remove(g)
                k.memset('dve', oss[:], 0.0)
                for h in range(4):
                    k.act(osq[:], oall[:, h, :], AF.Square, accum=oss[:, h:h + 1])
                k.act(ors[:, 0:4], oss[:], AF.Sqrt, bias=epsc[:, 0:1], scale=1.0 / 128)
                k.recip(ors[:, 4:8], ors[:, 0:4])
                for h in range(4):
                    k.stt('dve', ot[:], oall[:, h, :], ors[:, 4 + h:5 + h], onw[:], ALU.mult, ALU.mult)
                    k.tt('dve', ya[:, h, :], ot[:], sz[:, h * 128:(h + 1) * 128], ALU.mult)
                for h in range(4):
                    k.tr(pty[:, h * 128:(h + 1) * 128], ya[:, h, :], ident)
                i = tt % 2
                k.copy('act', yaT[i][:], pty.v(pty.t[:, :].rearrange("p (h t) -> p h t", h=4)))
                k.dma('sp', yT_t[tt].v(yT_d[tt][:, 0:512].rearrange("p (c t) -> p c t", c=4)), yaT[i][:])
        k.end_phase()


def _consts():
    i = np.arange(128)
    c = np.zeros((128, 8, 128), np.float32)
    c[:, 0, :] = np.eye(128)
    c[:, 1, :] = (i[:, None] <= i[None, :])
    c[:, 2, :] = (i[:, None] > i[None, :])
    c[:, 3, :] = 1.0
    c[:, 4, :] = (i[:, None] > i[None, :])
    c[:, 5, :] = (i[:, None] <= i[None, :])
    c[:, 6, :] = (i[:, None] >= i[None, :])
    return c


def _rot(T):
    inv = (np.float32(ROPE_THETA) ** (-np.arange(0, 16, 2, dtype=np.float32) / np.float32(16))).astype(np.float32)
    ang = (np.arange(T, dtype=np.float32)[:, None] * inv[None, :]).astype(np.float32)
    cos, sin = np.cos(ang).astype(np.float32), np.sin(ang).astype(np.float32)
    return np.concatenate([cos, cos, -sin, sin], axis=1).astype(np.float32)


def make_in_maps(inputs, T, depth, ncores):
    f = lambda a: np.ascontiguousarray(np.asarray(a, dtype=np.float32))
    shared = {
        "w_mod": f(inputs["w_mod"]), "b_mod": f(inputs["b_mod"]),
        "mix_norm_w": f(np.asarray(inputs["mix_norm_w"]).reshape(depth, KC, 128).transpose(0, 2, 1)),
        "ffn_norm_w": f(np.asarray(inputs["ffn_norm_w"]).reshape(depth, KC, 128).transpose(0, 2, 1)),
        "w_in": f(inputs["w_in"]), "w_out": f(inputs["w_out"]),
        "dn_conv_w": f(np.asarray(inputs["dn_conv_w"]).transpose(0, 2, 1).reshape(depth, 12, 128, 4).transpose(0, 2, 1, 3)),
        "dn_a_log": f(inputs["dn_a_log"]), "dn_dt_bias": f(inputs["dn_dt_bias"]),
        "dn_out_norm_w": f(inputs["dn_out_norm_w"]),
        "gm_ln_g": f(inputs["gm_ln_g"]), "gm_ln_b": f(inputs["gm_ln_b"]),
        "gm_w_sT": f(np.asarray(inputs["gm_w_s"]).transpose(0, 3, 1, 2)),
        "gm_b_s": f(np.asarray(inputs["gm_b_s"]).transpose(0, 2, 1)),
        "sw_q_norm_w": f(inputs["sw_q_norm_w"]), "sw_k_norm_w": f(inputs["sw_k_norm_w"]),
        "w_ffn_in": f(inputs["w_ffn_in"]), "w_ffn_out": f(inputs["w_ffn_out"]),
        "consts": _consts(), "rot": _rot(T),
    }
    x = np.asarray(inputs["x"], dtype=np.float32)
    c = np.asarray(inputs["c"], dtype=np.float32)
    maps = []
    for b in range(ncores):
        m = dict(shared)
        m["x"] = np.ascontiguousarray(x[b])
        m["c"] = np.ascontiguousarray(c[b].reshape(KC, 128).T)
        maps.append(m)
    return maps


def kernel(**inputs):
    nc = build(SEQ, DEPTH, False)
    maps = make_in_maps(inputs, SEQ, DEPTH, NCORES)
    res = run_bass_kernel_spmd(nc, maps, core_ids=list(range(NCORES)))
    return np.stack([np.asarray(r["out"], dtype=np.float32) for r in res.results], axis=0)
```

```python
import numpy as np
from contextlib import ExitStack
import concourse.bass as bass
import concourse.mybir as mybir
from concourse.bass_utils import run_bass_kernel_spmd

F32 = mybir.dt.float32
BF16 = mybir.dt.bfloat16
AF = mybir.ActivationFunctionType
ALU = mybir.AluOpType
AX = mybir.AxisListType

D = 1024
KC = 8
DEPTH = 4
SEQ = 4096
NCORES = 8
EPS = 1e-6
INW = 4872
FH = 2816
NFC = 22
SEM_ROLL = 15000
PATTERNS = ((128, 1), (512, 4), (2048, 16))
CUT = 99
F32R = False
ROPE_THETA = 500000.0


class Tile:
    def __init__(self, k, t, name, dram=False):
        self.k, self.t, self.name, self.dram = k, t, name, dram
        self.writes = {}
        self.reads = {}
        self.dsem = None
        self.dcnt = 0

    def __getitem__(self, idx):
        return TV(self, self.t[idx])

    def v(self, ap):
        return TV(self, ap)


class TV:
    def __init__(self, tile, ap, extra=()):
        self.tile, self.ap, self.extra = tile, ap, extra


def _m(d, s):
    for sem, v in s.items():
        if d.get(sem, 0) < v:
            d[sem] = v


class K:
    def __init__(self, nc, es):
        self.nc, self.es = nc, es
        self.engs = {'pe': nc.tensor, 'act': nc.scalar, 'dve': nc.vector, 'pool': nc.gpsimd, 'sp': nc.sync}
        self.cnt = {}
        self.sem = {}
        self.own = {e: set() for e in self.engs}
        self.waited = {e: {} for e in self.engs}
        self.nsem = 0
        for e in self.engs:
            self._newsem(e)
        self.dma_tiles = []
        self.dpool = {'sp': [], 'pool': []}
        self.scope = None
        self.ninst = {e: 0 for e in self.engs}

    def _newsem(self, e):
        s = self.es.enter_context(self.nc.semaphore(f"s_{e}_{self.nsem}"))
        self.nsem += 1
        self.sem[e] = s
        self.cnt[e] = 0
        self.own[e].add(s)

    def sb(self, name, shape, dt, es=None):
        self.nsem += 1
        name = f"{name}_{self.nsem}"
        t = (es or self.es).enter_context(self.nc.sbuf_tensor(name, list(shape), dt))
        tl = Tile(self, t, name)
        if es is not None and self.scope is not None:
            self.scope.append(tl)
        return tl

    def begin_phase(self):
        self.scope = []

    def end_phase(self):
        self.barrier()
        for tl in self.scope:
            if tl.dsem is not None:
                if 16 * tl.dcnt < 2000:
                    self.dpool[tl.dq].append((tl.dsem, tl.dcnt))
                else:
                    self.retired = getattr(self, 'retired', 0) + 1
                self.dma_tiles.remove(tl)
                tl.dsem = None
        self.scope = None

    def ps(self, name, shape, dt, es=None):
        self.nsem += 1
        name = f"{name}_{self.nsem}"
        t = (es or self.es).enter_context(self.nc.psum_tensor(name, list(shape), dt))
        tl = Tile(self, t, name)
        tl.psum = True
        return tl

    def dram(self, ap, name):
        return Tile(self, ap, name, dram=True)

    @staticmethod
    def _tiles(tvs):
        out = []
        for tv in tvs:
            out.append(tv.tile)
            out.extend(tv.extra)
        return out

    def _waits(self, e, reads, writes):
        need = {}
        for t in self._tiles(reads):
            _m(need, t.writes)
            if getattr(t, 'psum', False):
                for sem, v in t.reads.items():
                    if sem not in self.own[e] and need.get(sem, 0) < v:
                        need[sem] = v
        for t in self._tiles(writes):
            _m(need, t.writes)
            _m(need, t.reads)
        eng = self.engs[e]
        w = self.waited[e]
        for sem, val in need.items():
            if e == 'pe' and sem in self.own['pe']:
                continue
            if w.get(sem, 0) < val:
                eng.wait_ge(sem, val)
                w[sem] = val
                self.ninst[e] += 1

    def _done(self, tok, reads, writes):
        sem, val = tok
        for t in self._tiles(reads):
            if t.reads.get(sem, 0) < val:
                t.reads[sem] = val
        for t in self._tiles(writes):
            if t.dram:
                if t.writes.get(sem, 0) < val:
                    t.writes[sem] = val
            else:
                t.writes = {sem: val}
            t.reads = {}

    def op(self, e, emit, reads, writes):
        self._waits(e, reads, writes)
        ins = emit(self.engs[e])
        if self.cnt[e] >= SEM_ROLL:
            self._newsem(e)
        self.cnt[e] += 1
        ins.then_inc(self.sem[e], 1)
        self.ninst[e] += 1
        tok = (self.sem[e], self.cnt[e])
        self._done(tok, reads, writes)
        return tok

    def dma(self, q, out, in_, **kw):
        own = out.tile if not out.tile.dram else in_.tile
        assert not own.dram
        if own.dsem is None:
            own.dq = q
            if self.dpool[q]:
                own.dsem, own.dcnt = self.dpool[q].pop(0)
            else:
                own.dsem = self.es.enter_context(self.nc.semaphore(f"d_{self.nsem}"))
                self.nds = getattr(self, 'nds', 0) + 1
                self.nsem += 1
            self.dma_tiles.append(own)
        assert own.dq == q, (own.name, own.dq, q)
        self._waits(q, [in_], [out])
        ins = self.engs[q].dma_start(out=out.ap, in_=in_.ap, **kw)
        own.dcnt += 1
        ins.then_inc(own.dsem, 16)
        self.ninst[q] += 1
        tok = (own.dsem, 16 * own.dcnt)
        self._done(tok, [in_], [out])
        return tok

    def load_w(self, dst, src2d, nk, cols, RO, q='pool', cstep=4096):
        for kc in range(nk):
            for c0 in range(0, cols, cstep):
                c1 = min(cols, c0 + cstep)
                self.dma(q, dst[:, kc, c0:c1], RO(src2d[kc * 128:(kc + 1) * 128, c0:c1]))

    def barrier(self):
        need = {}
        for e in self.engs:
            if self.cnt[e] > 0:
                need[self.sem[e]] = self.cnt[e]
        for t in self.dma_tiles:
            need[t.dsem] = 16 * t.dcnt
        for e in self.engs:
            w = self.waited[e]
            for sem, val in need.items():
                if e == 'pe' and sem in self.own['pe']:
                    continue
                if w.get(sem, 0) < val:
                    self.engs[e].wait_ge(sem, val)
                    w[sem] = val
                    self.ninst[e] += 1

    def mm(self, out, lhsT, rhs, start=True, stop=True):
        la, ra = lhsT.ap, rhs.ap
        if F32R and la.dtype == F32 and ra.dtype == F32:
            la = la.bitcast(mybir.dt.float32r)
            ra = ra.bitcast(mybir.dt.float32r)
        return self.op('pe', lambda g: g.matmul(out.ap, la, ra, start=start, stop=stop),
                       [lhsT, rhs], [out])

    def pe_batch(self, descrs):
        rd, wr = [], []
        for d in descrs:
            wr.append(d[1])
            rd.extend([d[2], d[3]])
        self._waits('pe', rd, wr)
        for d in descrs:
            if d[0] == 'mm':
                self.mm(d[1], d[2], d[3])
            else:
                self.tr(d[1], d[2], d[3])

    def tr(self, out, in_, ident):
        return self.op('pe', lambda g: g.transpose(out.ap, in_.ap, ident.ap), [in_, ident], [out])

    def act(self, out, in_, func, bias=None, scale=None, accum=None):
        kw = {}
        rd = [in_]
        wr = [out]
        if bias is not None:
            if isinstance(bias, TV):
                kw['bias'] = bias.ap
                rd.append(bias)
            else:
                kw['bias'] = bias
        if scale is not None:
            if isinstance(scale, TV):
                kw['scale'] = scale.ap
                rd.append(scale)
            else:
                kw['scale'] = scale
        if accum is not None:
            kw['accum_out'] = accum.ap
            wr.append(accum)
        return self.op('act', lambda g: g.activation(out=out.ap, in_=in_.ap, func=func, **kw), rd, wr)

    def tt(self, e, out, in0, in1, op):
        return self.op(e, lambda g: g.tensor_tensor(out=out.ap, in0=in0.ap, in1=in1.ap, op=op), [in0, in1], [out])

    def ts(self, e, out, in0, s1, s2, op0, op1=None):
        rd = [in0]
        a1, a2 = s1, s2
        if isinstance(s1, TV):
            rd.append(s1)
            a1 = s1.ap
        if isinstance(s2, TV):
            rd.append(s2)
            a2 = s2.ap
        if op1 is None:
            return self.op(e, lambda g: g.tensor_scalar(out=out.ap, in0=in0.ap, scalar1=a1, scalar2=None, op0=op0),
                           rd, [out])
        return self.op(e, lambda g: g.tensor_scalar(out=out.ap, in0=in0.ap, scalar1=a1, scalar2=a2, op0=op0, op1=op1),
                       rd, [out])

    def stt(self, e, out, in0, s, in1, op0, op1):
        rd = [in0, in1]
        a = s
        if isinstance(s, TV):
            rd.append(s)
            a = s.ap
        return self.op(e, lambda g: g.scalar_tensor_tensor(out=out.ap, in0=in0.ap, scalar=a, in1=in1.ap,
                                                           op0=op0, op1=op1), rd, [out])

    def copy(self, e, out, in_):
        if e == 'act':
            return self.op(e, lambda g: g.copy(out=out.ap, in_=in_.ap), [in_], [out])
        return self.op(e, lambda g: g.tensor_copy(out=out.ap, in_=in_.ap), [in_], [out])

    def amul(self, out, in_, m):
        return self.op('act', lambda g: g.mul(out=out.ap, in_=in_.ap, mul=m.ap), [in_, m], [out])

    def memset(self, e, out, val):
        return self.op(e, lambda g: g.memset(out.ap, val), [], [out])

    def recip(self, out, in_):
        return self.op('dve', lambda g: g.reciprocal(out=out.ap, in_=in_.ap), [in_], [out])

    def reduce(self, e, out, in_, op):
        return self.op(e, lambda g: g.tensor_reduce(out=out.ap, in_=in_.ap, axis=AX.X, op=op), [in_], [out])


def build(T=SEQ, depth=DEPTH, debug=False, phases='MNABCDF'):
    NT = T // 128
    NB = T // 512
    nc = bass.Bass("TRN2", target_bir_lowering=False)
    es = ExitStack()

    def din(name, shape):
        return nc.dram_tensor(name, list(shape), F32, kind="ExternalInput").ap()

    x_in = din("x", [T, D])
    c_in = din("c", [128, KC])
    w_mod = din("w_mod", [depth, D, 6 * D])
    b_mod = din("b_mod", [depth, 6 * D])
    mixw = din("mix_norm_w", [depth, 128, KC])
    ffnw = din("ffn_norm_w", [depth, 128, KC])
    w_in = din("w_in", [depth, D, INW])
    w_out = din("w_out", [depth, D, D])
    convw = din("dn_conv_w", [depth, 128, 12, 4])
    a_log = din("dn_a_log", [depth, 4])
    dt_b = din("dn_dt_bias", [depth, 4])
    onw_in = din("dn_out_norm_w", [depth, 128])
    lng_in = din("gm_ln_g", [depth, 256])
    lnb_in = din("gm_ln_b", [depth, 256])
    wsT_in = din("gm_w_sT", [depth, 128, 4, 128])
    bs_in = din("gm_b_s", [depth, 128, 4])
    qnw_in = din("sw_q_norm_w", [depth, 64])
    knw_in = din("sw_k_norm_w", [depth, 64])
    wf_in = din("w_ffn_in", [depth, D, 2 * FH])
    wf_out = din("w_ffn_out", [depth, FH, D])
    cst_in = din("consts", [128, 8, 128])
    rot_in = din("rot", [T, 32])
    out_d = nc.dram_tensor("out", [T, D], F32, kind="ExternalOutput").ap()
    skind = "ExternalOutput" if debug else "Internal"
    modrow_d = nc.dram_tensor("modrow", [depth, 6 * D], F32, kind=skind).ap()
    yT_d = nc.dram_tensor("yT", [NT, 128, 768], BF16, kind="Internal").ap()
    nl_d = [nc.dram_tensor(f"nl{p}", [T, 260], F32, kind=skind).ap() for p in range(3)]
    dbg_d = nc.dram_tensor("dbg", [T, D], F32, kind=skind).ap() if debug else None

    with es:
        k = K(nc, es)
        xin_t = [k.dram(x_in[i * 128:(i + 1) * 128, :], f"xin{i}") for i in range(NT)]
        out_t = [k.dram(out_d[i * 128:(i + 1) * 128, :], f"out{i}") for i in range(NT)]
        wts = k.dram(w_mod, "wts")
        modrow_t = k.dram(modrow_d, "modrow")
        yT_t = [k.dram(yT_d[i], f"yT{i}") for i in range(NT)]
        nl_t = [k.dram(nl_d[p], f"nl{p}") for p in range(3)]
        dbg_t = k.dram(dbg_d, "dbg") if debug else None

        def RO(ap):
            return wts.v(ap)

        hT = k.sb("hT", [128, KC, T], BF16)
        hT_b = [Tile(k, hT.t, f"hTb{b}") for b in range(NB)]
        cst = k.sb("cst", [128, 8, 128], F32)
        identb = k.sb("identb", [128, 128], BF16)
        maskCb = k.sb("maskCb", [128, 128], BF16)
        maskPb = k.sb("maskPb", [128, 128], BF16)
        epsc = k.sb("epsc", [128, 1], F32)
        k.dma('sp', cst[:], RO(cst_in))
        ident = cst[:, 0, :]
        Um = cst[:, 1, :]
        SLm = cst[:, 2, :]
        ones = cst[:, 3, :]
        maskS = cst[:, 4, :]
        maskUi = cst[:, 5, :]
        maskP = cst[:, 6, :]
        k.copy('dve', identb[:], ident)
        k.copy('dve', maskCb[:], maskUi)
        k.copy('dve', maskPb[:], maskP)
        k.memset('dve', epsc[:], EPS)

        a1 = k.sb("a1", [128, KC], F32)
        b1 = k.sb("b1", [128, KC], F32)
        a2 = k.sb("a2", [128, KC], F32)
        b2 = k.sb("b2", [128, KC], F32)
        g1bc = k.sb("g1bc", [128, D], F32)
        g2bc = k.sb("g2bc", [128, D], F32)
        nw1 = k.sb("nw1", [128, KC], F32)
        nw2 = k.sb("nw2", [128, KC], F32)
        mcol = k.sb("mcol", [128, 4, KC], F32)

        with ExitStack() as pes:
            k.begin_phase()
            c_sb = k.sb("c_sb", [128, KC], F32, pes)
            cact = k.sb("cact", [128, KC], F32, pes)
            wm = [k.sb(f"wm{i}", [128, KC, 512], F32, pes) for i in range(2)]
            bm = k.sb("bm", [1, 6 * D], F32, pes)
            mr = k.sb("mr", [1, 6 * D], F32, pes)
            pm = [k.ps(f"pm{i}", [128, 512], F32, pes) for i in range(2)]
            k.dma('sp', c_sb[:], RO(c_in))
            k.act(cact[:], c_sb[:], AF.Silu)
            it = 0
            for l in range(depth):
                k.dma('sp', bm[:], RO(b_mod[l:l + 1, :]))
                for ec in range(12):
                    w = wm[it % 2]
                    p = pm[it % 2]
                    q = 'sp' if it % 2 == 0 else 'pool'
                    k.dma(q, w[:], RO(w_mod[l, :, ec * 512:(ec + 1) * 512].rearrange("(kc p) e -> p kc e", p=128)))
                    for kc in range(KC):
                        k.mm(p[0:1, :], cact[:, kc:kc + 1], w[:, kc, :], start=(kc == 0), stop=(kc == KC - 1))
                    k.tt('dve', mr[0:1, ec * 512:(ec + 1) * 512], p[0:1, :], bm[0:1, ec * 512:(ec + 1) * 512], ALU.add)
                    it += 1
                k.dma('sp', modrow_t.v(modrow_d[l:l + 1, :]), mr[:])
            k.end_phase()

        def load_layer_params(l):
            for j, src in enumerate((0, 1, 3, 4)):
                k.dma('sp', mcol[:, j, :],
                      modrow_t.v(modrow_d[l, src * D:(src + 1) * D].rearrange("(kc p) -> p kc", p=128)),
                      allow_slow_non_contiguous=True)
            k.dma('sp', g1bc[:], modrow_t.v(modrow_d[l, 2 * D:3 * D].partition_broadcast(128)))
            k.dma('sp', g2bc[:], modrow_t.v(modrow_d[l, 5 * D:6 * D].partition_broadcast(128)))
            k.dma('sp', nw1[:], RO(mixw[l]))
            k.dma('sp', nw2[:], RO(ffnw[l]))
            k.stt('dve', a1[:], mcol[:, 1, :], 1.0, nw1[:], ALU.add, ALU.mult)
            k.copy('dve', b1[:], mcol[:, 0, :])
            k.stt('dve', a2[:], mcol[:, 3, :], 1.0, nw2[:], ALU.add, ALU.mult)
            k.copy('dve', b2[:], mcol[:, 2, :])

        def norm_gen(xt, tt, aa, bb, sq, ss, rs, xn, ptr):
            hb = hT_b[tt // 4]
            k.memset('dve', ss[:, 0:1], 0.0)
            k.act(sq[:], xt[:], AF.Square, accum=ss[:, 0:1])
            yield
            k.act(rs[:, 0:1], ss[:, 0:1], AF.Sqrt, bias=epsc[:, 0:1], scale=1.0 / D)
            k.recip(rs[:, 1:2], rs[:, 0:1])
            yield
            k.ts('dve', xn[:], xt[:], rs[:, 1:2], None, ALU.mult)
            yield
            for half in range(2):
                p = ptr[half]
                for j in range(4):
                    kc = half * 4 + j
                    k.tr(p[:, j, :], xn[:, kc * 128:(kc + 1) * 128], ident)
                yield
                for j in range(4):
                    kc = half * 4 + j
                    k.ts('dve', hb.v(hT.t[:, kc, tt * 128:(tt + 1) * 128]), p[:, j, :], aa[:, kc:kc + 1], bb[:, kc:kc + 1],
                         ALU.mult, ALU.add)
                    if j % 2 == 1:
                        yield

        def norm_to_hT(*a):
            for _ in norm_gen(*a):
                pass

        def rr(gens):
            gens = list(gens)
            while gens:
                for g in list(gens):
                    try:
                        next(g)
                    except StopIteration:
                        gens.remove(g)

        for l in range(depth):
            load_layer_params(l)
            xsrc = xin_t if l == 0 else out_t

            with ExitStack() as pes:
                k.begin_phase()
                xt = [k.sb(f"n1x{i}", [128, D], F32, pes) for i in range(2)]
                sq = k.sb("n1sq", [128, D], F32, pes)
                xn = [k.sb(f"n1xn{i}", [128, D], F32, pes) for i in range(2)]
                ss = [k.sb(f"n1ss{i}", [128, 1], F32, pes) for i in range(2)]
                rs = [k.sb(f"n1rs{i}", [128, 2], F32, pes) for i in range(2)]
                ptr = [[k.ps(f"n1p{i}{h}", [128, 4, 128], F32, pes) for h in range(2)] for i in range(2)]
                for tt in range(NT):
                    i = tt % 2
                    k.dma('sp', xt[i][:], xsrc[tt][:])
                    norm_to_hT(xt[i], tt, a1, b1, sq, ss[i], rs[i], xn[i], ptr[i])
                k.end_phase()

            if 'A' in phases:
                phase_A(k, l, T, hT, hT_b, RO, w_in, convw, a_log, dt_b, onw_in, cst, identb, epsc, yT_t, yT_d)
            if 'B' in phases:
                phase_B(k, l, T, hT, hT_b, RO, w_in, lng_in, lnb_in, wsT_in, bs_in, cst, identb, epsc, yT_t, yT_d)
            if 'C' in phases:
                phase_C(k, l, T, hT, hT_b, RO, w_in, qnw_in, knw_in, rot_in, identb, maskCb, maskPb, epsc, nl_t, nl_d)

            with ExitStack() as pes:
                if 'D' not in phases:
                    break
                k.begin_phase()
                wo = k.sb("wo", [128, KC, D], BF16, pes)
                k.load_w(wo, w_out[l], KC, D, RO)
                nl = [[k.sb(f"dnl{i}{p}", [128, 4, 65], F32, pes) for p in range(3)] for i in range(2)]
                nsum = k.sb("dnsum", [128, 4, 65], F32, pes)
                rl = k.sb("drl", [128, 4], F32, pes)
                ycb = k.sb("dycb", [128, 4, 64], BF16, pes)
                yT = [k.sb(f"dyT{i}", [128, 8, 128], BF16, pes) for i in range(2)]
                xt = [k.sb(f"dx{i}", [128, D], F32, pes) for i in range(2)]
                xo = [k.sb(f"dxo{i}", [128, D], F32, pes) for i in range(2)]
                tmp = k.sb("dtmp", [128, D], F32, pes)
                sq = k.sb("dsq", [128, D], F32, pes)
                xn = k.sb("dxn", [128, D], F32, pes)
                ss = k.sb("dss", [128, 1], F32, pes)
                rs = k.sb("drs", [128, 2], F32, pes)
                ptc = k.ps("dptc", [128, 8, 128], BF16, pes)
                po = [k.ps(f"dpo{h}", [128, 512], F32, pes) for h in range(2)]
                ptr = [k.ps(f"dptr{h}", [128, 4, 128], F32, pes) for h in range(2)]
                def d1(tt):
                    i = tt % 2
                    for p in range(3):
                        k.dma('sp', nl[i][p][:], nl_t[p].v(nl_d[p][tt * 128:(tt + 1) * 128, :].rearrange("p (h e) -> p h e", h=4)))
                    k.dma('sp', yT[i][:, 0:6, :], yT_t[tt].v(yT_d[tt].rearrange("p (c t) -> p c t", c=6)))
                    k.dma('sp', xt[i][:], xsrc[tt][:])
                    k.tt('dve', nsum[:], nl[i][0][:], nl[i][1][:], ALU.add)
                    yield
                    k.tt('dve', nsum[:], nsum[:], nl[i][2][:], ALU.add)
                    yield
                    k.recip(rl[:], nsum[:, :, 64])
                    k.tt('dve', ycb[:], nsum[:, :, 0:64], rl.v(rl.t[:, :].unsqueeze(2).to_broadcast([128, 4, 64])), ALU.mult)
                    yield
                    for j in range(2):
                        k.tr(ptc[:, j, :], ycb.v(ycb.t[:, 2 * j:2 * j + 2, :].rearrange("p h e -> p (h e)")), identb[:])
                    k.copy('act', yT[i][:, 6:8, :], ptc[:, 0:2, :])
                    yield
                    for half in range(2):
                        for c in range(8):
                            k.mm(po[half][:], yT[i][:, c, :], wo[:, c, half * 512:(half + 1) * 512],
                                 start=(c == 0), stop=(c == 7))
                        yield
                    for half in range(2):
                        sl = slice(half * 512, (half + 1) * 512)
                        k.tt('dve', tmp[:, sl], po[half][:], g1bc[:, sl], ALU.mult)
                        k.tt('dve' if half == 0 else 'pool', xo[i][:, sl], tmp[:, sl], xt[i][:, sl], ALU.add)
                        yield
                    k.dma('sp', out_t[tt][:], xo[i][:])

                rr([d1(0)])
                for tt in range(NT):
                    g = [norm_gen(xo[tt % 2], tt, a2, b2, sq, ss, rs, xn, ptr)]
                    if tt + 1 < NT:
                        g.append(d1(tt + 1))
                    rr(g)
                k.end_phase()

            groups = [(0, 6), (6, 6), (12, 5), (17, 5)] if 'F' in phases else []
            if groups:
                with ExitStack() as pes:
                    k.begin_phase()
                    wg = [k.sb(f"fwg{i}", [128, KC, 768], BF16, pes) for i in range(2)]
                    wu = [k.sb(f"fwu{i}", [128, KC, 768], BF16, pes) for i in range(2)]
                    wo2 = [k.sb(f"fwo{i}", [128, 6, D], BF16, pes) for i in range(2)]
                    actT = [k.sb(f"fact{i}", [128, 6, 512], BF16, pes) for i in range(2)]
                    sg = [k.sb(f"fsg{i}", [128, 512], F32, pes) for i in range(2)]
                    xt = [k.sb(f"fx{i}", [128, D], F32, pes) for i in range(2)]
                    xo = [k.sb(f"fxo{i}", [128, D], F32, pes) for i in range(2)]
                    tmp = k.sb("ftmp", [128, D], F32, pes)
                    pg = [k.ps(f"fpg{i}", [128, 512], F32, pes) for i in range(2)]
                    pu = [k.ps(f"fpu{i}", [128, 512], F32, pes) for i in range(2)]
                    po = [[k.ps(f"fpo{i}{h}", [128, 512], F32, pes) for h in range(2)] for i in range(2)]

                    def loadw(gi):
                        f0, nf = groups[gi]
                        j = gi % 2
                        for kc in range(KC):
                            k.dma('pool', wg[j][:, kc, 0:nf * 128], RO(wf_in[l, kc * 128:(kc + 1) * 128, f0 * 128:(f0 + nf) * 128]))
                            k.dma('pool', wu[j][:, kc, 0:nf * 128], RO(wf_in[l, kc * 128:(kc + 1) * 128, FH + f0 * 128:FH + (f0 + nf) * 128]))
                        for c in range(nf):
                            k.dma('pool', wo2[j][:, c, :], RO(wf_out[l, (f0 + c) * 128:(f0 + c + 1) * 128, :]))

                    loadw(0)
                    it = 0
                    xi = 0
                    ab = 0
                    for gi, (f0, nf) in enumerate(groups):
                        if gi + 1 < len(groups):
                            loadw(gi + 1)
                        wj = gi % 2
                        for blk in range(NB):
                            hb = hT_b[blk]
                            a = actT[ab % 2]
                            ab += 1
                            for fi in range(nf):
                                j = it % 2
                                it += 1
                                for kc in range(KC):
                                    k.mm(pg[j][:], wg[wj][:, kc, fi * 128:(fi + 1) * 128], hb.v(hT.t[:, kc, blk * 512:(blk + 1) * 512]),
                                         start=(kc == 0), stop=(kc == KC - 1))
                                for kc in range(KC):
                                    k.mm(pu[j][:], wu[wj][:, kc, fi * 128:(fi + 1) * 128], hb.v(hT.t[:, kc, blk * 512:(blk + 1) * 512]),
                                         start=(kc == 0), stop=(kc == KC - 1))
                                k.act(sg[j][:], pg[j][:], AF.Silu)
                                k.tt('dve', a[:, fi, :], sg[j][:], pu[j][:], ALU.mult)
                            for t4 in range(4):
                                tt = blk * 4 + t4
                                i = xi % 2
                                xi += 1
                                k.dma('sp', xt[i][:], out_t[tt][:])
                                for half in range(2):
                                    for fi in range(nf):
                                        k.mm(po[i][half][:], a[:, fi, t4 * 128:(t4 + 1) * 128], wo2[wj][:, fi, half * 512:(half + 1) * 512],
                                             start=(fi == 0), stop=(fi == nf - 1))
                                for half in range(2):
                                    sl = slice(half * 512, (half + 1) * 512)
                                    k.tt('dve', tmp[:, sl], po[i][half][:], g2bc[:, sl], ALU.mult)
                                    k.tt('dve' if half == 0 else 'pool', xo[i][:, sl], tmp[:, sl], xt[i][:, sl], ALU.add)
                                k.dma('sp', out_t[tt][:], xo[i][:])
                    k.end_phase()
        k.barrier()
        build.ninst = dict(k.ninst)
        build.retired = (getattr(k, 'retired', 0), getattr(k, 'nds', 0), len(k.own['pe']) + len(k.own['act']) + len(k.own['dve']) + len(k.own['pool']) + len(k.own['sp']))
        build.maxd = (max([c for _, c in k.dpool['sp']] + [0]) * 16, max([c for _, c in k.dpool['pool']] + [0]) * 16, k.nsem)
    return nc


def phase_B(k, l, T, hT, hT_b, RO, w_in, lng_in, lnb_in, wsT_in, bs_in, cst, identb, epsc, yT_t, yT_d):
    NT = T // 128
    c0 = 2056
    maskUi = cst[:, 5, :]
    with ExitStack() as pes:
        k.begin_phase()
        wb = k.sb("bw", [128, KC, 512], BF16, pes)
        k.load_w(wb, w_in[l, :, c0:c0 + 512], KC, 512, RO)
        lng = k.sb("blng", [128, 256], F32, pes)
        lnb = k.sb("blnb", [128, 256], F32, pes)
        wsf = k.sb("bwsf", [128, 4, 128], F32, pes)
        wsb = k.sb("bwsb", [128, 4, 128], BF16, pes)
        bs = k.sb("bbs", [128, 4], F32, pes)
        k.dma('sp', lng[:], RO(lng_in[l].partition_broadcast(128)))
        k.dma('sp', lnb[:], RO(lnb_in[l].partition_broadcast(128)))
        k.dma('sp', wsf[:], RO(wsT_in[l]))
        k.dma('sp', bs[:], RO(bs_in[l]))
        k.tt('dve', wsb[:], wsf[:], maskUi.tile.v(cst.t[:, 5:6, :].to_broadcast([128, 4, 128])), ALU.mult)
        gl = [k.sb(f"bgl{i}", [128, 512], F32, pes) for i in range(2)]
        st = k.sb("bst", [128, 6], F32, pes)
        mv = k.sb("bmv", [128, 2], F32, pes)
        rs = k.sb("brs", [128, 2], F32, pes)
        vn = k.sb("bvn", [128, 256], F32, pes)
        vn2 = k.sb("bvn2", [128, 256], F32, pes)
        vnb = [k.sb(f"bvnb{i}", [128, 256], BF16, pes) for i in range(2)]
        yb = [k.sb(f"byb{i}", [128, 256], BF16, pes) for i in range(2)]
        ybT = [k.sb(f"bybT{i}", [128, 2, 128], BF16, pes) for i in range(2)]
        pp = [k.ps(f"bpp{i}", [128, 512], F32, pes) for i in range(2)]
        psv = [k.ps(f"bpsv{i}", [128, 512], F32, pes) for i in range(2)]
        ptr = [k.ps(f"bptr{i}", [128, 8, 128], BF16, pes) for i in range(2)]
        def b1(tt):
            i = tt % 2
            hb = hT_b[tt // 4]
            for kc in range(KC):
                k.mm(pp[i][:], hb.v(hT.t[:, kc, tt * 128:(tt + 1) * 128]), wb[:, kc, :], start=(kc == 0), stop=(kc == KC - 1))
            yield
            k.act(gl[i][:], pp[i][:], AF.Gelu)
            yield
            k.op('dve', lambda g: g.bn_stats(out=st.t[:, 0:6], in_=gl[i].t[:, 256:512]), [gl[i][:]], [st[:]])
            k.op('dve', lambda g: g.bn_aggr(out=mv.t[:, 0:2], in_=st.t[:, 0:6]), [st[:]], [mv[:]])
            yield
            k.act(rs[:, 0:1], mv[:, 1:2], AF.Sqrt, bias=epsc[:, 0:1], scale=1.0)
            k.recip(rs[:, 1:2], rs[:, 0:1])
            yield
            k.ts('dve', vn[:], gl[i][:, 256:512], mv[:, 0:1], rs[:, 1:2], ALU.subtract, ALU.mult)
            yield
            k.tt('dve', vn2[:], vn[:], lng[:], ALU.mult)
            yield
            k.tt('dve', vnb[i][:], vn2[:], lnb[:], ALU.add)
            yield

        def b2(tt):
            i = tt % 2
            for g in range(4):
                k.mm(psv[i][:, g * 64:(g + 1) * 64], wsb[:, g, :], vnb[i][:, g * 64:(g + 1) * 64])
            yield
            for g in range(4):
                k.stt('dve', yb[i][:, g * 64:(g + 1) * 64], psv[i][:, g * 64:(g + 1) * 64], bs[:, g:g + 1],
                      gl[i][:, g * 64:(g + 1) * 64], ALU.add, ALU.mult)
                if g % 2 == 1:
                    yield
            for j in range(2):
                k.tr(ptr[i][:, j, :], yb[i][:, j * 128:(j + 1) * 128], identb[:])
            yield
            k.copy('act', ybT[i][:], ptr[i][:, 0:2, :])
            yield
            k.dma('sp', yT_t[tt].v(yT_d[tt][:, 512:768].rearrange("p (c t) -> p c t", c=2)), ybT[i][:])

        def rr(gens):
            gens = list(gens)
            while gens:
                for g in list(gens):
                    try:
                        next(g)
                    except StopIteration:
                        gens.remove(g)

        rr([b1(0)])
        for tt in range(NT):
            g = [b2(tt)]
            if tt + 1 < NT:
                g.append(b1(tt + 1))
            rr(g)
        k.end_phase()


def phase_C(k, l, T, hT, hT_b, RO, w_in, qnw_in, knw_in, rot_in, identb, maskCb, maskPb, epsc, nl_t, nl_d):
    c0 = 2568
    with ExitStack() as pes:
        k.begin_phase()
        wc = k.sb("cw", [128, KC, 2304], BF16, pes)
        k.load_w(wc, w_in[l, :, c0:c0 + 2304], KC, 2304, RO)
        nwq = k.sb("cnwq", [128, 64], F32, pes)
        nwk = k.sb("cnwk", [128, 64], F32, pes)
        nw = k.sb("cnw", [128, 8, 64], F32, pes)
        k.dma('sp', nwq[:], RO(qnw_in[l].partition_broadcast(128)))
        k.dma('sp', nwk[:], RO(knw_in[l].partition_broadcast(128)))
        for h in range(4):
            k.copy('dve', nw[:, h, :], nwq[:])
            k.copy('dve', nw[:, 4 + h, :], nwk[:])
        rot = [k.sb(f"crot{i}", [128, 32], F32, pes) for i in range(2)]
        sqt = k.sb("csq", [128, 512], F32, pes)
        ss = k.sb("css", [128, 8], F32, pes)
        rs = k.sb("crs", [128, 8], F32, pes)
        ri = k.sb("cri", [128, 8], F32, pes)
        qkf = k.sb("cqkf", [128, 8, 64], F32, pes)
        qkg = k.sb("cqkg", [128, 8, 64], F32, pes)
        ra = k.sb("cra", [128, 8, 16], F32, pes)
        rb = k.sb("crb", [128, 8, 16], F32, pes)
        qkb = k.sb("cqkb", [128, 8, 64], BF16, pes)
        v1 = [k.sb(f"cv1{i}", [128, 4, 80], BF16, pes) for i in range(3)]
        qT = [k.sb(f"cqT{i}", [64, 4, 128], BF16, pes) for i in range(2)]
        kT = [k.sb(f"ckT{i}", [64, 4, 128], BF16, pes) for i in range(3)]
        P0 = k.sb("cP0", [128, 4, 128], BF16, pes)
        P1 = k.sb("cP1", [128, 4, 128], BF16, pes)
        P0m = k.sb("cP0m", [128, 4, 128], BF16, pes)
        P1m = k.sb("cP1m", [128, 4, 128], BF16, pes)
        nlo = [k.sb(f"cnlo{i}", [128, 4, 65], F32, pes) for i in range(2)]
        pc0 = k.ps("cpc0", [128, 512], F32, pes)
        pc1 = k.ps("cpc1", [128, 512], F32, pes)
        ptr = k.ps("cptr", [128, 8, 128], BF16, pes)
        pS0 = k.ps("cpS0", [128, 4, 128], F32, pes)
        pS1 = k.ps("cpS1", [128, 4, 128], F32, pes)
        pO = k.ps("cpO", [128, 4, 128], F32, pes)
        for i in range(3):
            k.memset('pool', v1[i][:], 1.0)
        mC = maskCb.v(maskCb.t[:, :].unsqueeze(1).to_broadcast([128, 4, 128]))
        mP = maskPb.v(maskPb.t[:, :].unsqueeze(1).to_broadcast([128, 4, 128]))
        tiles = []
        for p, (window, d) in enumerate(PATTERNS):
            nb = (T // d) // 128
            for r in range(d):
                for n in range(nb):
                    start = 128 * n * d + r
                    tiles.append((p, d, n, start, start + 127 * d + 1))

        def s1(i):
            p, d, n, start, stop_ = tiles[i]
            cur = i % 3
            b0 = start // 512
            b1 = (stop_ - 1) // 512
            hbs = [hT_b[b] for b in range(b0, b1 + 1)]
            rt = rot[i % 2]
            k.dma('sp', rt[:], RO(rot_in[start:stop_:d, :]))
            for kc in range(KC):
                lhs = TV(hbs[0], hT.t[:, kc, start:stop_:d], tuple(hbs[1:]))
                k.mm(pc0[:], lhs, wc[:, kc, p * 768:p * 768 + 512], start=(kc == 0), stop=(kc == KC - 1))
            yield
            for kc in range(KC):
                lhs = TV(hbs[0], hT.t[:, kc, start:stop_:d], tuple(hbs[1:]))
                k.mm(pc1[:, 0:256], lhs, wc[:, kc, p * 768 + 512:p * 768 + 768], start=(kc == 0), stop=(kc == KC - 1))
            k.act(sqt[:], pc0[:], AF.Square)
            yield
            k.reduce('dve', ss[:], sqt.v(sqt.t[:, :].rearrange("p (g e) -> p g e", g=8)), ALU.add)
            k.act(rs[:], ss[:], AF.Ln, bias=epsc[:, 0:1], scale=1.0 / 64)
            k.act(ri[:], rs[:], AF.Exp, scale=-0.5)
            k.copy('act', v1[cur][:, :, 0:64], pc1.v(pc1.t[:, 0:256].rearrange("p (h e) -> p h e", h=4)))
            yield
            k.tt('dve', qkf[:], pc0.v(pc0.t[:, :].rearrange("p (g e) -> p g e", g=8)),
                 ri.v(ri.t[:, :].unsqueeze(2).to_broadcast([128, 8, 64])), ALU.mult)
            yield
            k.tt('dve', qkg[:], qkf[:], nw[:], ALU.mult)
            yield
            C2 = rt.v(rt.t[:, 0:16].unsqueeze(1).to_broadcast([128, 8, 16]))
            Sa = rt.v(rt.t[:, 16:24].unsqueeze(1).to_broadcast([128, 8, 8]))
            Sb = rt.v(rt.t[:, 24:32].unsqueeze(1).to_broadcast([128, 8, 8]))
            k.tt('dve', ra[:], qkg[:, :, 0:16], C2, ALU.mult)
            k.tt('dve', rb[:, :, 0:8], qkg[:, :, 8:16], Sa, ALU.mult)
            k.tt('dve', rb[:, :, 8:16], qkg[:, :, 0:8], Sb, ALU.mult)
            k.copy('pool', qkb[:, :, 16:64], qkg[:, :, 16:64])
            yield
            k.tt('dve', qkb[:, :, 0:16], ra[:], rb[:], ALU.add)
            yield
            for j in range(8):
                k.tr(ptr[0:64, j, :], qkb[:, j, :], identb[:])
            yield
            k.copy('act', qT[i % 2][:], ptr[0:64, 0:4, :])
            k.copy('act', kT[cur][:], ptr[0:64, 4:8, :])
            yield

        def s2(i):
            p, d, n, start, stop_ = tiles[i]
            cur = i % 3
            prev = (i - 1) % 3
            q_ = qT[i % 2]
            for h in range(4):
                k.mm(pS0[:, h, :], kT[cur][:, h, :], q_[:, h, :])
            yield
            k.act(P0[:], pS0[:], AF.Exp, scale=0.125)
            if n > 0:
                for h in range(4):
                    k.mm(pS1[:, h, :], kT[prev][:, h, :], q_[:, h, :])
            yield
            k.tt('dve', P0m[:], P0[:], mC, ALU.mult)
            if n > 0:
                k.act(P1[:], pS1[:], AF.Exp, scale=0.125)
                yield
                k.tt('dve', P1m[:], P1[:], mP, ALU.mult)
            yield
            for h in range(4):
                if n > 0:
                    k.mm(pO[:, h, 0:65], P1m[:, h, :], v1[prev][:, h, 0:65], start=True, stop=False)
                k.mm(pO[:, h, 0:65], P0m[:, h, :], v1[cur][:, h, 0:65], start=(n == 0), stop=True)
            yield
            o_ = nlo[i % 2]
            k.copy('act', o_[:], pO[:, :, 0:65])
            yield
            k.dma('sp', nl_t[p].v(nl_d[p][start:stop_:d, :].rearrange("p (h e) -> p h e", h=4)), o_[:])
            yield

        def rr(gens):
            gens = list(gens)
            while gens:
                for g in list(gens):
                    try:
                        next(g)
                    except StopIteration:
                        gens.remove(g)

        NTL = len(tiles)
        rr([s1(0)])
        for i in range(NTL):
            rr([s2(i)] + ([s1(i + 1)] if i + 1 < NTL else []))
        k.end_phase()


def phase_A(k, l, T, hT, hT_b, RO, w_in, convw_in, a_log, dt_b, onw_in, cst, identb, epsc, yT_t, yT_d):
    NT = T // 128
    NB = T // 512
    ident = cst[:, 0, :]
    Um = cst[:, 1, :]
    SLm = cst[:, 2, :]
    ones = cst[:, 3, :]
    maskS = cst[:, 4, :]
    maskUi = cst[:, 5, :]
    SCALE = 128.0 ** -0.5
    with ExitStack() as pes:
        k.begin_phase()
        wa = k.sb("aw", [128, KC, 2056], BF16, pes)
        k.load_w(wa, w_in[l, :, 0:2056], KC, 2056, RO)
        cw = k.sb("acw", [128, 12, 4], F32, pes)
        k.dma('sp', cw[:], RO(convw_in[l]))
        alog = k.sb("aalog", [128, 4], F32, pes)
        dtb = k.sb("adtb", [128, 4], F32, pes)
        nea = k.sb("anea", [128, 4], F32, pes)
        onw = k.sb("aonw", [128, 128], F32, pes)
        k.dma('sp', alog[:], RO(a_log[l].partition_broadcast(128)))
        k.dma('sp', dtb[:], RO(dt_b[l].partition_broadcast(128)))
        k.dma('sp', onw[:], RO(onw_in[l].partition_broadcast(128)))
        k.act(nea[:], alog[:], AF.Exp)
        k.ts('dve', nea[:], nea[:], -1.0, None, ALU.mult)
        halo = k.sb("ahalo", [128, 12, 3], F32, pes)
        k.memset('dve', halo[:], 0.0)
        raw3 = [k.sb(f"araw{i}", [128, 515], F32, pes) for i in range(3)]
        acc3 = [k.sb(f"aacc{i}", [128, 512], F32, pes) for i in range(3)]
        qkv = k.sb("aqkv", [128, 12, 512], F32, pes)
        S = [[k.sb(f"aS{h}{i}", [128, 128], F32, pes) for i in range(2)] for h in range(4)]
        for h in range(4):
            k.memset('dve', S[h][0][:], 0.0)
        szb = [k.sb(f"asz{i}", [128, 512], F32, pes) for i in range(3)]
        smb = [k.sb(f"asm{i}", [128, 64], F32, pes) for i in range(2)]
        oallb = [k.sb(f"aoall{i}", [128, 4, 128], F32, pes) for i in range(2)]
        osq = k.sb("aosq", [128, 128], F32, pes)
        oss = k.sb("aoss", [128, 4], F32, pes)
        ors = k.sb("aors", [128, 8], F32, pes)
        ot = k.sb("aot", [128, 128], F32, pes)
        ya = k.sb("aya", [128, 4, 128], F32, pes)
        yaT = [k.sb(f"ayaT{i}", [128, 4, 128], BF16, pes) for i in range(2)]
        names = ["kbg", "kdec", "vb", "GU", "eD", "eDT", "tA", "N", "NT", "tQ", "aqk", "TTa", "TTb",
                 "Pa", "PTa", "Pb", "PTb", "u", "wT", "vnew", "o2"]
        W = [{n: k.sb(f"a{n}{s}", [128, 128], F32, pes) for n in names} for s in range(4)]
        BK = [k.ps(f"abk{i}", [128, 512], F32, pes) for i in range(8)]
        pq = [BK[0], BK[1]]
        pz = BK[6]
        pm = BK[7]
        pty = BK[7]

        def sc(j):
            return sm[:, j:j + 1]

        pwi = [0]

        def nxt():
            t = pw[pwi[0] % 3]
            s = (pwi[0] // 3) % 4
            pwi[0] += 1
            return t[:, s, :]

        it = 0
        for blk in range(NB):
            hb = hT_b[blk]
            tok = slice(blk * 512, (blk + 1) * 512)
            def conv_gen(fc, n_):
                j = n_ % 2
                rw = raw3[n_ % 3]
                ac = acc3[n_ % 3]
                for kc in range(KC):
                    k.mm(pq[j][:], wa[:, kc, fc * 128:(fc + 1) * 128], hb.v(hT.t[:, kc, tok]), start=(kc == 0), stop=(kc == KC - 1))
                k.copy('pool', rw[:, 0:3], halo[:, fc, :])
                k.copy('act', rw[:, 3:515], pq[j][:])
                k.copy('pool', halo[:, fc, :], rw[:, 512:515])
                yield
                k.ts('dve', ac[:], rw[:, 0:512], cw[:, fc, 0:1], None, ALU.mult)
                k.stt('dve', ac[:], rw[:, 1:513], cw[:, fc, 1:2], ac[:], ALU.mult, ALU.add)
                k.stt('dve', ac[:], rw[:, 2:514], cw[:, fc, 2:3], ac[:], ALU.mult, ALU.add)
                k.stt('dve', ac[:], rw[:, 3:515], cw[:, fc, 3:4], ac[:], ALU.mult, ALU.add)
                yield
                k.act(qkv[:, fc, :], ac[:], AF.Silu)
                yield

            def l2_gen(fc, n_):
                j = n_ % 2
                ac = acc3[n_ % 3]
                rw = raw3[n_ % 3]
                k.act(ac[:], qkv[:, fc, :], AF.Square)
                yield
                k.mm(pq[j][:], ones, ac[:])
                yield
                k.act(rw[:, 0:512], pq[j][:], AF.Ln, bias=epsc[:, 0:1], scale=1.0)
                k.act(ac[:], rw[:, 0:512], AF.Exp, scale=-0.5)
                yield
                k.tt('dve', qkv[:, fc, :], qkv[:, fc, :], ac[:], ALU.mult)
                yield

            def stagger(mk, n):
                act_ = []
                i_ = 0
                while act_ or i_ < n:
                    for g in list(act_):
                        try:
                            next(g)
                        except StopIteration:
                            act_.remove(g)
                    if i_ < n:
                        g = mk(i_)
                        next(g)
                        act_.append(g)
                        i_ += 1

            stagger(lambda i_: conv_gen(i_, i_), 12)
            stagger(lambda i_: l2_gen(i_, i_), 8)
            def pre_gen(tt):
                sm = smb[tt % 2]
                sz = szb[tt % 3]
                hts = slice(tt * 128, (tt + 1) * 128)
                for kc in range(KC):
                    k.mm(pz[:], hb.v(hT.t[:, kc, hts]), wa[:, kc, 1536:2048], start=(kc == 0), stop=(kc == KC - 1))
                yield
                for kc in range(KC):
                    k.mm(pm[:, 0:8], hb.v(hT.t[:, kc, hts]), wa[:, kc, 2048:2056], start=(kc == 0), stop=(kc == KC - 1))
                k.act(sz[:], pz[:], AF.Silu)
                yield
                k.tt('dve', sm[:, 0:4], pm[:, 0:4], dtb[:], ALU.add)
                k.act(sm[:, 16:20], pm[:, 4:8], AF.Exp, scale=-1.0)
                yield
                k.act(sm[:, 4:8], sm[:, 0:4], AF.Exp)
                k.ts('dve', sm[:, 16:20], sm[:, 16:20], 1.0, None, ALU.add)
                yield
                k.act(sm[:, 8:12], sm[:, 4:8], AF.Ln, bias=1.0)
                k.recip(sm[:, 20:24], sm[:, 16:20])
                yield
                k.tt('dve', sm[:, 12:16], sm[:, 8:12], nea[:], ALU.mult)
                k.ts('dve', sm[:, 24:28], sm[:, 20:24], -1.0, None, ALU.mult)
                yield
                k.mm(pm[:, 8:12], Um, sm[:, 12:16])
                k.mm(pm[:, 12:16], ones, sm[:, 12:16])
                yield
                k.copy('dve', sm[:, 28:32], pm[:, 8:12])
                k.copy('dve', sm[:, 36:40], pm[:, 12:16])
                yield
                k.act(sm[:, 32:36], sm[:, 28:32], AF.Exp)
                k.act(sm[:, 40:44], sm[:, 36:40], AF.Exp)
                k.tt('dve', sm[:, 44:48], sm[:, 36:40], sm[:, 28:32], ALU.subtract)
                yield
                k.act(sm[:, 48:52], sm[:, 44:48], AF.Exp)
                k.tt('dve', sm[:, 52:56], sm[:, 20:24], sm[:, 32:36], ALU.mult)
                k.ts('dve', sm[:, 56:60], sm[:, 32:36], SCALE, None, ALU.mult)
                yield

            def epi_gen(tt):
                oall = oallb[tt % 2]
                sz = szb[tt % 3]
                k.memset('dve', oss[:], 0.0)
                for h in range(4):
                    k.act(osq[:], oall[:, h, :], AF.Square, accum=oss[:, h:h + 1])
                    yield
                k.act(ors[:, 0:4], oss[:], AF.Ln, bias=epsc[:, 0:1], scale=1.0 / 128)
                k.act(ors[:, 4:8], ors[:, 0:4], AF.Exp, scale=-0.5)
                yield
                for h in range(4):
                    k.stt('dve', ot[:], oall[:, h, :], ors[:, 4 + h:5 + h], onw[:], ALU.mult, ALU.mult)
                    k.tt('dve', ya[:, h, :], ot[:], sz[:, h * 128:(h + 1) * 128], ALU.mult)
                    yield
                for h in range(4):
                    k.tr(pty[:, h * 128:(h + 1) * 128], ya[:, h, :], ident)
                yield
                i = tt % 2
                k.copy('act', yaT[i][:], pty.v(pty.t[:, :].rearrange("p (h t) -> p h t", h=4)))
                yield
                k.dma('sp', yT_t[tt].v(yT_d[tt][:, 0:512].rearrange("p (c t) -> p c t", c=4)), yaT[i][:])

            def head_gen(h, tt, cs, sm, oall):
                w = W[h]

                def sc(j):
                    return sm[:, j:j + 1]
                qT_ = qkv[:, h, cs]
                kT_ = qkv[:, 4 + h, cs]
                vT_ = qkv[:, 8 + h, cs]
                Sold = S[h][tt % 2]
                Snew = S[h][(tt + 1) % 2]
                cnt = [0]

                def nx():
                    c = cnt[0]
                    cnt[0] += 1
                    bk = BK[2 * h + (c % 2)]
                    o = ((c // 2) % 4) * 128
                    return bk[:, o:o + 128]
                p = nx()

                def post(p=p):
                    k.ts('dve', w["kbg"][:], p, sc(52 + h), None, ALU.mult)
                    k.amul(w["kdec"][:], p, sc(48 + h))
                yield [('tr', p, kT_, ident)], post
                p = nx()

                def post(p=p):
                    k.amul(w["vb"][:], p, sc(20 + h))
                    k.ts('dve', w["GU"][:], Um, sc(12 + h), None, ALU.mult)
                yield [('tr', p, vT_, ident)], post
                p = nx()

                def post(p=p):
                    k.act(w["eD"][:], p, AF.Exp)
                    k.tt('pool', w["tA"][:], w["eD"][:], maskS, ALU.mult)
                yield [('mm', p, w["GU"][:], SLm)], post
                p = nx()

                def post(p=p):
                    k.act(w["eDT"][:], p, AF.Exp)
                    k.tt('pool', w["tQ"][:], w["eDT"][:], maskUi, ALU.mult)
                yield [('mm', p, SLm, w["GU"][:])], post
                p = nx()

                def post(p=p):
                    k.stt('dve', w["N"][:], p, sc(24 + h), w["tA"][:], ALU.mult, ALU.mult)
                yield [('mm', p, kT_, kT_)], post
                p = nx()

                def post(p=p):
                    k.copy('act', w["NT"][:], p)
                    k.tt('pool', w["TTa"][:], w["NT"][:], ident, ALU.add)
                yield [('tr', p, w["N"][:], ident)], post
                p = nx()

                def post(p=p):
                    k.stt('dve', w["aqk"][:], p, SCALE, w["tQ"][:], ALU.mult, ALU.mult)
                yield [('mm', p, kT_, qT_)], post
                TT, TTo = w["TTa"], w["TTb"]
                P_, PT_ = w["N"], w["NT"]
                Pn, PTn = w["Pa"], w["PTa"]
                for s_ in range(6):
                    p = nx()
                    ops = [('mm', p, PT_[:], P_[:])]
                    p2 = None
                    if s_ < 5:
                        p2 = nx()
                        ops.append(('mm', p2, P_[:], PT_[:]))

                    def post(p=p, p2=p2, Pn=Pn, PTn=PTn, s_=s_):
                        k.copy('act', Pn[:], p)
                        if p2 is not None:
                            k.copy('dve', PTn[:], p2)
                    yield ops, post
                    p3 = nx()

                    def post(p3=p3, TT=TT, TTo=TTo):
                        k.tt('dve', TTo[:], p3, TT[:], ALU.add)
                    yield [('mm', p3, Pn[:], TT[:])], post
                    TT, TTo = TTo, TT
                    P_, PT_ = Pn, PTn
                    if s_ % 2 == 0:
                        Pn, PTn = w["Pb"], w["PTb"]
                    else:
                        Pn, PTn = w["Pa"], w["PTa"]
                p = nx()
                p2 = nx()

                def post(p=p, p2=p2):
                    k.copy('act', w["u"][:], p)
                    k.copy('dve', w["wT"][:], p2)
                yield [('mm', p, TT[:], w["vb"][:]), ('mm', p2, w["kbg"][:], TT[:])], post
                p = nx()
                p1 = nx()

                def post(p=p):
                    k.stt('dve', w["vnew"][:], p, -1.0, w["u"][:], ALU.mult, ALU.add)
                yield [('mm', p, w["wT"][:], Sold[:]), ('mm', p1, qT_, Sold[:])], post
                p2 = nx()
                p3 = nx()

                def post(p1=p1, p2=p2, p3=p3):
                    k.copy('act', w["o2"][:], p2)
                    k.stt('dve', Snew[:], Sold[:], sc(40 + h), p3, ALU.mult, ALU.add)
                    k.stt('dve', oall[:, h, :], p1, sc(56 + h), w["o2"][:], ALU.mult, ALU.add)
                yield [('mm', p2, w["aqk"][:], w["vnew"][:]), ('mm', p3, w["kdec"][:], w["vnew"][:])], post


            def run_tile(tt, side):
                cs = slice((tt % 4) * 128, (tt % 4 + 1) * 128)
                groups = [[head_gen(h, tt, cs, smb[tt % 2], oallb[tt % 2]) for h in (0, 1)],
                          [head_gen(h, tt, cs, smb[tt % 2], oallb[tt % 2]) for h in (2, 3)]]
                side = list(side)
                alive = True
                while alive:
                    alive = False
                    for grp in groups:
                        items = []
                        for g in list(grp):
                            try:
                                items.append(next(g))
                            except StopIteration:
                                grp.remove(g)
                        if not items:
                            continue
                        alive = True
                        k.pe_batch([d_ for ops, _ in items for d_ in ops])
                        for _, post in items:
                            post()
                    for g in list(side):
                        try:
                            next(g)
                        except StopIteration:
                            side.remove(g)
                for g in side:
                    for _ in g:
                        pass

            t0_ = blk * 4
            for _ in pre_gen(t0_):
                pass
            for t4 in range(4):
                tt = t0_ + t4
                side = []
                if t4 + 1 < 4:
                    side.append(pre_gen(tt + 1))
                if t4 > 0:
                    side.append(epi_gen(tt - 1))
                run_tile(tt, side)
            for _ in epi_gen(t0_ + 3):
                pass
        k.end_phase()


def _consts():
    i = np.arange(128)
    c = np.zeros((128, 8, 128), np.float32)
    c[:, 0, :] = np.eye(128)
    c[:, 1, :] = (i[:, None] <= i[None, :])
    c[:, 2, :] = (i[:, None] > i[None, :])
    c[:, 3, :] = 1.0
    c[:, 4, :] = (i[:, None] > i[None, :])
    c[:, 5, :] = (i[:, None] <= i[None, :])
    c[:, 6, :] = (i[:, None] >= i[None, :])
    return c


def _rot(T):
    inv = (np.float32(ROPE_THETA) ** (-np.arange(0, 16, 2, dtype=np.float32) / np.float32(16))).astype(np.float32)
    ang = (np.arange(T, dtype=np.float32)[:, None] * inv[None, :]).astype(np.float32)
    cos, sin = np.cos(ang).astype(np.float32), np.sin(ang).astype(np.float32)
    return np.concatenate([cos, cos, -sin, sin], axis=1).astype(np.float32)


def make_in_maps(inputs, T, depth, ncores):
    f = lambda a: np.ascontiguousarray(np.asarray(a, dtype=np.float32))
    shared = {
        "w_mod": f(inputs["w_mod"]), "b_mod": f(inputs["b_mod"]),
        "mix_norm_w": f(np.asarray(inputs["mix_norm_w"]).reshape(depth, KC, 128).transpose(0, 2, 1)),
        "ffn_norm_w": f(np.asarray(inputs["ffn_norm_w"]).reshape(depth, KC, 128).transpose(0, 2, 1)),
        "w_in": f(inputs["w_in"]), "w_out": f(inputs["w_out"]),
        "dn_conv_w": f(np.asarray(inputs["dn_conv_w"]).transpose(0, 2, 1).reshape(depth, 12, 128, 4).transpose(0, 2, 1, 3)),
        "dn_a_log": f(inputs["dn_a_log"]), "dn_dt_bias": f(inputs["dn_dt_bias"]),
        "dn_out_norm_w": f(inputs["dn_out_norm_w"]),
        "gm_ln_g": f(inputs["gm_ln_g"]), "gm_ln_b": f(inputs["gm_ln_b"]),
        "gm_w_sT": f(np.asarray(inputs["gm_w_s"]).transpose(0, 3, 1, 2)),
        "gm_b_s": f(np.asarray(inputs["gm_b_s"]).transpose(0, 2, 1)),
        "sw_q_norm_w": f(inputs["sw_q_norm_w"]), "sw_k_norm_w": f(inputs["sw_k_norm_w"]),
        "w_ffn_in": f(inputs["w_ffn_in"]), "w_ffn_out": f(inputs["w_ffn_out"]),
        "consts": _consts(), "rot": _rot(T),
    }
    x = np.asarray(inputs["x"], dtype=np.float32)
    c = np.asarray(inputs["c"], dtype=np.float32)
    maps = []
    for b in range(ncores):
        m = dict(shared)
        m["x"] = np.ascontiguousarray(x[b])
        m["c"] = np.ascontiguousarray(c[b].reshape(KC, 128).T)
        maps.append(m)
    return maps


def kernel(**inputs):
    nc = build(SEQ, DEPTH, False)
    maps = make_in_maps(inputs, SEQ, DEPTH, NCORES)
    res = run_bass_kernel_spmd(nc, maps, core_ids=list(range(NCORES)))
    return np.stack([np.asarray(r["out"], dtype=np.float32) for r in res.results], axis=0)
```

```python
import numpy as np
from contextlib import ExitStack
import concourse.bass as bass
import concourse.mybir as mybir
from concourse.bass_utils import run_bass_kernel_spmd

F32 = mybir.dt.float32
BF16 = mybir.dt.bfloat16
AF = mybir.ActivationFunctionType
ALU = mybir.AluOpType
AX = mybir.AxisListType

D = 1024
KC = 8
DEPTH = 4
SEQ = 4096
NCORES = 8
EPS = 1e-6
INW = 4872
FH = 2816
NFC = 22
SEM_ROLL = 15000
PATTERNS = ((128, 1), (512, 4), (2048, 16))
CUT = 99
F32R = False
ROPE_THETA = 500000.0


class Tile:
    def __init__(self, k, t, name, dram=False):
        self.k, self.t, self.name, self.dram = k, t, name, dram
        self.writes = {}
        self.reads = {}
        self.dsem = None
        self.dcnt = 0

    def __getitem__(self, idx):
        return TV(self, self.t[idx])

    def v(self, ap):
        return TV(self, ap)


class TV:
    def __init__(self, tile, ap, extra=()):
        self.tile, self.ap, self.extra = tile, ap, extra


def _m(d, s):
    for sem, v in s.items():
        if d.get(sem, 0) < v:
            d[sem] = v


class K:
    def __init__(self, nc, es):
        self.nc, self.es = nc, es
        self.engs = {'pe': nc.tensor, 'act': nc.scalar, 'dve': nc.vector, 'pool': nc.gpsimd, 'sp': nc.sync}
        self.cnt = {}
        self.sem = {}
        self.own = {e: set() for e in self.engs}
        self.waited = {e: {} for e in self.engs}
        self.nsem = 0
        for e in self.engs:
            self._newsem(e)
        self.dma_tiles = []
        self.dpool = {'sp': [], 'pool': []}
        self.scope = None
        self.ninst = {e: 0 for e in self.engs}

    def _newsem(self, e):
        s = self.es.enter_context(self.nc.semaphore(f"s_{e}_{self.nsem}"))
        self.nsem += 1
        self.sem[e] = s
        self.cnt[e] = 0
        self.own[e].add(s)

    def sb(self, name, shape, dt, es=None):
        self.nsem += 1
        name = f"{name}_{self.nsem}"
        t = (es or self.es).enter_context(self.nc.sbuf_tensor(name, list(shape), dt))
        tl = Tile(self, t, name)
        if es is not None and self.scope is not None:
            self.scope.append(tl)
        return tl

    def begin_phase(self):
        self.scope = []

    def end_phase(self):
        self.barrier()
        for tl in self.scope:
            if tl.dsem is not None:
                if 16 * tl.dcnt < 2000:
                    self.dpool[tl.dq].append((tl.dsem, tl.dcnt))
                else:
                    self.retired = getattr(self, 'retired', 0) + 1
                self.dma_tiles.remove(tl)
                tl.dsem = None
        self.scope = None

    def ps(self, name, shape, dt, es=None):
        self.nsem += 1
        name = f"{name}_{self.nsem}"
        t = (es or self.es).enter_context(self.nc.psum_tensor(name, list(shape), dt))
        tl = Tile(self, t, name)
        tl.psum = True
        return tl

    def dram(self, ap, name):
        return Tile(self, ap, name, dram=True)

    @staticmethod
    def _tiles(tvs):
        out = []
        for tv in tvs:
            out.append(tv.tile)
            out.extend(tv.extra)
        return out

    def _waits(self, e, reads, writes):
        need = {}
        for t in self._tiles(reads):
            _m(need, t.writes)
            if getattr(t, 'psum', False):
                for sem, v in t.reads.items():
                    if sem not in self.own[e] and need.get(sem, 0) < v:
                        need[sem] = v
        for t in self._tiles(writes):
            _m(need, t.writes)
            _m(need, t.reads)
        eng = self.engs[e]
        w = self.waited[e]
        for sem, val in need.items():
            if e == 'pe' and sem in self.own['pe']:
                continue
            if w.get(sem, 0) < val:
                eng.wait_ge(sem, val)
                w[sem] = val
                self.ninst[e] += 1

    def _done(self, tok, reads, writes):
        sem, val = tok
        for t in self._tiles(reads):
            if t.reads.get(sem, 0) < val:
                t.reads[sem] = val
        for t in self._tiles(writes):
            if t.dram:
                if t.writes.get(sem, 0) < val:
                    t.writes[sem] = val
            else:
                t.writes = {sem: val}
            t.reads = {}

    def op(self, e, emit, reads, writes):
        self._waits(e, reads, writes)
        ins = emit(self.engs[e])
        if self.cnt[e] >= SEM_ROLL:
            self._newsem(e)
        self.cnt[e] += 1
        ins.then_inc(self.sem[e], 1)
        self.ninst[e] += 1
        tok = (self.sem[e], self.cnt[e])
        self._done(tok, reads, writes)
        return tok

    def dma(self, q, out, in_, **kw):
        own = out.tile if not out.tile.dram else in_.tile
        assert not own.dram
        if own.dsem is None:
            own.dq = q
            if self.dpool[q]:
                own.dsem, own.dcnt = self.dpool[q].pop(0)
            else:
                own.dsem = self.es.enter_context(self.nc.semaphore(f"d_{self.nsem}"))
                self.nds = getattr(self, 'nds', 0) + 1
                self.nsem += 1
            self.dma_tiles.append(own)
        assert own.dq == q, (own.name, own.dq, q)
        self._waits(q, [in_], [out])
        ins = self.engs[q].dma_start(out=out.ap, in_=in_.ap, **kw)
        own.dcnt += 1
        ins.then_inc(own.dsem, 16)
        self.ninst[q] += 1
        tok = (own.dsem, 16 * own.dcnt)
        self._done(tok, [in_], [out])
        return tok

    def load_w(self, dst, src2d, nk, cols, RO, q='pool', cstep=4096):
        for kc in range(nk):
            for c0 in range(0, cols, cstep):
                c1 = min(cols, c0 + cstep)
                self.dma(q, dst[:, kc, c0:c1], RO(src2d[kc * 128:(kc + 1) * 128, c0:c1]))

    def barrier(self):
        need = {}
        for e in self.engs:
            if self.cnt[e] > 0:
                need[self.sem[e]] = self.cnt[e]
        for t in self.dma_tiles:
            need[t.dsem] = 16 * t.dcnt
        for e in self.engs:
            w = self.waited[e]
            for sem, val in need.items():
                if e == 'pe' and sem in self.own['pe']:
                    continue
                if w.get(sem, 0) < val:
                    self.engs[e].wait_ge(sem, val)
                    w[sem] = val
                    self.ninst[e] += 1

    def mm(self, out, lhsT, rhs, start=True, stop=True):
        la, ra = lhsT.ap, rhs.ap
        if F32R and la.dtype == F32 and ra.dtype == F32:
            la = la.bitcast(mybir.dt.float32r)
            ra = ra.bitcast(mybir.dt.float32r)
        return self.op('pe', lambda g: g.matmul(out.ap, la, ra, start=start, stop=stop),
                       [lhsT, rhs], [out])

    def pe_batch(self, descrs):
        rd, wr = [], []
        for d in descrs:
            wr.append(d[1])
            rd.extend([d[2], d[3]])
        self._waits('pe', rd, wr)
        for d in descrs:
            if d[0] == 'mm':
                self.mm(d[1], d[2], d[3])
            else:
                self.tr(d[1], d[2], d[3])

    def tr(self, out, in_, ident):
        return self.op('pe', lambda g: g.transpose(out.ap, in_.ap, ident.ap), [in_, ident], [out])

    def act(self, out, in_, func, bias=None, scale=None, accum=None):
        kw = {}
        rd = [in_]
        wr = [out]
        if bias is not None:
            if isinstance(bias, TV):
                kw['bias'] = bias.ap
                rd.append(bias)
            else:
                kw['bias'] = bias
        if scale is not None:
            if isinstance(scale, TV):
                kw['scale'] = scale.ap
                rd.append(scale)
            else:
                kw['scale'] = scale
        if accum is not None:
            kw['accum_out'] = accum.ap
            wr.append(accum)
        return self.op('act', lambda g: g.activation(out=out.ap, in_=in_.ap, func=func, **kw), rd, wr)

    def tt(self, e, out, in0, in1, op):
        return self.op(e, lambda g: g.tensor_tensor(out=out.ap, in0=in0.ap, in1=in1.ap, op=op), [in0, in1], [out])

    def ts(self, e, out, in0, s1, s2, op0, op1=None):
        rd = [in0]
        a1, a2 = s1, s2
        if isinstance(s1, TV):
            rd.append(s1)
            a1 = s1.ap
        if isinstance(s2, TV):
            rd.append(s2)
            a2 = s2.ap
        if op1 is None:
            return self.op(e, lambda g: g.tensor_scalar(out=out.ap, in0=in0.ap, scalar1=a1, scalar2=None, op0=op0),
                           rd, [out])
        return self.op(e, lambda g: g.tensor_scalar(out=out.ap, in0=in0.ap, scalar1=a1, scalar2=a2, op0=op0, op1=op1),
                       rd, [out])

    def stt(self, e, out, in0, s, in1, op0, op1):
        rd = [in0, in1]
        a = s
        if isinstance(s, TV):
            rd.append(s)
            a = s.ap
        return self.op(e, lambda g: g.scalar_tensor_tensor(out=out.ap, in0=in0.ap, scalar=a, in1=in1.ap,
                                                           op0=op0, op1=op1), rd, [out])

    def copy(self, e, out, in_):
        if e == 'act':
            return self.op(e, lambda g: g.copy(out=out.ap, in_=in_.ap), [in_], [out])
        return self.op(e, lambda g: g.tensor_copy(out=out.ap, in_=in_.ap), [in_], [out])

    def amul(self, out, in_, m):
        return self.op('act', lambda g: g.mul(out=out.ap, in_=in_.ap, mul=m.ap), [in_, m], [out])

    def memset(self, e, out, val):
        return self.op(e, lambda g: g.memset(out.ap, val), [], [out])

    def recip(self, out, in_):
        return self.op('dve', lambda g: g.reciprocal(out=out.ap, in_=in_.ap), [in_], [out])

    def reduce(self, e, out, in_, op):
        return self.op(e, lambda g: g.tensor_reduce(out=out.ap, in_=in_.ap, axis=AX.X, op=op), [in_], [out])


def build(T=SEQ, depth=DEPTH, debug=False, phases='MNABCDF'):
    NT = T // 128
    NB = T // 512
    nc = bass.Bass("TRN2", target_bir_lowering=False)
    es = ExitStack()

    def din(name, shape):
        return nc.dram_tensor(name, list(shape), F32, kind="ExternalInput").ap()

    x_in = din("x", [T, D])
    c_in = din("c", [128, KC])
    w_mod = din("w_mod", [depth, D, 6 * D])
    b_mod = din("b_mod", [depth, 6 * D])
    mixw = din("mix_norm_w", [depth, 128, KC])
    ffnw = din("ffn_norm_w", [depth, 128, KC])
    w_in = din("w_in", [depth, D, INW])
    w_out = din("w_out", [depth, D, D])
    convw = din("dn_conv_w", [depth, 128, 12, 4])
    a_log = din("dn_a_log", [depth, 4])
    dt_b = din("dn_dt_bias", [depth, 4])
    onw_in = din("dn_out_norm_w", [depth, 128])
    lng_in = din("gm_ln_g", [depth, 256])
    lnb_in = din("gm_ln_b", [depth, 256])
    wsT_in = din("gm_w_sT", [depth, 128, 4, 128])
    bs_in = din("gm_b_s", [depth, 128, 4])
    qnw_in = din("sw_q_norm_w", [depth, 64])
    knw_in = din("sw_k_norm_w", [depth, 64])
    wf_in = din("w_ffn_in", [depth, D, 2 * FH])
    wf_out = din("w_ffn_out", [depth, FH, D])
    cst_in = din("consts", [128, 8, 128])
    rot_in = din("rot", [T, 32])
    out_d = nc.dram_tensor("out", [T, D], F32, kind="ExternalOutput").ap()
    skind = "ExternalOutput" if debug else "Internal"
    modrow_d = nc.dram_tensor("modrow", [depth, 6 * D], F32, kind=skind).ap()
    yT_d = nc.dram_tensor("yT", [NT, 128, 768], BF16, kind="Internal").ap()
    nl_d = [nc.dram_tensor(f"nl{p}", [T, 260], F32, kind=skind).ap() for p in range(3)]
    dbg_d = nc.dram_tensor("dbg", [T, D], F32, kind=skind).ap() if debug else None

    with es:
        k = K(nc, es)
        xin_t = [k.dram(x_in[i * 128:(i + 1) * 128, :], f"xin{i}") for i in range(NT)]
        out_t = [k.dram(out_d[i * 128:(i + 1) * 128, :], f"out{i}") for i in range(NT)]
        wts = k.dram(w_mod, "wts")
        modrow_t = k.dram(modrow_d, "modrow")
        yT_t = [k.dram(yT_d[i], f"yT{i}") for i in range(NT)]
        nl_t = [k.dram(nl_d[p], f"nl{p}") for p in range(3)]
        dbg_t = k.dram(dbg_d, "dbg") if debug else None

        def RO(ap):
            return wts.v(ap)

        hT = k.sb("hT", [128, KC, T], BF16)
        hT_b = [Tile(k, hT.t, f"hTb{b}") for b in range(NB)]
        cst = k.sb("cst", [128, 8, 128], F32)
        identb = k.sb("identb", [128, 128], BF16)
        maskCb = k.sb("maskCb", [128, 128], BF16)
        maskPb = k.sb("maskPb", [128, 128], BF16)
        epsc = k.sb("epsc", [128, 1], F32)
        k.dma('sp', cst[:], RO(cst_in))
        ident = cst[:, 0, :]
        Um = cst[:, 1, :]
        SLm = cst[:, 2, :]
        ones = cst[:, 3, :]
        maskS = cst[:, 4, :]
        maskUi = cst[:, 5, :]
        maskP = cst[:, 6, :]
        k.copy('dve', identb[:], ident)
        k.copy('dve', maskCb[:], maskUi)
        k.copy('dve', maskPb[:], maskP)
        k.memset('dve', epsc[:], EPS)

        a1 = k.sb("a1", [128, KC], F32)
        b1 = k.sb("b1", [128, KC], F32)
        a2 = k.sb("a2", [128, KC], F32)
        b2 = k.sb("b2", [128, KC], F32)
        g1bc = k.sb("g1bc", [128, D], F32)
        g2bc = k.sb("g2bc", [128, D], F32)
        nw1 = k.sb("nw1", [128, KC], F32)
        nw2 = k.sb("nw2", [128, KC], F32)
        mcol = k.sb("mcol", [128, 4, KC], F32)

        with ExitStack() as pes:
            k.begin_phase()
            c_sb = k.sb("c_sb", [128, KC], F32, pes)
            cact = k.sb("cact", [128, KC], F32, pes)
            wm = [k.sb(f"wm{i}", [128, KC, 512], F32, pes) for i in range(4)]
            bm = k.sb("bm", [1, 6 * D], F32, pes)
            mr = k.sb("mr", [1, 6 * D], F32, pes)
            pm = [k.ps(f"pm{i}", [128, 512], F32, pes) for i in range(2)]
            k.dma('sp', c_sb[:], RO(c_in))
            k.act(cact[:], c_sb[:], AF.Silu)
            it = 0
            for l in range(depth):
                k.dma('sp', bm[:], RO(b_mod[l:l + 1, :]))
                for ec in range(12):
                    w = wm[it % 4]
                    p = pm[it % 2]
                    q = 'sp' if it % 2 == 0 else 'pool'
                    k.dma(q, w[:], RO(w_mod[l, :, ec * 512:(ec + 1) * 512].rearrange("(kc p) e -> p kc e", p=128)))
                    for kc in range(KC):
                        k.mm(p[0:1, :], cact[:, kc:kc + 1], w[:, kc, :], start=(kc == 0), stop=(kc == KC - 1))
                    k.tt('dve', mr[0:1, ec * 512:(ec + 1) * 512], p[0:1, :], bm[0:1, ec * 512:(ec + 1) * 512], ALU.add)
                    it += 1
                k.dma('sp', modrow_t.v(modrow_d[l:l + 1, :]), mr[:])
            k.end_phase()

        def load_layer_params(l):
            for j, src in enumerate((0, 1, 3, 4)):
                k.dma('sp', mcol[:, j, :],
                      modrow_t.v(modrow_d[l, src * D:(src + 1) * D].rearrange("(kc p) -> p kc", p=128)),
                      allow_slow_non_contiguous=True)
            k.dma('sp', g1bc[:], modrow_t.v(modrow_d[l, 2 * D:3 * D].partition_broadcast(128)))
            k.dma('sp', g2bc[:], modrow_t.v(modrow_d[l, 5 * D:6 * D].partition_broadcast(128)))
            k.dma('sp', nw1[:], RO(mixw[l]))
            k.dma('sp', nw2[:], RO(ffnw[l]))
            k.stt('dve', a1[:], mcol[:, 1, :], 1.0, nw1[:], ALU.add, ALU.mult)
            k.copy('dve', b1[:], mcol[:, 0, :])
            k.stt('dve', a2[:], mcol[:, 3, :], 1.0, nw2[:], ALU.add, ALU.mult)
            k.copy('dve', b2[:], mcol[:, 2, :])

        def norm_gen(xt, tt, aa, bb, sq, ss, rs, xn, ptr):
            hb = hT_b[tt // 4]
            k.memset('dve', ss[:, 0:1], 0.0)
            k.act(sq[:], xt[:], AF.Square, accum=ss[:, 0:1])
            yield
            k.act(rs[:, 0:1], ss[:, 0:1], AF.Sqrt, bias=epsc[:, 0:1], scale=1.0 / D)
            k.recip(rs[:, 1:2], rs[:, 0:1])
            yield
            k.ts('dve', xn[:], xt[:], rs[:, 1:2], None, ALU.mult)
            yield
            for half in range(2):
                p = ptr[half]
                for j in range(4):
                    kc = half * 4 + j
                    k.tr(p[:, j, :], xn[:, kc * 128:(kc + 1) * 128], ident)
                yield
                for j in range(4):
                    kc = half * 4 + j
                    k.ts('dve', hb.v(hT.t[:, kc, tt * 128:(tt + 1) * 128]), p[:, j, :], aa[:, kc:kc + 1], bb[:, kc:kc + 1],
                         ALU.mult, ALU.add)
                    if j % 2 == 1:
                        yield

        def norm_to_hT(*a):
            for _ in norm_gen(*a):
                pass

        def rr(gens):
            gens = list(gens)
            while gens:
                for g in list(gens):
                    try:
                        next(g)
                    except StopIteration:
                        gens.remove(g)

        for l in range(depth):
            load_layer_params(l)
            xsrc = xin_t if l == 0 else out_t

            with ExitStack() as pes:
                k.begin_phase()
                xt = [k.sb(f"n1x{i}", [128, D], F32, pes) for i in range(2)]
                sq = k.sb("n1sq", [128, D], F32, pes)
                xn = [k.sb(f"n1xn{i}", [128, D], F32, pes) for i in range(2)]
                ss = [k.sb(f"n1ss{i}", [128, 1], F32, pes) for i in range(2)]
                rs = [k.sb(f"n1rs{i}", [128, 2], F32, pes) for i in range(2)]
                ptr = [[k.ps(f"n1p{i}{h}", [128, 4, 128], F32, pes) for h in range(2)] for i in range(2)]
                for tt in range(NT):
                    i = tt % 2
                    k.dma('sp', xt[i][:], xsrc[tt][:])
                    norm_to_hT(xt[i], tt, a1, b1, sq, ss[i], rs[i], xn[i], ptr[i])
                k.end_phase()

            if 'A' in phases:
                phase_A(k, l, T, hT, hT_b, RO, w_in, convw, a_log, dt_b, onw_in, cst, identb, epsc, yT_t, yT_d)
            if 'B' in phases:
                phase_B(k, l, T, hT, hT_b, RO, w_in, lng_in, lnb_in, wsT_in, bs_in, cst, identb, epsc, yT_t, yT_d)
            if 'C' in phases:
                phase_C(k, l, T, hT, hT_b, RO, w_in, qnw_in, knw_in, rot_in, identb, maskCb, maskPb, epsc, nl_t, nl_d)

            with ExitStack() as pes:
                if 'D' not in phases:
                    break
                k.begin_phase()
                wo = k.sb("wo", [128, KC, D], BF16, pes)
                k.load_w(wo, w_out[l], KC, D, RO)
                nl = [[k.sb(f"dnl{i}{p}", [128, 4, 65], F32, pes) for p in range(3)] for i in range(2)]
                nsum = k.sb("dnsum", [128, 4, 65], F32, pes)
                rl = k.sb("drl", [128, 4], F32, pes)
                ycb = k.sb("dycb", [128, 4, 64], BF16, pes)
                yT = [k.sb(f"dyT{i}", [128, 8, 128], BF16, pes) for i in range(2)]
                xt = [k.sb(f"dx{i}", [128, D], F32, pes) for i in range(2)]
                xo = [k.sb(f"dxo{i}", [128, D], F32, pes) for i in range(2)]
                tmp = k.sb("dtmp", [128, D], F32, pes)
                sq = k.sb("dsq", [128, D], F32, pes)
                xn = k.sb("dxn", [128, D], F32, pes)
                ss = k.sb("dss", [128, 1], F32, pes)
                rs = k.sb("drs", [128, 2], F32, pes)
                ptc = k.ps("dptc", [128, 8, 128], BF16, pes)
                po = [k.ps(f"dpo{h}", [128, 512], F32, pes) for h in range(2)]
                ptr = [k.ps(f"dptr{h}", [128, 4, 128], F32, pes) for h in range(2)]
                def d1a(tt):
                    i = tt % 2
                    for p in range(3):
                        k.dma('sp', nl[i][p][:], nl_t[p].v(nl_d[p][tt * 128:(tt + 1) * 128, :].rearrange("p (h e) -> p h e", h=4)))
                    k.dma('sp', yT[i][:, 0:6, :], yT_t[tt].v(yT_d[tt].rearrange("p (c t) -> p c t", c=6)))
                    k.dma('sp', xt[i][:], xsrc[tt][:])
                    k.tt('dve', nsum[:], nl[i][0][:], nl[i][1][:], ALU.add)
                    yield
                    k.tt('dve', nsum[:], nsum[:], nl[i][2][:], ALU.add)
                    yield
                    k.recip(rl[:], nsum[:, :, 64])
                    yield
                    k.tt('dve', ycb[:], nsum[:, :, 0:64], rl.v(rl.t[:, :].unsqueeze(2).to_broadcast([128, 4, 64])), ALU.mult)
                    yield
                    for j in range(2):
                        k.tr(ptc[:, j, :], ycb.v(ycb.t[:, 2 * j:2 * j + 2, :].rearrange("p h e -> p (h e)")), identb[:])
                    yield
                    k.copy('act', yT[i][:, 6:8, :], ptc[:, 0:2, :])
                    yield

                def d1b(tt):
                    i = tt % 2
                    for half in range(2):
                        for c in range(8):
                            k.mm(po[half][:], yT[i][:, c, :], wo[:, c, half * 512:(half + 1) * 512],
                                 start=(c == 0), stop=(c == 7))
                        yield
                    for half in range(2):
                        sl = slice(half * 512, (half + 1) * 512)
                        k.tt('dve', tmp[:, sl], po[half][:], g1bc[:, sl], ALU.mult)
                        yield
                        k.tt('dve' if half == 0 else 'pool', xo[i][:, sl], tmp[:, sl], xt[i][:, sl], ALU.add)
                        yield
                    k.dma('sp', out_t[tt][:], xo[i][:])
                    yield

                for it_ in range(NT + 2):
                    g = []
                    if 0 <= it_ - 2 < NT:
                        t2 = it_ - 2
                        g.append(norm_gen(xo[t2 % 2], t2, a2, b2, sq, ss, rs, xn, ptr))
                    if 0 <= it_ - 1 < NT:
                        g.append(d1b(it_ - 1))
                    if it_ < NT:
                        g.append(d1a(it_))
                    rr(g)
                k.end_phase()

            groups = [(0, 6), (6, 6), (12, 5), (17, 5)] if 'F' in phases else []
            if groups:
                with ExitStack() as pes:
                    k.begin_phase()
                    wg = [k.sb(f"fwg{i}", [128, KC, 768], BF16, pes) for i in range(2)]
                    wu = [k.sb(f"fwu{i}", [128, KC, 768], BF16, pes) for i in range(2)]
                    wo2 = [k.sb(f"fwo{i}", [128, 6, D], BF16, pes) for i in range(2)]
                    actT = [k.sb(f"fact{i}", [128, 6, 512], BF16, pes) for i in range(2)]
                    sg = [k.sb(f"fsg{i}", [128, 512], F32, pes) for i in range(2)]
                    xt = [k.sb(f"fx{i}", [128, D], F32, pes) for i in range(2)]
                    xo = [k.sb(f"fxo{i}", [128, D], F32, pes) for i in range(2)]
                    tmp = k.sb("ftmp", [128, D], F32, pes)
                    pg = [k.ps(f"fpg{i}", [128, 512], F32, pes) for i in range(2)]
                    pu = [k.ps(f"fpu{i}", [128, 512], F32, pes) for i in range(2)]
                    po = [[k.ps(f"fpo{i}{h}", [128, 512], F32, pes) for h in range(2)] for i in range(2)]

                    def loadw(gi):
                        f0, nf = groups[gi]
                        j = gi % 2
                        for kc in range(KC):
                            k.dma('pool', wg[j][:, kc, 0:nf * 128], RO(wf_in[l, kc * 128:(kc + 1) * 128, f0 * 128:(f0 + nf) * 128]))
                            k.dma('pool', wu[j][:, kc, 0:nf * 128], RO(wf_in[l, kc * 128:(kc + 1) * 128, FH + f0 * 128:FH + (f0 + nf) * 128]))
                        for c in range(nf):
                            k.dma('pool', wo2[j][:, c, :], RO(wf_out[l, (f0 + c) * 128:(f0 + c + 1) * 128, :]))

                    loadw(0)
                    it = 0
                    xi = 0
                    ab = 0
                    for gi, (f0, nf) in enumerate(groups):
                        if gi + 1 < len(groups):
                            loadw(gi + 1)
                        wj = gi % 2
                        for blk in range(NB):
                            hb = hT_b[blk]
                            a = actT[ab % 2]
                            ab += 1
                            for fi in range(nf):
                                j = it % 2
                                it += 1
                                for kc in range(KC):
                                    k.mm(pg[j][:], wg[wj][:, kc, fi * 128:(fi + 1) * 128], hb.v(hT.t[:, kc, blk * 512:(blk + 1) * 512]),
                                         start=(kc == 0), stop=(kc == KC - 1))
                                for kc in range(KC):
                                    k.mm(pu[j][:], wu[wj][:, kc, fi * 128:(fi + 1) * 128], hb.v(hT.t[:, kc, blk * 512:(blk + 1) * 512]),
                                         start=(kc == 0), stop=(kc == KC - 1))
                                k.act(sg[j][:], pg[j][:], AF.Silu)
                                k.tt('dve', a[:, fi, :], sg[j][:], pu[j][:], ALU.mult)
                            for t4 in range(4):
                                tt = blk * 4 + t4
                                i = xi % 2
                                xi += 1
                                k.dma('sp', xt[i][:], out_t[tt][:])
                                for half in range(2):
                                    for fi in range(nf):
                                        k.mm(po[i][half][:], a[:, fi, t4 * 128:(t4 + 1) * 128], wo2[wj][:, fi, half * 512:(half + 1) * 512],
                                             start=(fi == 0), stop=(fi == nf - 1))
                                for half in range(2):
                                    sl = slice(half * 512, (half + 1) * 512)
                                    k.tt('dve', tmp[:, sl], po[i][half][:], g2bc[:, sl], ALU.mult)
                                    k.tt('dve' if half == 0 else 'pool', xo[i][:, sl], tmp[:, sl], xt[i][:, sl], ALU.add)
                                k.dma('sp', out_t[tt][:], xo[i][:])
                    k.end_phase()
        k.barrier()
        build.ninst = dict(k.ninst)
        build.retired = (getattr(k, 'retired', 0), getattr(k, 'nds', 0), len(k.own['pe']) + len(k.own['act']) + len(k.own['dve']) + len(k.own['pool']) + len(k.own['sp']))
        build.maxd = (max([c for _, c in k.dpool['sp']] + [0]) * 16, max([c for _, c in k.dpool['pool']] + [0]) * 16, k.nsem)
    return nc


def phase_B(k, l, T, hT, hT_b, RO, w_in, lng_in, lnb_in, wsT_in, bs_in, cst, identb, epsc, yT_t, yT_d):
    NT = T // 128
    c0 = 2056
    maskUi = cst[:, 5, :]
    with ExitStack() as pes:
        k.begin_phase()
        wb = k.sb("bw", [128, KC, 512], BF16, pes)
        k.load_w(wb, w_in[l, :, c0:c0 + 512], KC, 512, RO)
        lng = k.sb("blng", [128, 256], F32, pes)
        lnb = k.sb("blnb", [128, 256], F32, pes)
        wsf = k.sb("bwsf", [128, 4, 128], F32, pes)
        wsb = k.sb("bwsb", [128, 4, 128], BF16, pes)
        bs = k.sb("bbs", [128, 4], F32, pes)
        k.dma('sp', lng[:], RO(lng_in[l].partition_broadcast(128)))
        k.dma('sp', lnb[:], RO(lnb_in[l].partition_broadcast(128)))
        k.dma('sp', wsf[:], RO(wsT_in[l]))
        k.dma('sp', bs[:], RO(bs_in[l]))
        k.tt('dve', wsb[:], wsf[:], maskUi.tile.v(cst.t[:, 5:6, :].to_broadcast([128, 4, 128])), ALU.mult)
        gl = [k.sb(f"bgl{i}", [128, 512], F32, pes) for i in range(2)]
        st = k.sb("bst", [128, 6], F32, pes)
        mv = k.sb("bmv", [128, 2], F32, pes)
        rs = k.sb("brs", [128, 2], F32, pes)
        vn = k.sb("bvn", [128, 256], F32, pes)
        vn2 = k.sb("bvn2", [128, 256], F32, pes)
        vnb = [k.sb(f"bvnb{i}", [128, 256], BF16, pes) for i in range(2)]
        yb = [k.sb(f"byb{i}", [128, 256], BF16, pes) for i in range(2)]
        ybT = [k.sb(f"bybT{i}", [128, 2, 128], BF16, pes) for i in range(2)]
        pp = [k.ps(f"bpp{i}", [128, 512], F32, pes) for i in range(2)]
        psv = [k.ps(f"bpsv{i}", [128, 512], F32, pes) for i in range(2)]
        ptr = [k.ps(f"bptr{i}", [128, 8, 128], BF16, pes) for i in range(2)]
        def b1(tt):
            i = tt % 2
            hb = hT_b[tt // 4]
            for kc in range(KC):
                k.mm(pp[i][:], hb.v(hT.t[:, kc, tt * 128:(tt + 1) * 128]), wb[:, kc, :], start=(kc == 0), stop=(kc == KC - 1))
            yield
            k.act(gl[i][:], pp[i][:], AF.Gelu)
            yield
            k.op('dve', lambda g: g.bn_stats(out=st.t[:, 0:6], in_=gl[i].t[:, 256:512]), [gl[i][:]], [st[:]])
            k.op('dve', lambda g: g.bn_aggr(out=mv.t[:, 0:2], in_=st.t[:, 0:6]), [st[:]], [mv[:]])
            yield
            k.act(rs[:, 0:1], mv[:, 1:2], AF.Sqrt, bias=epsc[:, 0:1], scale=1.0)
            k.recip(rs[:, 1:2], rs[:, 0:1])
            yield
            k.ts('dve', vn[:], gl[i][:, 256:512], mv[:, 0:1], rs[:, 1:2], ALU.subtract, ALU.mult)
            yield
            k.tt('dve', vn2[:], vn[:], lng[:], ALU.mult)
            yield
            k.tt('dve', vnb[i][:], vn2[:], lnb[:], ALU.add)
            yield

        def b2(tt):
            i = tt % 2
            for g in range(4):
                k.mm(psv[i][:, g * 64:(g + 1) * 64], wsb[:, g, :], vnb[i][:, g * 64:(g + 1) * 64])
            yield
            for g in range(4):
                k.stt('dve', yb[i][:, g * 64:(g + 1) * 64], psv[i][:, g * 64:(g + 1) * 64], bs[:, g:g + 1],
                      gl[i][:, g * 64:(g + 1) * 64], ALU.add, ALU.mult)
                if g % 2 == 1:
                    yield
            for j in range(2):
                k.tr(ptr[i][:, j, :], yb[i][:, j * 128:(j + 1) * 128], identb[:])
            yield
            k.copy('act', ybT[i][:], ptr[i][:, 0:2, :])
            yield
            k.dma('sp', yT_t[tt].v(yT_d[tt][:, 512:768].rearrange("p (c t) -> p c t", c=2)), ybT[i][:])

        def rr(gens):
            gens = list(gens)
            while gens:
                for g in list(gens):
                    try:
                        next(g)
                    except StopIteration:
                        gens.remove(g)

        rr([b1(0)])
        for tt in range(NT):
            g = [b2(tt)]
            if tt + 1 < NT:
                g.append(b1(tt + 1))
            rr(g)
        k.end_phase()


def phase_C(k, l, T, hT, hT_b, RO, w_in, qnw_in, knw_in, rot_in, identb, maskCb, maskPb, epsc, nl_t, nl_d):
    c0 = 2568
    with ExitStack() as pes:
        k.begin_phase()
        wc = k.sb("cw", [128, KC, 2304], BF16, pes)
        k.load_w(wc, w_in[l, :, c0:c0 + 2304], KC, 2304, RO)
        nwq = k.sb("cnwq", [128, 64], F32, pes)
        nwk = k.sb("cnwk", [128, 64], F32, pes)
        nw = k.sb("cnw", [128, 8, 64], F32, pes)
        k.dma('sp', nwq[:], RO(qnw_in[l].partition_broadcast(128)))
        k.dma('sp', nwk[:], RO(knw_in[l].partition_broadcast(128)))
        for h in range(4):
            k.copy('dve', nw[:, h, :], nwq[:])
            k.copy('dve', nw[:, 4 + h, :], nwk[:])
        rot = [k.sb(f"crot{i}", [128, 32], F32, pes) for i in range(3)]
        sqt = [k.sb(f"csq{i}", [128, 512], F32, pes) for i in range(2)]
        ss = [k.sb(f"css{i}", [128, 8], F32, pes) for i in range(2)]
        rs = [k.sb(f"crs{i}", [128, 8], F32, pes) for i in range(2)]
        ri = [k.sb(f"cri{i}", [128, 8], F32, pes) for i in range(2)]
        qkf = k.sb("cqkf", [128, 8, 64], F32, pes)
        qkg = k.sb("cqkg", [128, 8, 64], F32, pes)
        ra = k.sb("cra", [128, 8, 16], F32, pes)
        rb = k.sb("crb", [128, 8, 16], F32, pes)
        qkb = k.sb("cqkb", [128, 8, 64], BF16, pes)
        v1 = [k.sb(f"cv1{i}", [128, 4, 80], BF16, pes) for i in range(4)]
        qT = [k.sb(f"cqT{i}", [64, 4, 128], BF16, pes) for i in range(2)]
        kT = [k.sb(f"ckT{i}", [64, 4, 128], BF16, pes) for i in range(3)]
        P0 = k.sb("cP0", [128, 4, 128], BF16, pes)
        P1 = k.sb("cP1", [128, 4, 128], BF16, pes)
        P0m = k.sb("cP0m", [128, 4, 128], BF16, pes)
        P1m = k.sb("cP1m", [128, 4, 128], BF16, pes)
        nlo = [k.sb(f"cnlo{i}", [128, 4, 65], F32, pes) for i in range(2)]
        pc0b = [k.ps(f"cpc0{i}", [128, 512], F32, pes) for i in range(2)]
        pc1 = k.ps("cpc1", [128, 512], F32, pes)
        ptr = k.ps("cptr", [128, 8, 128], BF16, pes)
        pS0 = k.ps("cpS0", [128, 4, 128], F32, pes)
        pS1 = k.ps("cpS1", [128, 4, 128], F32, pes)
        pO = k.ps("cpO", [128, 4, 128], F32, pes)
        for i in range(4):
            k.memset('pool', v1[i][:], 1.0)
        mC = maskCb.v(maskCb.t[:, :].unsqueeze(1).to_broadcast([128, 4, 128]))
        mP = maskPb.v(maskPb.t[:, :].unsqueeze(1).to_broadcast([128, 4, 128]))
        tiles = []
        for p, (window, d) in enumerate(PATTERNS):
            nb = (T // d) // 128
            for r in range(d):
                for n in range(nb):
                    start = 128 * n * d + r
                    tiles.append((p, d, n, start, start + 127 * d + 1))

        def s1a(i):
            p, d, n, start, stop_ = tiles[i]
            b0 = start // 512
            b1 = (stop_ - 1) // 512
            hbs = [hT_b[b] for b in range(b0, b1 + 1)]
            rt = rot[i % 3]
            pc0 = pc0b[i % 2]
            k.dma('sp', rt[:], RO(rot_in[start:stop_:d, :]))
            for kc in range(KC):
                lhs = TV(hbs[0], hT.t[:, kc, start:stop_:d], tuple(hbs[1:]))
                k.mm(pc0[:], lhs, wc[:, kc, p * 768:p * 768 + 512], start=(kc == 0), stop=(kc == KC - 1))
            yield
            for kc in range(KC):
                lhs = TV(hbs[0], hT.t[:, kc, start:stop_:d], tuple(hbs[1:]))
                k.mm(pc1[:, 0:256], lhs, wc[:, kc, p * 768 + 512:p * 768 + 768], start=(kc == 0), stop=(kc == KC - 1))
            k.act(sqt[i % 2][:], pc0[:], AF.Square)
            yield
            k.reduce('dve', ss[i % 2][:], sqt[i % 2].v(sqt[i % 2].t[:, :].rearrange("p (g e) -> p g e", g=8)), ALU.add)
            k.copy('act', v1[i % 4][:, :, 0:64], pc1.v(pc1.t[:, 0:256].rearrange("p (h e) -> p h e", h=4)))
            yield
            k.act(rs[i % 2][:], ss[i % 2][:], AF.Ln, bias=epsc[:, 0:1], scale=1.0 / 64)
            k.act(ri[i % 2][:], rs[i % 2][:], AF.Exp, scale=-0.5)
            yield

        def s1b(i):
            p, d, n, start, stop_ = tiles[i]
            rt = rot[i % 3]
            pc0 = pc0b[i % 2]
            k.tt('dve', qkf[:], pc0.v(pc0.t[:, :].rearrange("p (g e) -> p g e", g=8)),
                 ri[i % 2].v(ri[i % 2].t[:, :].unsqueeze(2).to_broadcast([128, 8, 64])), ALU.mult)
            yield
            k.tt('dve', qkg[:], qkf[:], nw[:], ALU.mult)
            yield
            C2 = rt.v(rt.t[:, 0:16].unsqueeze(1).to_broadcast([128, 8, 16]))
            Sa = rt.v(rt.t[:, 16:24].unsqueeze(1).to_broadcast([128, 8, 8]))
            Sb = rt.v(rt.t[:, 24:32].unsqueeze(1).to_broadcast([128, 8, 8]))
            k.tt('dve', ra[:], qkg[:, :, 0:16], C2, ALU.mult)
            k.tt('dve', rb[:, :, 0:8], qkg[:, :, 8:16], Sa, ALU.mult)
            k.copy('pool', qkb[:, :, 16:64], qkg[:, :, 16:64])
            yield
            k.tt('dve', rb[:, :, 8:16], qkg[:, :, 0:8], Sb, ALU.mult)
            k.tt('dve', qkb[:, :, 0:16], ra[:], rb[:], ALU.add)
            yield
            for j in range(8):
                k.tr(ptr[0:64, j, :], qkb[:, j, :], identb[:])
            yield
            k.copy('act', qT[i % 2][:], ptr[0:64, 0:4, :])
            k.copy('act', kT[i % 3][:], ptr[0:64, 4:8, :])
            yield

        def s2(i):
            p, d, n, start, stop_ = tiles[i]
            cur = i % 3
            prev = (i - 1) % 3
            vc = i % 4
            vp = (i - 1) % 4
            q_ = qT[i % 2]
            for h in range(4):
                k.mm(pS0[:, h, :], kT[cur][:, h, :], q_[:, h, :])
            yield
            k.act(P0[:], pS0[:], AF.Exp, scale=0.125)
            if n > 0:
                for h in range(4):
                    k.mm(pS1[:, h, :], kT[prev][:, h, :], q_[:, h, :])
            yield
            k.tt('dve', P0m[:], P0[:], mC, ALU.mult)
            if n > 0:
                k.act(P1[:], pS1[:], AF.Exp, scale=0.125)
                yield
                k.tt('dve', P1m[:], P1[:], mP, ALU.mult)
            yield
            for h in range(4):
                if n > 0:
                    k.mm(pO[:, h, 0:65], P1m[:, h, :], v1[vp][:, h, 0:65], start=True, stop=False)
                k.mm(pO[:, h, 0:65], P0m[:, h, :], v1[vc][:, h, 0:65], start=(n == 0), stop=True)
            yield
            o_ = nlo[i % 2]
            k.copy('act', o_[:], pO[:, :, 0:65])
            yield
            k.dma('sp', nl_t[p].v(nl_d[p][start:stop_:d, :].rearrange("p (h e) -> p h e", h=4)), o_[:])
            yield

        def rr(gens):
            gens = list(gens)
            while gens:
                for g in list(gens):
                    try:
                        next(g)
                    except StopIteration:
                        gens.remove(g)

        NTL = len(tiles)
        for i in range(NTL + 2):
            g = []
            if 0 <= i - 2 < NTL:
                g.append(s2(i - 2))
            if 0 <= i - 1 < NTL:
                g.append(s1b(i - 1))
            if i < NTL:
                g.append(s1a(i))
            rr(g)
        k.end_phase()


def phase_A(k, l, T, hT, hT_b, RO, w_in, convw_in, a_log, dt_b, onw_in, cst, identb, epsc, yT_t, yT_d):
    NT = T // 128
    NB = T // 512
    ident = cst[:, 0, :]
    Um = cst[:, 1, :]
    SLm = cst[:, 2, :]
    ones = cst[:, 3, :]
    maskS = cst[:, 4, :]
    maskUi = cst[:, 5, :]
    SCALE = 128.0 ** -0.5
    with ExitStack() as pes:
        k.begin_phase()
        wa = k.sb("aw", [128, KC, 2056], BF16, pes)
        k.load_w(wa, w_in[l, :, 0:2056], KC, 2056, RO)
        cw = k.sb("acw", [128, 12, 4], F32, pes)
        k.dma('sp', cw[:], RO(convw_in[l]))
        alog = k.sb("aalog", [128, 4], F32, pes)
        dtb = k.sb("adtb", [128, 4], F32, pes)
        nea = k.sb("anea", [128, 4], F32, pes)
        onw = k.sb("aonw", [128, 128], F32, pes)
        k.dma('sp', alog[:], RO(a_log[l].partition_broadcast(128)))
        k.dma('sp', dtb[:], RO(dt_b[l].partition_broadcast(128)))
        k.dma('sp', onw[:], RO(onw_in[l].partition_broadcast(128)))
        k.act(nea[:], alog[:], AF.Exp)
        k.ts('dve', nea[:], nea[:], -1.0, None, ALU.mult)
        halo = k.sb("ahalo", [128, 12, 3], F32, pes)
        k.memset('dve', halo[:], 0.0)
        raw3 = [k.sb(f"araw{i}", [128, 515], F32, pes) for i in range(3)]
        acc3 = [k.sb(f"aacc{i}", [128, 512], F32, pes) for i in range(3)]
        qkv = k.sb("aqkv", [128, 12, 512], F32, pes)
        S = [[k.sb(f"aS{h}{i}", [128, 128], F32, pes) for i in range(2)] for h in range(4)]
        for h in range(4):
            k.memset('dve', S[h][0][:], 0.0)
        szb = [k.sb(f"asz{i}", [128, 512], F32, pes) for i in range(3)]
        smb = [k.sb(f"asm{i}", [128, 64], F32, pes) for i in range(2)]
        oallb = [k.sb(f"aoall{i}", [128, 4, 128], F32, pes) for i in range(2)]
        osq = k.sb("aosq", [128, 128], F32, pes)
        oss = k.sb("aoss", [128, 4], F32, pes)
        ors = k.sb("aors", [128, 8], F32, pes)
        ot = k.sb("aot", [128, 128], F32, pes)
        ya = k.sb("aya", [128, 4, 128], F32, pes)
        yaT = [k.sb(f"ayaT{i}", [128, 4, 128], BF16, pes) for i in range(2)]
        names = ["kbg", "kdec", "vb", "GU", "eD", "eDT", "tA", "N", "NT", "tQ", "aqk", "TTa", "TTb",
                 "Pa", "PTa", "Pb", "PTb", "u", "wT", "vnew", "o2"]
        W = [{n: k.sb(f"a{n}{s}", [128, 128], F32, pes) for n in names} for s in range(4)]
        BK = [k.ps(f"abk{i}", [128, 512], F32, pes) for i in range(8)]
        pq = [BK[0], BK[1]]
        pz = BK[6]
        pm = BK[7]
        pty = BK[7]

        def sc(j):
            return sm[:, j:j + 1]

        pwi = [0]

        def nxt():
            t = pw[pwi[0] % 3]
            s = (pwi[0] // 3) % 4
            pwi[0] += 1
            return t[:, s, :]

        it = 0
        for blk in range(NB):
            hb = hT_b[blk]
            tok = slice(blk * 512, (blk + 1) * 512)
            def conv_gen(fc, n_):
                j = n_ % 2
                rw = raw3[n_ % 3]
                ac = acc3[n_ % 3]
                for kc in range(KC):
                    k.mm(pq[j][:], wa[:, kc, fc * 128:(fc + 1) * 128], hb.v(hT.t[:, kc, tok]), start=(kc == 0), stop=(kc == KC - 1))
                k.copy('pool', rw[:, 0:3], halo[:, fc, :])
                k.copy('act', rw[:, 3:515], pq[j][:])
                k.copy('pool', halo[:, fc, :], rw[:, 512:515])
                yield
                k.ts('dve', ac[:], rw[:, 0:512], cw[:, fc, 0:1], None, ALU.mult)
                k.stt('dve', ac[:], rw[:, 1:513], cw[:, fc, 1:2], ac[:], ALU.mult, ALU.add)
                k.stt('dve', ac[:], rw[:, 2:514], cw[:, fc, 2:3], ac[:], ALU.mult, ALU.add)
                k.stt('dve', ac[:], rw[:, 3:515], cw[:, fc, 3:4], ac[:], ALU.mult, ALU.add)
                yield
                k.act(qkv[:, fc, :], ac[:], AF.Silu)
                yield

            def l2_gen(fc, n_):
                j = n_ % 2
                ac = acc3[n_ % 3]
                rw = raw3[n_ % 3]
                k.act(ac[:], qkv[:, fc, :], AF.Square)
                yield
                k.mm(pq[j][:], ones, ac[:])
                yield
                k.act(rw[:, 0:512], pq[j][:], AF.Ln, bias=epsc[:, 0:1], scale=1.0)
                k.act(ac[:], rw[:, 0:512], AF.Exp, scale=-0.5)
                yield
                k.tt('dve', qkv[:, fc, :], qkv[:, fc, :], ac[:], ALU.mult)
                yield

            def stagger(mk, n):
                act_ = []
                i_ = 0
                while act_ or i_ < n:
                    for g in list(act_):
                        try:
                            next(g)
                        except StopIteration:
                            act_.remove(g)
                    if i_ < n:
                        g = mk(i_)
                        next(g)
                        act_.append(g)
                        i_ += 1

            stagger(lambda i_: conv_gen(i_, i_), 12)
            stagger(lambda i_: l2_gen(i_, i_), 8)
            def pre_gen(tt):
                sm = smb[tt % 2]
                sz = szb[tt % 3]
                hts = slice(tt * 128, (tt + 1) * 128)
                for kc in range(KC):
                    k.mm(pz[:], hb.v(hT.t[:, kc, hts]), wa[:, kc, 1536:2048], start=(kc == 0), stop=(kc == KC - 1))
                yield
                for kc in range(KC):
                    k.mm(pm[:, 0:8], hb.v(hT.t[:, kc, hts]), wa[:, kc, 2048:2056], start=(kc == 0), stop=(kc == KC - 1))
                k.act(sz[:], pz[:], AF.Silu)
                yield
                k.tt('dve', sm[:, 0:4], pm[:, 0:4], dtb[:], ALU.add)
                k.act(sm[:, 16:20], pm[:, 4:8], AF.Exp, scale=-1.0)
                yield
                k.act(sm[:, 4:8], sm[:, 0:4], AF.Exp)
                k.ts('dve', sm[:, 16:20], sm[:, 16:20], 1.0, None, ALU.add)
                yield
                k.act(sm[:, 8:12], sm[:, 4:8], AF.Ln, bias=1.0)
                k.recip(sm[:, 20:24], sm[:, 16:20])
                yield
                k.tt('dve', sm[:, 12:16], sm[:, 8:12], nea[:], ALU.mult)
                k.ts('dve', sm[:, 24:28], sm[:, 20:24], -1.0, None, ALU.mult)
                yield
                k.mm(pm[:, 8:12], Um, sm[:, 12:16])
                k.mm(pm[:, 12:16], ones, sm[:, 12:16])
                yield
                k.copy('dve', sm[:, 28:32], pm[:, 8:12])
                k.copy('dve', sm[:, 36:40], pm[:, 12:16])
                yield
                k.act(sm[:, 32:36], sm[:, 28:32], AF.Exp)
                k.act(sm[:, 40:44], sm[:, 36:40], AF.Exp)
                k.tt('dve', sm[:, 44:48], sm[:, 36:40], sm[:, 28:32], ALU.subtract)
                yield
                k.act(sm[:, 48:52], sm[:, 44:48], AF.Exp)
                k.tt('dve', sm[:, 52:56], sm[:, 20:24], sm[:, 32:36], ALU.mult)
                k.ts('dve', sm[:, 56:60], sm[:, 32:36], SCALE, None, ALU.mult)
                yield

            def epi_gen(tt):
                oall = oallb[tt % 2]
                sz = szb[tt % 3]
                k.memset('dve', oss[:], 0.0)
                for h in range(4):
                    k.act(osq[:], oall[:, h, :], AF.Square, accum=oss[:, h:h + 1])
                    yield
                k.act(ors[:, 0:4], oss[:], AF.Ln, bias=epsc[:, 0:1], scale=1.0 / 128)
                k.act(ors[:, 4:8], ors[:, 0:4], AF.Exp, scale=-0.5)
                yield
                for h in range(4):
                    k.stt('dve', ot[:], oall[:, h, :], ors[:, 4 + h:5 + h], onw[:], ALU.mult, ALU.mult)
                    k.tt('dve', ya[:, h, :], ot[:], sz[:, h * 128:(h + 1) * 128], ALU.mult)
                    yield
                for h in range(4):
                    k.tr(pty[:, h * 128:(h + 1) * 128], ya[:, h, :], ident)
                yield
                i = tt % 2
                k.copy('act', yaT[i][:], pty.v(pty.t[:, :].rearrange("p (h t) -> p h t", h=4)))
                yield
                k.dma('sp', yT_t[tt].v(yT_d[tt][:, 0:512].rearrange("p (c t) -> p c t", c=4)), yaT[i][:])

            def head_gen(h, tt, cs, sm, oall):
                w = W[h]

                def sc(j):
                    return sm[:, j:j + 1]
                qT_ = qkv[:, h, cs]
                kT_ = qkv[:, 4 + h, cs]
                vT_ = qkv[:, 8 + h, cs]
                Sold = S[h][tt % 2]
                Snew = S[h][(tt + 1) % 2]
                cnt = [0]

                def nx():
                    c = cnt[0]
                    cnt[0] += 1
                    bk = BK[2 * h + (c % 2)]
                    o = ((c // 2) % 4) * 128
                    return bk[:, o:o + 128]
                p = nx()

                def post(p=p):
                    k.ts('dve', w["kbg"][:], p, sc(52 + h), None, ALU.mult)
                    k.amul(w["kdec"][:], p, sc(48 + h))
                yield [('tr', p, kT_, ident)], post
                p = nx()

                def post(p=p):
                    k.amul(w["vb"][:], p, sc(20 + h))
                    k.ts('dve', w["GU"][:], Um, sc(12 + h), None, ALU.mult)
                yield [('tr', p, vT_, ident)], post
                p = nx()

                def post(p=p):
                    k.act(w["eD"][:], p, AF.Exp)
                    k.tt('pool', w["tA"][:], w["eD"][:], maskS, ALU.mult)
                yield [('mm', p, w["GU"][:], SLm)], post
                p = nx()

                def post(p=p):
                    k.act(w["eDT"][:], p, AF.Exp)
                    k.tt('pool', w["tQ"][:], w["eDT"][:], maskUi, ALU.mult)
                yield [('mm', p, SLm, w["GU"][:])], post
                p = nx()

                def post(p=p):
                    k.stt('dve', w["N"][:], p, sc(24 + h), w["tA"][:], ALU.mult, ALU.mult)
                yield [('mm', p, kT_, kT_)], post
                p = nx()

                def post(p=p):
                    k.copy('act', w["NT"][:], p)
                    k.tt('pool', w["TTa"][:], w["NT"][:], ident, ALU.add)
                yield [('tr', p, w["N"][:], ident)], post
                p = nx()

                def post(p=p):
                    k.stt('dve', w["aqk"][:], p, SCALE, w["tQ"][:], ALU.mult, ALU.mult)
                yield [('mm', p, kT_, qT_)], post
                TT, TTo = w["TTa"], w["TTb"]
                P_, PT_ = w["N"], w["NT"]
                Pn, PTn = w["Pa"], w["PTa"]
                for s_ in range(6):
                    p = nx()
                    ops = [('mm', p, PT_[:], P_[:])]
                    p2 = None
                    if s_ < 5:
                        p2 = nx()
                        ops.append(('mm', p2, P_[:], PT_[:]))

                    def post(p=p, p2=p2, Pn=Pn, PTn=PTn, s_=s_):
                        k.copy('act', Pn[:], p)
                        if p2 is not None:
                            k.copy('dve', PTn[:], p2)
                    yield ops, post
                    p3 = nx()

                    def post(p3=p3, TT=TT, TTo=TTo):
                        k.tt('dve', TTo[:], p3, TT[:], ALU.add)
                    yield [('mm', p3, Pn[:], TT[:])], post
                    TT, TTo = TTo, TT
                    P_, PT_ = Pn, PTn
                    if s_ % 2 == 0:
                        Pn, PTn = w["Pb"], w["PTb"]
                    else:
                        Pn, PTn = w["Pa"], w["PTa"]
                p = nx()
                p2 = nx()

                def post(p=p, p2=p2):
                    k.copy('act', w["u"][:], p)
                    k.copy('dve', w["wT"][:], p2)
                yield [('mm', p, TT[:], w["vb"][:]), ('mm', p2, w["kbg"][:], TT[:])], post
                p = nx()
                p1 = nx()

                def post(p=p):
                    k.stt('dve', w["vnew"][:], p, -1.0, w["u"][:], ALU.mult, ALU.add)
                yield [('mm', p, w["wT"][:], Sold[:]), ('mm', p1, qT_, Sold[:])], post
                p2 = nx()
                p3 = nx()

                def post(p1=p1, p2=p2, p3=p3):
                    k.copy('act', w["o2"][:], p2)
                    k.stt('dve', Snew[:], Sold[:], sc(40 + h), p3, ALU.mult, ALU.add)
                    k.stt('dve', oall[:, h, :], p1, sc(56 + h), w["o2"][:], ALU.mult, ALU.add)
                yield [('mm', p2, w["aqk"][:], w["vnew"][:]), ('mm', p3, w["kdec"][:], w["vnew"][:])], post


            def run_tile(tt, side):
                cs = slice((tt % 4) * 128, (tt % 4 + 1) * 128)
                groups = [[head_gen(h, tt, cs, smb[tt % 2], oallb[tt % 2]) for h in (0, 1)],
                          [head_gen(h, tt, cs, smb[tt % 2], oallb[tt % 2]) for h in (2, 3)]]
                side = list(side)
                alive = True
                while alive:
                    alive = False
                    for grp in groups:
                        items = []
                        for g in list(grp):
                            try:
                                items.append(next(g))
                            except StopIteration:
                                grp.remove(g)
                        if not items:
                            continue
                        alive = True
                        k.pe_batch([d_ for ops, _ in items for d_ in ops])
                        for _, post in items:
                            post()
                    for g in list(side):
                        try:
                            next(g)
                        except StopIteration:
                            side.remove(g)
                for g in side:
                    for _ in g:
                        pass

            t0_ = blk * 4
            for _ in pre_gen(t0_):
                pass
            for t4 in range(4):
                tt = t0_ + t4
                side = []
                if t4 + 1 < 4:
                    side.append(pre_gen(tt + 1))
                if t4 > 0:
                    side.append(epi_gen(tt - 1))
                run_tile(tt, side)
            for _ in epi_gen(t0_ + 3):
                pass
        k.end_phase()


def _consts():
    i = np.arange(128)
    c = np.zeros((128, 8, 128), np.float32)
    c[:, 0, :] = np.eye(128)
    c[:, 1, :] = (i[:, None] <= i[None, :])
    c[:, 2, :] = (i[:, None] > i[None, :])
    c[:, 3, :] = 1.0
    c[:, 4, :] = (i[:, None] > i[None, :])
    c[:, 5, :] = (i[:, None] <= i[None, :])
    c[:, 6, :] = (i[:, None] >= i[None, :])
    return c


def _rot(T):
    inv = (np.float32(ROPE_THETA) ** (-np.arange(0, 16, 2, dtype=np.float32) / np.float32(16))).astype(np.float32)
    ang = (np.arange(T, dtype=np.float32)[:, None] * inv[None, :]).astype(np.float32)
    cos, sin = np.cos(ang).astype(np.float32), np.sin(ang).astype(np.float32)
    return np.concatenate([cos, cos, -sin, sin], axis=1).astype(np.float32)


def make_in_maps(inputs, T, depth, ncores):
    f = lambda a: np.ascontiguousarray(np.asarray(a, dtype=np.float32))
    shared = {
        "w_mod": f(inputs["w_mod"]), "b_mod": f(inputs["b_mod"]),
        "mix_norm_w": f(np.asarray(inputs["mix_norm_w"]).reshape(depth, KC, 128).transpose(0, 2, 1)),
        "ffn_norm_w": f(np.asarray(inputs["ffn_norm_w"]).reshape(depth, KC, 128).transpose(0, 2, 1)),
        "w_in": f(inputs["w_in"]), "w_out": f(inputs["w_out"]),
        "dn_conv_w": f(np.asarray(inputs["dn_conv_w"]).transpose(0, 2, 1).reshape(depth, 12, 128, 4).transpose(0, 2, 1, 3)),
        "dn_a_log": f(inputs["dn_a_log"]), "dn_dt_bias": f(inputs["dn_dt_bias"]),
        "dn_out_norm_w": f(inputs["dn_out_norm_w"]),
        "gm_ln_g": f(inputs["gm_ln_g"]), "gm_ln_b": f(inputs["gm_ln_b"]),
        "gm_w_sT": f(np.asarray(inputs["gm_w_s"]).transpose(0, 3, 1, 2)),
        "gm_b_s": f(np.asarray(inputs["gm_b_s"]).transpose(0, 2, 1)),
        "sw_q_norm_w": f(inputs["sw_q_norm_w"]), "sw_k_norm_w": f(inputs["sw_k_norm_w"]),
        "w_ffn_in": f(inputs["w_ffn_in"]), "w_ffn_out": f(inputs["w_ffn_out"]),
        "consts": _consts(), "rot": _rot(T),
    }
    x = np.asarray(inputs["x"], dtype=np.float32)
    c = np.asarray(inputs["c"], dtype=np.float32)
    maps = []
    for b in range(ncores):
        m = dict(shared)
        m["x"] = np.ascontiguousarray(x[b])
        m["c"] = np.ascontiguousarray(c[b].reshape(KC, 128).T)
        maps.append(m)
    return maps


def kernel(**inputs):
    nc = build(SEQ, DEPTH, False)
    maps = make_in_maps(inputs, SEQ, DEPTH, NCORES)
    res = run_bass_kernel_spmd(nc, maps, core_ids=list(range(NCORES)))
    return np.stack([np.asarray(r["out"], dtype=np.float32) for r in res.results], axis=0)
```

```python
import numpy as np
from contextlib import ExitStack
import concourse.bass as bass
import concourse.mybir as mybir
from concourse.bass_utils import run_bass_kernel_spmd

F32 = mybir.dt.float32
BF16 = mybir.dt.bfloat16
AF = mybir.ActivationFunctionType
ALU = mybir.AluOpType
AX = mybir.AxisListType

D = 1024
KC = 8
DEPTH = 4
SEQ = 4096
NCORES = 8
EPS = 1e-6
INW = 4872
FH = 2816
NFC = 22
SEM_ROLL = 15000
PATTERNS = ((128, 1), (512, 4), (2048, 16))
CUT = 99
F32R = False
ROPE_THETA = 500000.0


class Tile:
    def __init__(self, k, t, name, dram=False):
        self.k, self.t, self.name, self.dram = k, t, name, dram
        self.writes = {}
        self.reads = {}
        self.dsem = None
        self.dcnt = 0

    def __getitem__(self, idx):
        return TV(self, self.t[idx])

    def v(self, ap):
        return TV(self, ap)


class TV:
    def __init__(self, tile, ap, extra=()):
        self.tile, self.ap, self.extra = tile, ap, extra


def _m(d, s):
    for sem, v in s.items():
        if d.get(sem, 0) < v:
            d[sem] = v


class K:
    def __init__(self, nc, es):
        self.nc, self.es = nc, es
        self.engs = {'pe': nc.tensor, 'act': nc.scalar, 'dve': nc.vector, 'pool': nc.gpsimd, 'sp': nc.sync}
        self.cnt = {}
        self.sem = {}
        self.own = {e: set() for e in self.engs}
        self.waited = {e: {} for e in self.engs}
        self.nsem = 0
        for e in self.engs:
            self._newsem(e)
        self.dma_tiles = []
        self.dpool = {'sp': [], 'pool': []}
        self.scope = None
        self.ninst = {e: 0 for e in self.engs}

    def _newsem(self, e):
        s = self.es.enter_context(self.nc.semaphore(f"s_{e}_{self.nsem}"))
        self.nsem += 1
        self.sem[e] = s
        self.cnt[e] = 0
        self.own[e].add(s)

    def sb(self, name, shape, dt, es=None):
        self.nsem += 1
        name = f"{name}_{self.nsem}"
        t = (es or self.es).enter_context(self.nc.sbuf_tensor(name, list(shape), dt))
        tl = Tile(self, t, name)
        if es is not None and self.scope is not None:
            self.scope.append(tl)
        return tl

    def begin_phase(self):
        self.scope = []

    def end_phase(self):
        self.barrier()
        for tl in self.scope:
            if tl.dsem is not None:
                if 16 * tl.dcnt < 2000:
                    self.dpool[tl.dq].append((tl.dsem, tl.dcnt))
                else:
                    self.retired = getattr(self, 'retired', 0) + 1
                self.dma_tiles.remove(tl)
                tl.dsem = None
        self.scope = None

    def ps(self, name, shape, dt, es=None):
        self.nsem += 1
        name = f"{name}_{self.nsem}"
        t = (es or self.es).enter_context(self.nc.psum_tensor(name, list(shape), dt))
        tl = Tile(self, t, name)
        tl.psum = True
        return tl

    def dram(self, ap, name):
        return Tile(self, ap, name, dram=True)

    @staticmethod
    def _tiles(tvs):
        out = []
        for tv in tvs:
            out.append(tv.tile)
            out.extend(tv.extra)
        return out

    def _waits(self, e, reads, writes):
        need = {}
        for t in self._tiles(reads):
            _m(need, t.writes)
            if getattr(t, 'psum', False):
                for sem, v in t.reads.items():
                    if sem not in self.own[e] and need.get(sem, 0) < v:
                        need[sem] = v
        for t in self._tiles(writes):
            _m(need, t.writes)
            _m(need, t.reads)
        eng = self.engs[e]
        w = self.waited[e]
        for sem, val in need.items():
            if e == 'pe' and sem in self.own['pe']:
                continue
            if w.get(sem, 0) < val:
                eng.wait_ge(sem, val)
                w[sem] = val
                self.ninst[e] += 1

    def _done(self, tok, reads, writes):
        sem, val = tok
        for t in self._tiles(reads):
            if t.reads.get(sem, 0) < val:
                t.reads[sem] = val
        for t in self._tiles(writes):
            if t.dram:
                if t.writes.get(sem, 0) < val:
                    t.writes[sem] = val
            else:
                t.writes = {sem: val}
            t.reads = {}

    def op(self, e, emit, reads, writes):
        self._waits(e, reads, writes)
        ins = emit(self.engs[e])
        if self.cnt[e] >= SEM_ROLL:
            self._newsem(e)
        self.cnt[e] += 1
        ins.then_inc(self.sem[e], 1)
        self.ninst[e] += 1
        tok = (self.sem[e], self.cnt[e])
        self._done(tok, reads, writes)
        return tok

    def dma(self, q, out, in_, **kw):
        own = out.tile if not out.tile.dram else in_.tile
        assert not own.dram
        if own.dsem is None:
            own.dq = q
            if self.dpool[q]:
                own.dsem, own.dcnt = self.dpool[q].pop(0)
            else:
                own.dsem = self.es.enter_context(self.nc.semaphore(f"d_{self.nsem}"))
                self.nds = getattr(self, 'nds', 0) + 1
                self.nsem += 1
            self.dma_tiles.append(own)
        assert own.dq == q, (own.name, own.dq, q)
        self._waits(q, [in_], [out])
        ins = self.engs[q].dma_start(out=out.ap, in_=in_.ap, **kw)
        own.dcnt += 1
        ins.then_inc(own.dsem, 16)
        self.ninst[q] += 1
        tok = (own.dsem, 16 * own.dcnt)
        self._done(tok, [in_], [out])
        return tok

    def load_w(self, dst, src2d, nk, cols, RO, q='pool', cstep=4096):
        for kc in range(nk):
            for c0 in range(0, cols, cstep):
                c1 = min(cols, c0 + cstep)
                self.dma(q, dst[:, kc, c0:c1], RO(src2d[kc * 128:(kc + 1) * 128, c0:c1]))

    def barrier(self):
        need = {}
        for e in self.engs:
            if self.cnt[e] > 0:
                need[self.sem[e]] = self.cnt[e]
        for t in self.dma_tiles:
            need[t.dsem] = 16 * t.dcnt
        for e in self.engs:
            w = self.waited[e]
            for sem, val in need.items():
                if e == 'pe' and sem in self.own['pe']:
                    continue
                if w.get(sem, 0) < val:
                    self.engs[e].wait_ge(sem, val)
                    w[sem] = val
                    self.ninst[e] += 1

    def mm(self, out, lhsT, rhs, start=True, stop=True):
        la, ra = lhsT.ap, rhs.ap
        if F32R and la.dtype == F32 and ra.dtype == F32:
            la = la.bitcast(mybir.dt.float32r)
            ra = ra.bitcast(mybir.dt.float32r)
        return self.op('pe', lambda g: g.matmul(out.ap, la, ra, start=start, stop=stop),
                       [lhsT, rhs], [out])

    def pe_batch(self, descrs):
        rd, wr = [], []
        for d in descrs:
            wr.append(d[1])
            rd.extend([d[2], d[3]])
        self._waits('pe', rd, wr)
        for d in descrs:
            if d[0] == 'mm':
                self.mm(d[1], d[2], d[3])
            else:
                self.tr(d[1], d[2], d[3])

    def tr(self, out, in_, ident):
        return self.op('pe', lambda g: g.transpose(out.ap, in_.ap, ident.ap), [in_, ident], [out])

    def act(self, out, in_, func, bias=None, scale=None, accum=None):
        kw = {}
        rd = [in_]
        wr = [out]
        if bias is not None:
            if isinstance(bias, TV):
                kw['bias'] = bias.ap
                rd.append(bias)
            else:
                kw['bias'] = bias
        if scale is not None:
            if isinstance(scale, TV):
                kw['scale'] = scale.ap
                rd.append(scale)
            else:
                kw['scale'] = scale
        if accum is not None:
            kw['accum_out'] = accum.ap
            wr.append(accum)
        return self.op('act', lambda g: g.activation(out=out.ap, in_=in_.ap, func=func, **kw), rd, wr)

    def tt(self, e, out, in0, in1, op):
        return self.op(e, lambda g: g.tensor_tensor(out=out.ap, in0=in0.ap, in1=in1.ap, op=op), [in0, in1], [out])

    def ts(self, e, out, in0, s1, s2, op0, op1=None):
        rd = [in0]
        a1, a2 = s1, s2
        if isinstance(s1, TV):
            rd.append(s1)
            a1 = s1.ap
        if isinstance(s2, TV):
            rd.append(s2)
            a2 = s2.ap
        if op1 is None:
            return self.op(e, lambda g: g.tensor_scalar(out=out.ap, in0=in0.ap, scalar1=a1, scalar2=None, op0=op0),
                           rd, [out])
        return self.op(e, lambda g: g.tensor_scalar(out=out.ap, in0=in0.ap, scalar1=a1, scalar2=a2, op0=op0, op1=op1),
                       rd, [out])

    def stt(self, e, out, in0, s, in1, op0, op1):
        rd = [in0, in1]
        a = s
        if isinstance(s, TV):
            rd.append(s)
            a = s.ap
        return self.op(e, lambda g: g.scalar_tensor_tensor(out=out.ap, in0=in0.ap, scalar=a, in1=in1.ap,
                                                           op0=op0, op1=op1), rd, [out])

    def copy(self, e, out, in_):
        if e == 'act':
            return self.op(e, lambda g: g.copy(out=out.ap, in_=in_.ap), [in_], [out])
        return self.op(e, lambda g: g.tensor_copy(out=out.ap, in_=in_.ap), [in_], [out])

    def amul(self, out, in_, m):
        return self.op('act', lambda g: g.mul(out=out.ap, in_=in_.ap, mul=m.ap), [in_, m], [out])

    def memset(self, e, out, val):
        return self.op(e, lambda g: g.memset(out.ap, val), [], [out])

    def recip(self, out, in_):
        return self.op('dve', lambda g: g.reciprocal(out=out.ap, in_=in_.ap), [in_], [out])

    def reduce(self, e, out, in_, op):
        return self.op(e, lambda g: g.tensor_reduce(out=out.ap, in_=in_.ap, axis=AX.X, op=op), [in_], [out])


def build(T=SEQ, depth=DEPTH, debug=False, phases='MNABCDF'):
    NT = T // 128
    NB = T // 512
    nc = bass.Bass("TRN2", target_bir_lowering=False)
    es = ExitStack()

    def din(name, shape):
        return nc.dram_tensor(name, list(shape), F32, kind="ExternalInput").ap()

    x_in = din("x", [T, D])
    c_in = din("c", [128, KC])
    w_mod = din("w_mod", [depth, D, 6 * D])
    b_mod = din("b_mod", [depth, 6 * D])
    mixw = din("mix_norm_w", [depth, 128, KC])
    ffnw = din("ffn_norm_w", [depth, 128, KC])
    w_in = din("w_in", [depth, D, INW])
    w_out = din("w_out", [depth, D, D])
    convw = din("dn_conv_w", [depth, 128, 12, 4])
    a_log = din("dn_a_log", [depth, 4])
    dt_b = din("dn_dt_bias", [depth, 4])
    onw_in = din("dn_out_norm_w", [depth, 128])
    lng_in = din("gm_ln_g", [depth, 256])
    lnb_in = din("gm_ln_b", [depth, 256])
    wsT_in = din("gm_w_sT", [depth, 128, 4, 128])
    bs_in = din("gm_b_s", [depth, 128, 4])
    qnw_in = din("sw_q_norm_w", [depth, 64])
    knw_in = din("sw_k_norm_w", [depth, 64])
    wf_in = din("w_ffn_in", [depth, D, 2 * FH])
    wf_out = din("w_ffn_out", [depth, FH, D])
    cst_in = din("consts", [128, 8, 128])
    rot_in = din("rot", [T, 32])
    out_d = nc.dram_tensor("out", [T, D], F32, kind="ExternalOutput").ap()
    skind = "ExternalOutput" if debug else "Internal"
    modrow_d = nc.dram_tensor("modrow", [depth, 6 * D], F32, kind=skind).ap()
    yT_d = nc.dram_tensor("yT", [NT, 128, 768], BF16, kind="Internal").ap()
    nl_d = [nc.dram_tensor(f"nl{p}", [T, 260], F32, kind=skind).ap() for p in range(3)]
    dbg_d = nc.dram_tensor("dbg", [T, D], F32, kind=skind).ap() if debug else None

    with es:
        k = K(nc, es)
        xin_t = [k.dram(x_in[i * 128:(i + 1) * 128, :], f"xin{i}") for i in range(NT)]
        out_t = [k.dram(out_d[i * 128:(i + 1) * 128, :], f"out{i}") for i in range(NT)]
        wts = k.dram(w_mod, "wts")
        modrow_t = k.dram(modrow_d, "modrow")
        yT_t = [k.dram(yT_d[i], f"yT{i}") for i in range(NT)]
        nl_t = [k.dram(nl_d[p], f"nl{p}") for p in range(3)]
        dbg_t = k.dram(dbg_d, "dbg") if debug else None

        def RO(ap):
            return wts.v(ap)

        hT = k.sb("hT", [128, KC, T], BF16)
        hT_b = [Tile(k, hT.t, f"hTb{b}") for b in range(NB)]
        cst = k.sb("cst", [128, 8, 128], F32)
        identb = k.sb("identb", [128, 128], BF16)
        maskCb = k.sb("maskCb", [128, 128], BF16)
        maskPb = k.sb("maskPb", [128, 128], BF16)
        epsc = k.sb("epsc", [128, 1], F32)
        k.dma('sp', cst[:], RO(cst_in))
        ident = cst[:, 0, :]
        Um = cst[:, 1, :]
        SLm = cst[:, 2, :]
        ones = cst[:, 3, :]
        maskS = cst[:, 4, :]
        maskUi = cst[:, 5, :]
        maskP = cst[:, 6, :]
        k.copy('dve', identb[:], ident)
        k.copy('dve', maskCb[:], maskUi)
        k.copy('dve', maskPb[:], maskP)
        k.memset('dve', epsc[:], EPS)

        a1 = k.sb("a1", [128, KC], F32)
        b1 = k.sb("b1", [128, KC], F32)
        a2 = k.sb("a2", [128, KC], F32)
        b2 = k.sb("b2", [128, KC], F32)
        g1bc = k.sb("g1bc", [128, D], F32)
        g2bc = k.sb("g2bc", [128, D], F32)
        nw1 = k.sb("nw1", [128, KC], F32)
        nw2 = k.sb("nw2", [128, KC], F32)
        mcol = k.sb("mcol", [128, 4, KC], F32)

        with ExitStack() as pes:
            k.begin_phase()
            c_sb = k.sb("c_sb", [128, KC], F32, pes)
            cact = k.sb("cact", [128, KC], F32, pes)
            wm = [k.sb(f"wm{i}", [128, KC, 512], F32, pes) for i in range(4)]
            bm = k.sb("bm", [1, 6 * D], F32, pes)
            mr = k.sb("mr", [1, 6 * D], F32, pes)
            pm = [k.ps(f"pm{i}", [128, 512], F32, pes) for i in range(2)]
            k.dma('sp', c_sb[:], RO(c_in))
            k.act(cact[:], c_sb[:], AF.Silu)
            it = 0
            for l in range(depth):
                k.dma('sp', bm[:], RO(b_mod[l:l + 1, :]))
                for ec in range(12):
                    w = wm[it % 4]
                    p = pm[it % 2]
                    q = 'sp' if it % 2 == 0 else 'pool'
                    k.dma(q, w[:], RO(w_mod[l, :, ec * 512:(ec + 1) * 512].rearrange("(kc p) e -> p kc e", p=128)))
                    for kc in range(KC):
                        k.mm(p[0:1, :], cact[:, kc:kc + 1], w[:, kc, :], start=(kc == 0), stop=(kc == KC - 1))
                    k.tt('dve', mr[0:1, ec * 512:(ec + 1) * 512], p[0:1, :], bm[0:1, ec * 512:(ec + 1) * 512], ALU.add)
                    it += 1
                k.dma('sp', modrow_t.v(modrow_d[l:l + 1, :]), mr[:])
            k.end_phase()

        def load_layer_params(l):
            for j, src in enumerate((0, 1, 3, 4)):
                k.dma('sp', mcol[:, j, :],
                      modrow_t.v(modrow_d[l, src * D:(src + 1) * D].rearrange("(kc p) -> p kc", p=128)),
                      allow_slow_non_contiguous=True)
            k.dma('sp', g1bc[:], modrow_t.v(modrow_d[l, 2 * D:3 * D].partition_broadcast(128)))
            k.dma('sp', g2bc[:], modrow_t.v(modrow_d[l, 5 * D:6 * D].partition_broadcast(128)))
            k.dma('sp', nw1[:], RO(mixw[l]))
            k.dma('sp', nw2[:], RO(ffnw[l]))
            k.stt('dve', a1[:], mcol[:, 1, :], 1.0, nw1[:], ALU.add, ALU.mult)
            k.copy('dve', b1[:], mcol[:, 0, :])
            k.stt('dve', a2[:], mcol[:, 3, :], 1.0, nw2[:], ALU.add, ALU.mult)
            k.copy('dve', b2[:], mcol[:, 2, :])

        def norm_gen(xt, tt, aa, bb, sq, ss, rs, xn, ptr):
            hb = hT_b[tt // 4]
            k.memset('dve', ss[:, 0:1], 0.0)
            k.act(sq[:], xt[:], AF.Square, accum=ss[:, 0:1])
            yield
            k.act(rs[:, 0:1], ss[:, 0:1], AF.Sqrt, bias=epsc[:, 0:1], scale=1.0 / D)
            k.recip(rs[:, 1:2], rs[:, 0:1])
            yield
            k.ts('dve', xn[:], xt[:], rs[:, 1:2], None, ALU.mult)
            yield
            for half in range(2):
                p = ptr[half]
                for j in range(4):
                    kc = half * 4 + j
                    k.tr(p[:, j, :], xn[:, kc * 128:(kc + 1) * 128], ident)
                yield
                for j in range(4):
                    kc = half * 4 + j
                    k.ts('dve', hb.v(hT.t[:, kc, tt * 128:(tt + 1) * 128]), p[:, j, :], aa[:, kc:kc + 1], bb[:, kc:kc + 1],
                         ALU.mult, ALU.add)
                    if j % 2 == 1:
                        yield

        def norm_to_hT(*a):
            for _ in norm_gen(*a):
                pass

        def rr(gens):
            gens = list(gens)
            while gens:
                for g in list(gens):
                    try:
                        next(g)
                    except StopIteration:
                        gens.remove(g)

        for l in range(depth):
            load_layer_params(l)
            xsrc = xin_t if l == 0 else out_t

            with ExitStack() as pes:
                k.begin_phase()
                xt = [k.sb(f"n1x{i}", [128, D], F32, pes) for i in range(2)]
                sq = k.sb("n1sq", [128, D], F32, pes)
                xn = [k.sb(f"n1xn{i}", [128, D], F32, pes) for i in range(2)]
                ss = [k.sb(f"n1ss{i}", [128, 1], F32, pes) for i in range(2)]
                rs = [k.sb(f"n1rs{i}", [128, 2], F32, pes) for i in range(2)]
                ptr = [[k.ps(f"n1p{i}{h}", [128, 4, 128], F32, pes) for h in range(2)] for i in range(2)]
                for tt in range(NT):
                    i = tt % 2
                    k.dma('sp', xt[i][:], xsrc[tt][:])
                    norm_to_hT(xt[i], tt, a1, b1, sq, ss[i], rs[i], xn[i], ptr[i])
                k.end_phase()

            if 'A' in phases:
                phase_A(k, l, T, hT, hT_b, RO, w_in, convw, a_log, dt_b, onw_in, cst, identb, epsc, yT_t, yT_d)
            if 'B' in phases:
                phase_B(k, l, T, hT, hT_b, RO, w_in, lng_in, lnb_in, wsT_in, bs_in, cst, identb, epsc, yT_t, yT_d)
            if 'C' in phases:
                phase_C(k, l, T, hT, hT_b, RO, w_in, qnw_in, knw_in, rot_in, identb, maskCb, maskPb, epsc, nl_t, nl_d)

            with ExitStack() as pes:
                if 'D' not in phases:
                    break
                k.begin_phase()
                wo = k.sb("wo", [128, KC, D], BF16, pes)
                k.load_w(wo, w_out[l], KC, D, RO)
                k.tt('dve', wo[:], wo[:], g1bc.v(g1bc.t[:, :].unsqueeze(1).to_broadcast([128, KC, D])), ALU.mult)
                nl = [[k.sb(f"dnl{i}{p}", [128, 4, 65], F32, pes) for p in range(3)] for i in range(2)]
                nsum = k.sb("dnsum", [128, 4, 65], F32, pes)
                rl = k.sb("drl", [128, 4], F32, pes)
                ycb = k.sb("dycb", [128, 4, 64], BF16, pes)
                yT = [k.sb(f"dyT{i}", [128, 8, 128], BF16, pes) for i in range(2)]
                xt = [k.sb(f"dx{i}", [128, D], F32, pes) for i in range(2)]
                xo = [k.sb(f"dxo{i}", [128, D], F32, pes) for i in range(2)]
                tmp = k.sb("dtmp", [128, D], F32, pes)
                sq = k.sb("dsq", [128, D], F32, pes)
                xn = k.sb("dxn", [128, D], F32, pes)
                ss = k.sb("dss", [128, 1], F32, pes)
                rs = k.sb("drs", [128, 2], F32, pes)
                ptc = k.ps("dptc", [128, 8, 128], BF16, pes)
                po = [k.ps(f"dpo{h}", [128, 512], F32, pes) for h in range(2)]
                ptr = [k.ps(f"dptr{h}", [128, 4, 128], F32, pes) for h in range(2)]
                def d1a(tt):
                    i = tt % 2
                    for p in range(3):
                        k.dma('sp', nl[i][p][:], nl_t[p].v(nl_d[p][tt * 128:(tt + 1) * 128, :].rearrange("p (h e) -> p h e", h=4)))
                    k.dma('sp', yT[i][:, 0:6, :], yT_t[tt].v(yT_d[tt].rearrange("p (c t) -> p c t", c=6)))
                    k.dma('sp', xt[i][:], xsrc[tt][:])
                    k.tt('dve', nsum[:], nl[i][0][:], nl[i][1][:], ALU.add)
                    yield
                    k.tt('dve', nsum[:], nsum[:], nl[i][2][:], ALU.add)
                    yield
                    k.recip(rl[:], nsum[:, :, 64])
                    yield
                    k.tt('dve', ycb[:], nsum[:, :, 0:64], rl.v(rl.t[:, :].unsqueeze(2).to_broadcast([128, 4, 64])), ALU.mult)
                    yield
                    for j in range(2):
                        k.tr(ptc[:, j, :], ycb.v(ycb.t[:, 2 * j:2 * j + 2, :].rearrange("p h e -> p (h e)")), identb[:])
                    yield
                    k.copy('act', yT[i][:, 6:8, :], ptc[:, 0:2, :])
                    yield

                def d1b(tt):
                    i = tt % 2
                    for half in range(2):
                        for c in range(8):
                            k.mm(po[half][:], yT[i][:, c, :], wo[:, c, half * 512:(half + 1) * 512],
                                 start=(c == 0), stop=(c == 7))
                        yield
                    for half in range(2):
                        sl = slice(half * 512, (half + 1) * 512)
                        k.tt('dve', xo[i][:, sl], po[half][:], xt[i][:, sl], ALU.add)
                        yield
                    k.dma('sp', out_t[tt][:], xo[i][:])
                    yield

                for it_ in range(NT + 2):
                    g = []
                    if 0 <= it_ - 2 < NT:
                        t2 = it_ - 2
                        g.append(norm_gen(xo[t2 % 2], t2, a2, b2, sq, ss, rs, xn, ptr))
                    if 0 <= it_ - 1 < NT:
                        g.append(d1b(it_ - 1))
                    if it_ < NT:
                        g.append(d1a(it_))
                    rr(g)
                k.end_phase()

            groups = [(0, 6), (6, 6), (12, 5), (17, 5)] if 'F' in phases else []
            if groups:
                with ExitStack() as pes:
                    k.begin_phase()
                    wg = [k.sb(f"fwg{i}", [128, KC, 768], BF16, pes) for i in range(2)]
                    wu = [k.sb(f"fwu{i}", [128, KC, 768], BF16, pes) for i in range(2)]
                    wo2 = [k.sb(f"fwo{i}", [128, 6, D], BF16, pes) for i in range(2)]
                    actT = [k.sb(f"fact{i}", [128, 6, 512], BF16, pes) for i in range(2)]
                    sg = [k.sb(f"fsg{i}", [128, 512], F32, pes) for i in range(2)]
                    xt = [k.sb(f"fx{i}", [128, D], F32, pes) for i in range(2)]
                    xo = [k.sb(f"fxo{i}", [128, D], F32, pes) for i in range(2)]
                    tmp = k.sb("ftmp", [128, D], F32, pes)
                    pg = [k.ps(f"fpg{i}", [128, 512], F32, pes) for i in range(2)]
                    pu = [k.ps(f"fpu{i}", [128, 512], F32, pes) for i in range(2)]
                    po = [[k.ps(f"fpo{i}{h}", [128, 512], F32, pes) for h in range(2)] for i in range(2)]

                    def loadw(gi):
                        f0, nf = groups[gi]
                        j = gi % 2
                        for kc in range(KC):
                            k.dma('pool', wg[j][:, kc, 0:nf * 128], RO(wf_in[l, kc * 128:(kc + 1) * 128, f0 * 128:(f0 + nf) * 128]))
                            k.dma('pool', wu[j][:, kc, 0:nf * 128], RO(wf_in[l, kc * 128:(kc + 1) * 128, FH + f0 * 128:FH + (f0 + nf) * 128]))
                        for c in range(nf):
                            k.dma('pool', wo2[j][:, c, :], RO(wf_out[l, (f0 + c) * 128:(f0 + c + 1) * 128, :]))

                    def scalew(gi):
                        f0, nf = groups[gi]
                        j = gi % 2
                        k.tt('dve', wo2[j][:, 0:nf, :], wo2[j][:, 0:nf, :],
                             g2bc.v(g2bc.t[:, :].unsqueeze(1).to_broadcast([128, nf, D])), ALU.mult)

                    loadw(0)
                    it = 0
                    xi = 0
                    ab = 0
                    for gi, (f0, nf) in enumerate(groups):
                        if gi + 1 < len(groups):
                            loadw(gi + 1)
                        scalew(gi)
                        wj = gi % 2
                        for blk in range(NB):
                            hb = hT_b[blk]
                            a = actT[ab % 2]
                            ab += 1
                            for fi in range(nf):
                                j = it % 2
                                it += 1
                                for kc in range(KC):
                                    k.mm(pg[j][:], wg[wj][:, kc, fi * 128:(fi + 1) * 128], hb.v(hT.t[:, kc, blk * 512:(blk + 1) * 512]),
                                         start=(kc == 0), stop=(kc == KC - 1))
                                for kc in range(KC):
                                    k.mm(pu[j][:], wu[wj][:, kc, fi * 128:(fi + 1) * 128], hb.v(hT.t[:, kc, blk * 512:(blk + 1) * 512]),
                                         start=(kc == 0), stop=(kc == KC - 1))
                                k.act(sg[j][:], pg[j][:], AF.Silu)
                                k.tt('dve', a[:, fi, :], sg[j][:], pu[j][:], ALU.mult)
                            for t4 in range(4):
                                tt = blk * 4 + t4
                                i = xi % 2
                                xi += 1
                                k.dma('sp', xt[i][:], out_t[tt][:])
                                for half in range(2):
                                    for fi in range(nf):
                                        k.mm(po[i][half][:], a[:, fi, t4 * 128:(t4 + 1) * 128], wo2[wj][:, fi, half * 512:(half + 1) * 512],
                                             start=(fi == 0), stop=(fi == nf - 1))
                                for half in range(2):
                                    sl = slice(half * 512, (half + 1) * 512)
                                    k.tt('dve', xo[i][:, sl], po[i][half][:], xt[i][:, sl], ALU.add)
                                k.dma('sp', out_t[tt][:], xo[i][:])
                    k.end_phase()
        k.barrier()
        build.ninst = dict(k.ninst)
        build.retired = (getattr(k, 'retired', 0), getattr(k, 'nds', 0), len(k.own['pe']) + len(k.own['act']) + len(k.own['dve']) + len(k.own['pool']) + len(k.own['sp']))
        build.maxd = (max([c for _, c in k.dpool['sp']] + [0]) * 16, max([c for _, c in k.dpool['pool']] + [0]) * 16, k.nsem)
    return nc


def phase_B(k, l, T, hT, hT_b, RO, w_in, lng_in, lnb_in, wsT_in, bs_in, cst, identb, epsc, yT_t, yT_d):
    NT = T // 128
    c0 = 2056
    maskUi = cst[:, 5, :]
    with ExitStack() as pes:
        k.begin_phase()
        wb = k.sb("bw", [128, KC, 512], BF16, pes)
        k.load_w(wb, w_in[l, :, c0:c0 + 512], KC, 512, RO)
        lng = k.sb("blng", [128, 256], F32, pes)
        lnb = k.sb("blnb", [128, 256], F32, pes)
        wsf = k.sb("bwsf", [128, 4, 128], F32, pes)
        wsb = k.sb("bwsb", [128, 4, 128], BF16, pes)
        bs = k.sb("bbs", [128, 4], F32, pes)
        k.dma('sp', lng[:], RO(lng_in[l].partition_broadcast(128)))
        k.dma('sp', lnb[:], RO(lnb_in[l].partition_broadcast(128)))
        k.dma('sp', wsf[:], RO(wsT_in[l]))
        k.dma('sp', bs[:], RO(bs_in[l]))
        k.tt('dve', wsb[:], wsf[:], maskUi.tile.v(cst.t[:, 5:6, :].to_broadcast([128, 4, 128])), ALU.mult)
        gl = [k.sb(f"bgl{i}", [128, 512], F32, pes) for i in range(2)]
        st = k.sb("bst", [128, 6], F32, pes)
        mv = k.sb("bmv", [128, 2], F32, pes)
        rs = k.sb("brs", [128, 2], F32, pes)
        vn = k.sb("bvn", [128, 256], F32, pes)
        vn2 = k.sb("bvn2", [128, 256], F32, pes)
        vnb = [k.sb(f"bvnb{i}", [128, 256], BF16, pes) for i in range(2)]
        yb = [k.sb(f"byb{i}", [128, 256], BF16, pes) for i in range(2)]
        ybT = [k.sb(f"bybT{i}", [128, 2, 128], BF16, pes) for i in range(2)]
        pp = [k.ps(f"bpp{i}", [128, 512], F32, pes) for i in range(2)]
        psv = [k.ps(f"bpsv{i}", [128, 512], F32, pes) for i in range(2)]
        ptr = [k.ps(f"bptr{i}", [128, 8, 128], BF16, pes) for i in range(2)]
        def b1(tt):
            i = tt % 2
            hb = hT_b[tt // 4]
            for kc in range(KC):
                k.mm(pp[i][:], hb.v(hT.t[:, kc, tt * 128:(tt + 1) * 128]), wb[:, kc, :], start=(kc == 0), stop=(kc == KC - 1))
            yield
            k.act(gl[i][:], pp[i][:], AF.Gelu)
            yield
            k.op('dve', lambda g: g.bn_stats(out=st.t[:, 0:6], in_=gl[i].t[:, 256:512]), [gl[i][:]], [st[:]])
            k.op('dve', lambda g: g.bn_aggr(out=mv.t[:, 0:2], in_=st.t[:, 0:6]), [st[:]], [mv[:]])
            yield
            k.act(rs[:, 0:1], mv[:, 1:2], AF.Sqrt, bias=epsc[:, 0:1], scale=1.0)
            k.recip(rs[:, 1:2], rs[:, 0:1])
            yield
            k.ts('dve', vn[:], gl[i][:, 256:512], mv[:, 0:1], rs[:, 1:2], ALU.subtract, ALU.mult)
            yield
            k.tt('dve', vn2[:], vn[:], lng[:], ALU.mult)
            yield
            k.tt('dve', vnb[i][:], vn2[:], lnb[:], ALU.add)
            yield

        def b2(tt):
            i = tt % 2
            for g in range(4):
                k.mm(psv[i][:, g * 64:(g + 1) * 64], wsb[:, g, :], vnb[i][:, g * 64:(g + 1) * 64])
            yield
            for g in range(4):
                k.stt('dve', yb[i][:, g * 64:(g + 1) * 64], psv[i][:, g * 64:(g + 1) * 64], bs[:, g:g + 1],
                      gl[i][:, g * 64:(g + 1) * 64], ALU.add, ALU.mult)
                if g % 2 == 1:
                    yield
            for j in range(2):
                k.tr(ptr[i][:, j, :], yb[i][:, j * 128:(j + 1) * 128], identb[:])
            yield
            k.copy('act', ybT[i][:], ptr[i][:, 0:2, :])
            yield
            k.dma('sp', yT_t[tt].v(yT_d[tt][:, 512:768].rearrange("p (c t) -> p c t", c=2)), ybT[i][:])

        def rr(gens):
            gens = list(gens)
            while gens:
                for g in list(gens):
                    try:
                        next(g)
                    except StopIteration:
                        gens.remove(g)

        rr([b1(0)])
        for tt in range(NT):
            g = [b2(tt)]
            if tt + 1 < NT:
                g.append(b1(tt + 1))
            rr(g)
        k.end_phase()


def phase_C(k, l, T, hT, hT_b, RO, w_in, qnw_in, knw_in, rot_in, identb, maskCb, maskPb, epsc, nl_t, nl_d):
    c0 = 2568
    with ExitStack() as pes:
        k.begin_phase()
        wc = k.sb("cw", [128, KC, 2304], BF16, pes)
        k.load_w(wc, w_in[l, :, c0:c0 + 2304], KC, 2304, RO)
        nwq = k.sb("cnwq", [128, 64], F32, pes)
        nwk = k.sb("cnwk", [128, 64], F32, pes)
        nw = k.sb("cnw", [128, 8, 64], F32, pes)
        k.dma('sp', nwq[:], RO(qnw_in[l].partition_broadcast(128)))
        k.dma('sp', nwk[:], RO(knw_in[l].partition_broadcast(128)))
        for h in range(4):
            k.copy('dve', nw[:, h, :], nwq[:])
            k.copy('dve', nw[:, 4 + h, :], nwk[:])
        rot = [k.sb(f"crot{i}", [128, 32], F32, pes) for i in range(3)]
        sqt = [k.sb(f"csq{i}", [128, 512], F32, pes) for i in range(2)]
        ss = [k.sb(f"css{i}", [128, 8], F32, pes) for i in range(2)]
        rs = [k.sb(f"crs{i}", [128, 8], F32, pes) for i in range(2)]
        ri = [k.sb(f"cri{i}", [128, 8], F32, pes) for i in range(2)]
        qkf = k.sb("cqkf", [128, 8, 64], F32, pes)
        qkg = k.sb("cqkg", [128, 8, 64], F32, pes)
        ra = k.sb("cra", [128, 8, 16], F32, pes)
        rb = k.sb("crb", [128, 8, 16], F32, pes)
        qkb = k.sb("cqkb", [128, 8, 64], BF16, pes)
        v1 = [k.sb(f"cv1{i}", [128, 4, 80], BF16, pes) for i in range(4)]
        qT = [k.sb(f"cqT{i}", [64, 4, 128], BF16, pes) for i in range(2)]
        kT = [k.sb(f"ckT{i}", [64, 4, 128], BF16, pes) for i in range(3)]
        P0 = k.sb("cP0", [128, 4, 128], BF16, pes)
        P1 = k.sb("cP1", [128, 4, 128], BF16, pes)
        P0m = k.sb("cP0m", [128, 4, 128], BF16, pes)
        P1m = k.sb("cP1m", [128, 4, 128], BF16, pes)
        nlo = [k.sb(f"cnlo{i}", [128, 4, 65], F32, pes) for i in range(2)]
        pc0b = [k.ps(f"cpc0{i}", [128, 512], F32, pes) for i in range(2)]
        pc1 = k.ps("cpc1", [128, 512], F32, pes)
        ptr = k.ps("cptr", [128, 8, 128], BF16, pes)
        pS0 = k.ps("cpS0", [128, 4, 128], F32, pes)
        pS1 = k.ps("cpS1", [128, 4, 128], F32, pes)
        pO = k.ps("cpO", [128, 4, 128], F32, pes)
        for i in range(4):
            k.memset('pool', v1[i][:], 1.0)
        mC = maskCb.v(maskCb.t[:, :].unsqueeze(1).to_broadcast([128, 4, 128]))
        mP = maskPb.v(maskPb.t[:, :].unsqueeze(1).to_broadcast([128, 4, 128]))
        tiles = []
        for p, (window, d) in enumerate(PATTERNS):
            nb = (T // d) // 128
            for r in range(d):
                for n in range(nb):
                    start = 128 * n * d + r
                    tiles.append((p, d, n, start, start + 127 * d + 1))

        def s1a(i):
            p, d, n, start, stop_ = tiles[i]
            b0 = start // 512
            b1 = (stop_ - 1) // 512
            hbs = [hT_b[b] for b in range(b0, b1 + 1)]
            rt = rot[i % 3]
            pc0 = pc0b[i % 2]
            k.dma('sp', rt[:], RO(rot_in[start:stop_:d, :]))
            for kc in range(KC):
                lhs = TV(hbs[0], hT.t[:, kc, start:stop_:d], tuple(hbs[1:]))
                k.mm(pc0[:], lhs, wc[:, kc, p * 768:p * 768 + 512], start=(kc == 0), stop=(kc == KC - 1))
            yield
            for kc in range(KC):
                lhs = TV(hbs[0], hT.t[:, kc, start:stop_:d], tuple(hbs[1:]))
                k.mm(pc1[:, 0:256], lhs, wc[:, kc, p * 768 + 512:p * 768 + 768], start=(kc == 0), stop=(kc == KC - 1))
            k.act(sqt[i % 2][:], pc0[:], AF.Square)
            yield
            k.reduce('dve', ss[i % 2][:], sqt[i % 2].v(sqt[i % 2].t[:, :].rearrange("p (g e) -> p g e", g=8)), ALU.add)
            k.copy('act', v1[i % 4][:, :, 0:64], pc1.v(pc1.t[:, 0:256].rearrange("p (h e) -> p h e", h=4)))
            yield
            k.act(rs[i % 2][:], ss[i % 2][:], AF.Ln, bias=epsc[:, 0:1], scale=1.0 / 64)
            k.act(ri[i % 2][:], rs[i % 2][:], AF.Exp, scale=-0.5)
            yield

        def s1b(i):
            p, d, n, start, stop_ = tiles[i]
            rt = rot[i % 3]
            pc0 = pc0b[i % 2]
            k.tt('dve', qkf[:], pc0.v(pc0.t[:, :].rearrange("p (g e) -> p g e", g=8)),
                 ri[i % 2].v(ri[i % 2].t[:, :].unsqueeze(2).to_broadcast([128, 8, 64])), ALU.mult)
            yield
            k.tt('dve', qkg[:], qkf[:], nw[:], ALU.mult)
            yield
            C2 = rt.v(rt.t[:, 0:16].unsqueeze(1).to_broadcast([128, 8, 16]))
            Sa = rt.v(rt.t[:, 16:24].unsqueeze(1).to_broadcast([128, 8, 8]))
            Sb = rt.v(rt.t[:, 24:32].unsqueeze(1).to_broadcast([128, 8, 8]))
            k.tt('dve', ra[:], qkg[:, :, 0:16], C2, ALU.mult)
            k.tt('dve', rb[:, :, 0:8], qkg[:, :, 8:16], Sa, ALU.mult)
            k.copy('pool', qkb[:, :, 16:64], qkg[:, :, 16:64])
            yield
            k.tt('dve', rb[:, :, 8:16], qkg[:, :, 0:8], Sb, ALU.mult)
            k.tt('dve', qkb[:, :, 0:16], ra[:], rb[:], ALU.add)
            yield
            for j in range(8):
                k.tr(ptr[0:64, j, :], qkb[:, j, :], identb[:])
            yield
            k.copy('act', qT[i % 2][:], ptr[0:64, 0:4, :])
            k.copy('act', kT[i % 3][:], ptr[0:64, 4:8, :])
            yield

        def s2(i):
            p, d, n, start, stop_ = tiles[i]
            cur = i % 3
            prev = (i - 1) % 3
            vc = i % 4
            vp = (i - 1) % 4
            q_ = qT[i % 2]
            for h in range(4):
                k.mm(pS0[:, h, :], kT[cur][:, h, :], q_[:, h, :])
            yield
            k.act(P0[:], pS0[:], AF.Exp, scale=0.125)
            if n > 0:
                for h in range(4):
                    k.mm(pS1[:, h, :], kT[prev][:, h, :], q_[:, h, :])
            yield
            k.tt('dve', P0m[:], P0[:], mC, ALU.mult)
            if n > 0:
                k.act(P1[:], pS1[:], AF.Exp, scale=0.125)
                yield
                k.tt('dve', P1m[:], P1[:], mP, ALU.mult)
            yield
            for h in range(4):
                if n > 0:
                    k.mm(pO[:, h, 0:65], P1m[:, h, :], v1[vp][:, h, 0:65], start=True, stop=False)
                k.mm(pO[:, h, 0:65], P0m[:, h, :], v1[vc][:, h, 0:65], start=(n == 0), stop=True)
            yield
            o_ = nlo[i % 2]
            k.copy('act', o_[:], pO[:, :, 0:65])
            yield
            k.dma('sp', nl_t[p].v(nl_d[p][start:stop_:d, :].rearrange("p (h e) -> p h e", h=4)), o_[:])
            yield

        def rr(gens):
            gens = list(gens)
            while gens:
                for g in list(gens):
                    try:
                        next(g)
                    except StopIteration:
                        gens.remove(g)

        NTL = len(tiles)
        for i in range(NTL + 2):
            g = []
            if 0 <= i - 2 < NTL:
                g.append(s2(i - 2))
            if 0 <= i - 1 < NTL:
                g.append(s1b(i - 1))
            if i < NTL:
                g.append(s1a(i))
            rr(g)
        k.end_phase()


def phase_A(k, l, T, hT, hT_b, RO, w_in, convw_in, a_log, dt_b, onw_in, cst, identb, epsc, yT_t, yT_d):
    NT = T // 128
    NB = T // 512
    ident = cst[:, 0, :]
    Um = cst[:, 1, :]
    SLm = cst[:, 2, :]
    ones = cst[:, 3, :]
    maskS = cst[:, 4, :]
    maskUi = cst[:, 5, :]
    SCALE = 128.0 ** -0.5
    with ExitStack() as pes:
        k.begin_phase()
        wa = k.sb("aw", [128, KC, 2056], BF16, pes)
        k.load_w(wa, w_in[l, :, 0:2056], KC, 2056, RO)
        cw = k.sb("acw", [128, 12, 4], F32, pes)
        k.dma('sp', cw[:], RO(convw_in[l]))
        alog = k.sb("aalog", [128, 4], F32, pes)
        dtb = k.sb("adtb", [128, 4], F32, pes)
        nea = k.sb("anea", [128, 4], F32, pes)
        onw = k.sb("aonw", [128, 128], F32, pes)
        k.dma('sp', alog[:], RO(a_log[l].partition_broadcast(128)))
        k.dma('sp', dtb[:], RO(dt_b[l].partition_broadcast(128)))
        k.dma('sp', onw[:], RO(onw_in[l].partition_broadcast(128)))
        k.act(nea[:], alog[:], AF.Exp)
        k.ts('dve', nea[:], nea[:], -1.0, None, ALU.mult)
        halo = k.sb("ahalo", [128, 12, 3], F32, pes)
        k.memset('dve', halo[:], 0.0)
        raw3 = [k.sb(f"araw{i}", [128, 515], F32, pes) for i in range(3)]
        acc3 = [k.sb(f"aacc{i}", [128, 512], F32, pes) for i in range(3)]
        qkv = k.sb("aqkv", [128, 12, 512], F32, pes)
        S = [[k.sb(f"aS{h}{i}", [128, 128], F32, pes) for i in range(2)] for h in range(4)]
        for h in range(4):
            k.memset('dve', S[h][0][:], 0.0)
        szb = [k.sb(f"asz{i}", [128, 512], F32, pes) for i in range(3)]
        smb = [k.sb(f"asm{i}", [128, 64], F32, pes) for i in range(2)]
        oallb = [k.sb(f"aoall{i}", [128, 4, 128], F32, pes) for i in range(2)]
        osq = k.sb("aosq", [128, 128], F32, pes)
        oss = k.sb("aoss", [128, 4], F32, pes)
        ors = k.sb("aors", [128, 8], F32, pes)
        ot = k.sb("aot", [128, 128], F32, pes)
        ya = k.sb("aya", [128, 4, 128], F32, pes)
        yaT = [k.sb(f"ayaT{i}", [128, 4, 128], BF16, pes) for i in range(2)]
        names = ["kbg", "kdec", "vb", "GU", "eD", "eDT", "tA", "N", "NT", "tQ", "aqk", "TTa", "TTb",
                 "Pa", "PTa", "Pb", "PTb", "u", "wT", "vnew", "o2"]
        W = [{n: k.sb(f"a{n}{s}", [128, 128], F32, pes) for n in names} for s in range(4)]
        BK = [k.ps(f"abk{i}", [128, 512], F32, pes) for i in range(8)]
        pq = [BK[0], BK[1]]
        pz = BK[6]
        pm = BK[7]
        pty = BK[7]

        def sc(j):
            return sm[:, j:j + 1]

        pwi = [0]

        def nxt():
            t = pw[pwi[0] % 3]
            s = (pwi[0] // 3) % 4
            pwi[0] += 1
            return t[:, s, :]

        it = 0
        for blk in range(NB):
            hb = hT_b[blk]
            tok = slice(blk * 512, (blk + 1) * 512)
            def conv_gen(fc, n_):
                j = n_ % 2
                rw = raw3[n_ % 3]
                ac = acc3[n_ % 3]
                for kc in range(KC):
                    k.mm(pq[j][:], wa[:, kc, fc * 128:(fc + 1) * 128], hb.v(hT.t[:, kc, tok]), start=(kc == 0), stop=(kc == KC - 1))
                k.copy('pool', rw[:, 0:3], halo[:, fc, :])
                k.copy('act', rw[:, 3:515], pq[j][:])
                k.copy('pool', halo[:, fc, :], rw[:, 512:515])
                yield
                k.ts('dve', ac[:], rw[:, 0:512], cw[:, fc, 0:1], None, ALU.mult)
                k.stt('dve', ac[:], rw[:, 1:513], cw[:, fc, 1:2], ac[:], ALU.mult, ALU.add)
                k.stt('dve', ac[:], rw[:, 2:514], cw[:, fc, 2:3], ac[:], ALU.mult, ALU.add)
                k.stt('dve', ac[:], rw[:, 3:515], cw[:, fc, 3:4], ac[:], ALU.mult, ALU.add)
                yield
                k.act(qkv[:, fc, :], ac[:], AF.Silu)
                yield

            def l2_gen(fc, n_):
                j = n_ % 2
                ac = acc3[n_ % 3]
                rw = raw3[n_ % 3]
                k.act(ac[:], qkv[:, fc, :], AF.Square)
                yield
                k.mm(pq[j][:], ones, ac[:])
                yield
                k.act(rw[:, 0:512], pq[j][:], AF.Ln, bias=epsc[:, 0:1], scale=1.0)
                k.act(ac[:], rw[:, 0:512], AF.Exp, scale=-0.5)
                yield
                k.tt('dve', qkv[:, fc, :], qkv[:, fc, :], ac[:], ALU.mult)
                yield

            def stagger(mk, n):
                act_ = []
                i_ = 0
                while act_ or i_ < n:
                    for g in list(act_):
                        try:
                            next(g)
                        except StopIteration:
                            act_.remove(g)
                    if i_ < n:
                        g = mk(i_)
                        next(g)
                        act_.append(g)
                        i_ += 1

            stagger(lambda i_: conv_gen(i_, i_), 12)
            stagger(lambda i_: l2_gen(i_, i_), 8)
            def pre_gen(tt):
                sm = smb[tt % 2]
                sz = szb[tt % 3]
                hts = slice(tt * 128, (tt + 1) * 128)
                for kc in range(KC):
                    k.mm(pz[:], hb.v(hT.t[:, kc, hts]), wa[:, kc, 1536:2048], start=(kc == 0), stop=(kc == KC - 1))
                yield
                for kc in range(KC):
                    k.mm(pm[:, 0:8], hb.v(hT.t[:, kc, hts]), wa[:, kc, 2048:2056], start=(kc == 0), stop=(kc == KC - 1))
                k.act(sz[:], pz[:], AF.Silu)
                yield
                k.tt('dve', sm[:, 0:4], pm[:, 0:4], dtb[:], ALU.add)
                k.act(sm[:, 16:20], pm[:, 4:8], AF.Exp, scale=-1.0)
                yield
                k.act(sm[:, 4:8], sm[:, 0:4], AF.Exp)
                k.ts('dve', sm[:, 16:20], sm[:, 16:20], 1.0, None, ALU.add)
                yield
                k.act(sm[:, 8:12], sm[:, 4:8], AF.Ln, bias=1.0)
                k.recip(sm[:, 20:24], sm[:, 16:20])
                yield
                k.tt('dve', sm[:, 12:16], sm[:, 8:12], nea[:], ALU.mult)
                k.ts('dve', sm[:, 24:28], sm[:, 20:24], -1.0, None, ALU.mult)
                yield
                k.mm(pm[:, 8:12], Um, sm[:, 12:16])
                k.mm(pm[:, 12:16], ones, sm[:, 12:16])
                yield
                k.copy('dve', sm[:, 28:32], pm[:, 8:12])
                k.copy('dve', sm[:, 36:40], pm[:, 12:16])
                yield
                k.act(sm[:, 32:36], sm[:, 28:32], AF.Exp)
                k.act(sm[:, 40:44], sm[:, 36:40], AF.Exp)
                k.tt('dve', sm[:, 44:48], sm[:, 36:40], sm[:, 28:32], ALU.subtract)
                yield
                k.act(sm[:, 48:52], sm[:, 44:48], AF.Exp)
                k.tt('dve', sm[:, 52:56], sm[:, 20:24], sm[:, 32:36], ALU.mult)
                k.ts('dve', sm[:, 56:60], sm[:, 32:36], SCALE, None, ALU.mult)
                yield

            def epi_gen(tt):
                oall = oallb[tt % 2]
                sz = szb[tt % 3]
                k.memset('dve', oss[:], 0.0)
                for h in range(4):
                    k.act(osq[:], oall[:, h, :], AF.Square, accum=oss[:, h:h + 1])
                    yield
                k.act(ors[:, 0:4], oss[:], AF.Ln, bias=epsc[:, 0:1], scale=1.0 / 128)
                k.act(ors[:, 4:8], ors[:, 0:4], AF.Exp, scale=-0.5)
                yield
                for h in range(4):
                    k.stt('dve', ot[:], oall[:, h, :], ors[:, 4 + h:5 + h], onw[:], ALU.mult, ALU.mult)
                    k.tt('dve', ya[:, h, :], ot[:], sz[:, h * 128:(h + 1) * 128], ALU.mult)
                    yield
                for h in range(4):
                    k.tr(pty[:, h * 128:(h + 1) * 128], ya[:, h, :], ident)
                yield
                i = tt % 2
                k.copy('act', yaT[i][:], pty.v(pty.t[:, :].rearrange("p (h t) -> p h t", h=4)))
                yield
                k.dma('sp', yT_t[tt].v(yT_d[tt][:, 0:512].rearrange("p (c t) -> p c t", c=4)), yaT[i][:])

            def head_gen(h, tt, cs, sm, oall):
                w = W[h]

                def sc(j):
                    return sm[:, j:j + 1]
                qT_ = qkv[:, h, cs]
                kT_ = qkv[:, 4 + h, cs]
                vT_ = qkv[:, 8 + h, cs]
                Sold = S[h][tt % 2]
                Snew = S[h][(tt + 1) % 2]
                cnt = [0]

                def nx():
                    c = cnt[0]
                    cnt[0] += 1
                    bk = BK[2 * h + (c % 2)]
                    o = ((c // 2) % 4) * 128
                    return bk[:, o:o + 128]
                p = nx()

                def post(p=p):
                    k.ts('dve', w["kbg"][:], p, sc(52 + h), None, ALU.mult)
                    k.amul(w["kdec"][:], p, sc(48 + h))
                yield [('tr', p, kT_, ident)], post
                p = nx()

                def post(p=p):
                    k.amul(w["vb"][:], p, sc(20 + h))
                    k.ts('dve', w["GU"][:], Um, sc(12 + h), None, ALU.mult)
                yield [('tr', p, vT_, ident)], post
                p = nx()

                def post(p=p):
                    k.act(w["eD"][:], p, AF.Exp)
                    k.tt('pool', w["tA"][:], w["eD"][:], maskS, ALU.mult)
                yield [('mm', p, w["GU"][:], SLm)], post
                p = nx()

                def post(p=p):
                    k.act(w["eDT"][:], p, AF.Exp)
                    k.tt('pool', w["tQ"][:], w["eDT"][:], maskUi, ALU.mult)
                yield [('mm', p, SLm, w["GU"][:])], post
                p = nx()

                def post(p=p):
                    k.stt('dve', w["N"][:], p, sc(24 + h), w["tA"][:], ALU.mult, ALU.mult)
                yield [('mm', p, kT_, kT_)], post
                p = nx()

                def post(p=p):
                    k.copy('act', w["NT"][:], p)
                    k.tt('pool', w["TTa"][:], w["NT"][:], ident, ALU.add)
                yield [('tr', p, w["N"][:], ident)], post
                p = nx()

                def post(p=p):
                    k.stt('dve', w["aqk"][:], p, SCALE, w["tQ"][:], ALU.mult, ALU.mult)
                yield [('mm', p, kT_, qT_)], post
                TT, TTo = w["TTa"], w["TTb"]
                P_, PT_ = w["N"], w["NT"]
                Pn, PTn = w["Pa"], w["PTa"]
                for s_ in range(6):
                    p = nx()
                    ops = [('mm', p, PT_[:], P_[:])]
                    p2 = None
                    if s_ < 5:
                        p2 = nx()
                        ops.append(('mm', p2, P_[:], PT_[:]))

                    def post(p=p, p2=p2, Pn=Pn, PTn=PTn, s_=s_):
                        k.copy('act', Pn[:], p)
                        if p2 is not None:
                            k.copy('dve', PTn[:], p2)
                    yield ops, post
                    p3 = nx()

                    def post(p3=p3, TT=TT, TTo=TTo):
                        k.tt('dve', TTo[:], p3, TT[:], ALU.add)
                    yield [('mm', p3, Pn[:], TT[:])], post
                    TT, TTo = TTo, TT
                    P_, PT_ = Pn, PTn
                    if s_ % 2 == 0:
                        Pn, PTn = w["Pb"], w["PTb"]
                    else:
                        Pn, PTn = w["Pa"], w["PTa"]
                p = nx()
                p2 = nx()

                def post(p=p, p2=p2):
                    k.copy('act', w["u"][:], p)
                    k.copy('dve', w["wT"][:], p2)
                yield [('mm', p, TT[:], w["vb"][:]), ('mm', p2, w["kbg"][:], TT[:])], post
                p = nx()
                p1 = nx()

                def post(p=p):
                    k.stt('dve', w["vnew"][:], p, -1.0, w["u"][:], ALU.mult, ALU.add)
                yield [('mm', p, w["wT"][:], Sold[:]), ('mm', p1, qT_, Sold[:])], post
                p2 = nx()
                p3 = nx()

                def post(p1=p1, p2=p2, p3=p3):
                    k.copy('act', w["o2"][:], p2)
                    k.stt('dve', Snew[:], Sold[:], sc(40 + h), p3, ALU.mult, ALU.add)
                    k.stt('dve', oall[:, h, :], p1, sc(56 + h), w["o2"][:], ALU.mult, ALU.add)
                yield [('mm', p2, w["aqk"][:], w["vnew"][:]), ('mm', p3, w["kdec"][:], w["vnew"][:])], post


            def run_tile(tt, side):
                cs = slice((tt % 4) * 128, (tt % 4 + 1) * 128)
                groups = [[head_gen(h, tt, cs, smb[tt % 2], oallb[tt % 2]) for h in (0, 1)],
                          [head_gen(h, tt, cs, smb[tt % 2], oallb[tt % 2]) for h in (2, 3)]]
                side = list(side)
                alive = True
                while alive:
                    alive = False
                    for grp in groups:
                        items = []
                        for g in list(grp):
                            try:
                                items.append(next(g))
                            except StopIteration:
                                grp.remove(g)
                        if not items:
                            continue
                        alive = True
                        k.pe_batch([d_ for ops, _ in items for d_ in ops])
                        for _, post in items:
                            post()
                    for g in list(side):
                        try:
                            next(g)
                        except StopIteration:
                            side.remove(g)
                for g in side:
                    for _ in g:
                        pass

            t0_ = blk * 4
            for _ in pre_gen(t0_):
                pass
            for t4 in range(4):
                tt = t0_ + t4
                side = []
                if t4 + 1 < 4:
                    side.append(pre_gen(tt + 1))
                if t4 > 0:
                    side.append(epi_gen(tt - 1))
                run_tile(tt, side)
            for _ in epi_gen(t0_ + 3):
                pass
        k.end_phase()


def _consts():
    i = np.arange(128)
    c = np.zeros((128, 8, 128), np.float32)
    c[:, 0, :] = np.eye(128)
    c[:, 1, :] = (i[:, None] <= i[None, :])
    c[:, 2, :] = (i[:, None] > i[None, :])
    c[:, 3, :] = 1.0
    c[:, 4, :] = (i[:, None] > i[None, :])
    c[:, 5, :] = (i[:, None] <= i[None, :])
    c[:, 6, :] = (i[:, None] >= i[None, :])
    return c


def _rot(T):
    inv = (np.float32(ROPE_THETA) ** (-np.arange(0, 16, 2, dtype=np.float32) / np.float32(16))).astype(np.float32)
    ang = (np.arange(T, dtype=np.float32)[:, None] * inv[None, :]).astype(np.float32)
    cos, sin = np.cos(ang).astype(np.float32), np.sin(ang).astype(np.float32)
    return np.concatenate([cos, cos, -sin, sin], axis=1).astype(np.float32)


def make_in_maps(inputs, T, depth, ncores):
    f = lambda a: np.ascontiguousarray(np.asarray(a, dtype=np.float32))
    shared = {
        "w_mod": f(inputs["w_mod"]), "b_mod": f(inputs["b_mod"]),
        "mix_norm_w": f(np.asarray(inputs["mix_norm_w"]).reshape(depth, KC, 128).transpose(0, 2, 1)),
        "ffn_norm_w": f(np.asarray(inputs["ffn_norm_w"]).reshape(depth, KC, 128).transpose(0, 2, 1)),
        "w_in": f(inputs["w_in"]), "w_out": f(inputs["w_out"]),
        "dn_conv_w": f(np.asarray(inputs["dn_conv_w"]).transpose(0, 2, 1).reshape(depth, 12, 128, 4).transpose(0, 2, 1, 3)),
        "dn_a_log": f(inputs["dn_a_log"]), "dn_dt_bias": f(inputs["dn_dt_bias"]),
        "dn_out_norm_w": f(inputs["dn_out_norm_w"]),
        "gm_ln_g": f(inputs["gm_ln_g"]), "gm_ln_b": f(inputs["gm_ln_b"]),
        "gm_w_sT": f(np.asarray(inputs["gm_w_s"]).transpose(0, 3, 1, 2)),
        "gm_b_s": f(np.asarray(inputs["gm_b_s"]).transpose(0, 2, 1)),
        "sw_q_norm_w": f(inputs["sw_q_norm_w"]), "sw_k_norm_w": f(inputs["sw_k_norm_w"]),
        "w_ffn_in": f(inputs["w_ffn_in"]), "w_ffn_out": f(inputs["w_ffn_out"]),
        "consts": _consts(), "rot": _rot(T),
    }
    x = np.asarray(inputs["x"], dtype=np.float32)
    c = np.asarray(inputs["c"], dtype=np.float32)
    maps = []
    for b in range(ncores):
        m = dict(shared)
        m["x"] = np.ascontiguousarray(x[b])
        m["c"] = np.ascontiguousarray(c[b].reshape(KC, 128).T)
        maps.append(m)
    return maps


def kernel(**inputs):
    nc = build(SEQ, DEPTH, False)
    maps = make_in_maps(inputs, SEQ, DEPTH, NCORES)
    res = run_bass_kernel_spmd(nc, maps, core_ids=list(range(NCORES)))
    return np.stack([np.asarray(r["out"], dtype=np.float32) for r in res.results], axis=0)
```

```python
import numpy as np
from contextlib import ExitStack
import concourse.bass as bass
import concourse.mybir as mybir
from concourse.bass_utils import run_bass_kernel_spmd

F32 = mybir.dt.float32
BF16 = mybir.dt.bfloat16
AF = mybir.ActivationFunctionType
ALU = mybir.AluOpType
AX = mybir.AxisListType

D = 1024
KC = 8
DEPTH = 4
SEQ = 4096
NCORES = 8
EPS = 1e-6
INW = 4872
FH = 2816
NFC = 22
SEM_ROLL = 15000
PATTERNS = ((128, 1), (512, 4), (2048, 16))
CUT = 99
F32R = False
ROPE_THETA = 500000.0


class Tile:
    def __init__(self, k, t, name, dram=False):
        self.k, self.t, self.name, self.dram = k, t, name, dram
        self.writes = {}
        self.reads = {}
        self.dsem = None
        self.dcnt = 0

    def __getitem__(self, idx):
        return TV(self, self.t[idx])

    def v(self, ap):
        return TV(self, ap)


class TV:
    def __init__(self, tile, ap, extra=()):
        self.tile, self.ap, self.extra = tile, ap, extra


def _m(d, s):
    for sem, v in s.items():
        if d.get(sem, 0) < v:
            d[sem] = v


class K:
    def __init__(self, nc, es):
        self.nc, self.es = nc, es
        self.engs = {'pe': nc.tensor, 'act': nc.scalar, 'dve': nc.vector, 'pool': nc.gpsimd, 'sp': nc.sync}
        self.cnt = {}
        self.sem = {}
        self.own = {e: set() for e in self.engs}
        self.waited = {e: {} for e in self.engs}
        self.nsem = 0
        for e in self.engs:
            self._newsem(e)
        self.dma_tiles = []
        self.dpool = {'sp': [], 'pool': []}
        self.scope = None
        self.ninst = {e: 0 for e in self.engs}

    def _newsem(self, e):
        s = self.es.enter_context(self.nc.semaphore(f"s_{e}_{self.nsem}"))
        self.nsem += 1
        self.sem[e] = s
        self.cnt[e] = 0
        self.own[e].add(s)

    def sb(self, name, shape, dt, es=None):
        self.nsem += 1
        name = f"{name}_{self.nsem}"
        t = (es or self.es).enter_context(self.nc.sbuf_tensor(name, list(shape), dt))
        tl = Tile(self, t, name)
        if es is not None and self.scope is not None:
            self.scope.append(tl)
        return tl

    def begin_phase(self):
        self.scope = []

    def end_phase(self):
        self.barrier()
        for tl in self.scope:
            if tl.dsem is not None:
                if 16 * tl.dcnt < 2000:
                    self.dpool[tl.dq].append((tl.dsem, tl.dcnt))
                else:
                    self.retired = getattr(self, 'retired', 0) + 1
                self.dma_tiles.remove(tl)
                tl.dsem = None
        self.scope = None

    def ps(self, name, shape, dt, es=None):
        self.nsem += 1
        name = f"{name}_{self.nsem}"
        t = (es or self.es).enter_context(self.nc.psum_tensor(name, list(shape), dt))
        tl = Tile(self, t, name)
        tl.psum = True
        return tl

    def dram(self, ap, name):
        return Tile(self, ap, name, dram=True)

    @staticmethod
    def _tiles(tvs):
        out = []
        for tv in tvs:
            out.append(tv.tile)
            out.extend(tv.extra)
        return out

    def _waits(self, e, reads, writes):
        need = {}
        for t in self._tiles(reads):
            _m(need, t.writes)
            if getattr(t, 'psum', False):
                for sem, v in t.reads.items():
                    if sem not in self.own[e] and need.get(sem, 0) < v:
                        need[sem] = v
        for t in self._tiles(writes):
            _m(need, t.writes)
            _m(need, t.reads)
        eng = self.engs[e]
        w = self.waited[e]
        for sem, val in need.items():
            if e == 'pe' and sem in self.own['pe']:
                continue
            if w.get(sem, 0) < val:
                eng.wait_ge(sem, val)
                w[sem] = val
                self.ninst[e] += 1

    def _done(self, tok, reads, writes):
        sem, val = tok
        for t in self._tiles(reads):
            if t.reads.get(sem, 0) < val:
                t.reads[sem] = val
        for t in self._tiles(writes):
            if t.dram:
                if t.writes.get(sem, 0) < val:
                    t.writes[sem] = val
            else:
                t.writes = {sem: val}
            t.reads = {}

    def op(self, e, emit, reads, writes):
        self._waits(e, reads, writes)
        ins = emit(self.engs[e])
        if self.cnt[e] >= SEM_ROLL:
            self._newsem(e)
        self.cnt[e] += 1
        ins.then_inc(self.sem[e], 1)
        self.ninst[e] += 1
        tok = (self.sem[e], self.cnt[e])
        self._done(tok, reads, writes)
        return tok

    def dma(self, q, out, in_, **kw):
        own = out.tile if not out.tile.dram else in_.tile
        assert not own.dram
        if own.dsem is None:
            own.dq = q
            if self.dpool[q]:
                own.dsem, own.dcnt = self.dpool[q].pop(0)
            else:
                own.dsem = self.es.enter_context(self.nc.semaphore(f"d_{self.nsem}"))
                self.nds = getattr(self, 'nds', 0) + 1
                self.nsem += 1
            self.dma_tiles.append(own)
        assert own.dq == q, (own.name, own.dq, q)
        self._waits(q, [in_], [out])
        ins = self.engs[q].dma_start(out=out.ap, in_=in_.ap, **kw)
        own.dcnt += 1
        ins.then_inc(own.dsem, 16)
        self.ninst[q] += 1
        tok = (own.dsem, 16 * own.dcnt)
        self._done(tok, [in_], [out])
        return tok

    def load_w(self, dst, src2d, nk, cols, RO, q='pool', cstep=4096):
        for kc in range(nk):
            for c0 in range(0, cols, cstep):
                c1 = min(cols, c0 + cstep)
                self.dma(q, dst[:, kc, c0:c1], RO(src2d[kc * 128:(kc + 1) * 128, c0:c1]))

    def barrier(self):
        need = {}
        for e in self.engs:
            if self.cnt[e] > 0:
                need[self.sem[e]] = self.cnt[e]
        for t in self.dma_tiles:
            need[t.dsem] = 16 * t.dcnt
        for e in self.engs:
            w = self.waited[e]
            for sem, val in need.items():
                if e == 'pe' and sem in self.own['pe']:
                    continue
                if w.get(sem, 0) < val:
                    self.engs[e].wait_ge(sem, val)
                    w[sem] = val
                    self.ninst[e] += 1

    def mm(self, out, lhsT, rhs, start=True, stop=True):
        la, ra = lhsT.ap, rhs.ap
        if F32R and la.dtype == F32 and ra.dtype == F32:
            la = la.bitcast(mybir.dt.float32r)
            ra = ra.bitcast(mybir.dt.float32r)
        return self.op('pe', lambda g: g.matmul(out.ap, la, ra, start=start, stop=stop),
                       [lhsT, rhs], [out])

    def pe_batch(self, descrs):
        rd, wr = [], []
        for d in descrs:
            wr.append(d[1])
            rd.extend([d[2], d[3]])
        self._waits('pe', rd, wr)
        for d in descrs:
            if d[0] == 'mm':
                self.mm(d[1], d[2], d[3])
            else:
                self.tr(d[1], d[2], d[3])

    def tr(self, out, in_, ident):
        return self.op('pe', lambda g: g.transpose(out.ap, in_.ap, ident.ap), [in_, ident], [out])

    def act(self, out, in_, func, bias=None, scale=None, accum=None):
        kw = {}
        rd = [in_]
        wr = [out]
        if bias is not None:
            if isinstance(bias, TV):
                kw['bias'] = bias.ap
                rd.append(bias)
            else:
                kw['bias'] = bias
        if scale is not None:
            if isinstance(scale, TV):
                kw['scale'] = scale.ap
                rd.append(scale)
            else:
                kw['scale'] = scale
        if accum is not None:
            kw['accum_out'] = accum.ap
            wr.append(accum)
        return self.op('act', lambda g: g.activation(out=out.ap, in_=in_.ap, func=func, **kw), rd, wr)

    def tt(self, e, out, in0, in1, op):
        return self.op(e, lambda g: g.tensor_tensor(out=out.ap, in0=in0.ap, in1=in1.ap, op=op), [in0, in1], [out])

    def ts(self, e, out, in0, s1, s2, op0, op1=None):
        rd = [in0]
        a1, a2 = s1, s2
        if isinstance(s1, TV):
            rd.append(s1)
            a1 = s1.ap
        if isinstance(s2, TV):
            rd.append(s2)
            a2 = s2.ap
        if op1 is None:
            return self.op(e, lambda g: g.tensor_scalar(out=out.ap, in0=in0.ap, scalar1=a1, scalar2=None, op0=op0),
                           rd, [out])
        return self.op(e, lambda g: g.tensor_scalar(out=out.ap, in0=in0.ap, scalar1=a1, scalar2=a2, op0=op0, op1=op1),
                       rd, [out])

    def stt(self, e, out, in0, s, in1, op0, op1):
        rd = [in0, in1]
        a = s
        if isinstance(s, TV):
            rd.append(s)
            a = s.ap
        return self.op(e, lambda g: g.scalar_tensor_tensor(out=out.ap, in0=in0.ap, scalar=a, in1=in1.ap,
                                                           op0=op0, op1=op1), rd, [out])

    def copy(self, e, out, in_):
        if e == 'act':
            return self.op(e, lambda g: g.copy(out=out.ap, in_=in_.ap), [in_], [out])
        return self.op(e, lambda g: g.tensor_copy(out=out.ap, in_=in_.ap), [in_], [out])

    def amul(self, out, in_, m):
        return self.op('act', lambda g: g.mul(out=out.ap, in_=in_.ap, mul=m.ap), [in_, m], [out])

    def memset(self, e, out, val):
        return self.op(e, lambda g: g.memset(out.ap, val), [], [out])

    def recip(self, out, in_):
        return self.op('dve', lambda g: g.reciprocal(out=out.ap, in_=in_.ap), [in_], [out])

    def reduce(self, e, out, in_, op):
        return self.op(e, lambda g: g.tensor_reduce(out=out.ap, in_=in_.ap, axis=AX.X, op=op), [in_], [out])


def build(T=SEQ, depth=DEPTH, debug=False, phases='MNABCDF'):
    NT = T // 128
    NB = T // 512
    nc = bass.Bass("TRN2", target_bir_lowering=False)
    es = ExitStack()

    def din(name, shape):
        return nc.dram_tensor(name, list(shape), F32, kind="ExternalInput").ap()

    x_in = din("x", [T, D])
    c_in = din("c", [128, KC])
    w_mod = din("w_mod", [depth, D, 6 * D])
    b_mod = din("b_mod", [depth, 6 * D])
    mixw = din("mix_norm_w", [depth, 128, KC])
    ffnw = din("ffn_norm_w", [depth, 128, KC])
    w_in = din("w_in", [depth, D, INW])
    w_out = din("w_out", [depth, D, D])
    convw = din("dn_conv_w", [depth, 128, 12, 4])
    a_log = din("dn_a_log", [depth, 4])
    dt_b = din("dn_dt_bias", [depth, 4])
    onw_in = din("dn_out_norm_w", [depth, 128])
    lng_in = din("gm_ln_g", [depth, 256])
    lnb_in = din("gm_ln_b", [depth, 256])
    wsT_in = din("gm_w_sT", [depth, 128, 4, 128])
    bs_in = din("gm_b_s", [depth, 128, 4])
    qnw_in = din("sw_q_norm_w", [depth, 64])
    knw_in = din("sw_k_norm_w", [depth, 64])
    wf_in = din("w_ffn_in", [depth, D, 2 * FH])
    wf_out = din("w_ffn_out", [depth, FH, D])
    cst_in = din("consts", [128, 8, 128])
    rot_in = din("rot", [T, 32])
    out_d = nc.dram_tensor("out", [T, D], F32, kind="ExternalOutput").ap()
    skind = "ExternalOutput" if debug else "Internal"
    modrow_d = nc.dram_tensor("modrow", [depth, 6 * D], F32, kind=skind).ap()
    yT_d = nc.dram_tensor("yT", [NT, 128, 768], BF16, kind="Internal").ap()
    nl_d = [nc.dram_tensor(f"nl{p}", [T, 260], F32, kind=skind).ap() for p in range(3)]
    dbg_d = nc.dram_tensor("dbg", [T, D], F32, kind=skind).ap() if debug else None

    with es:
        k = K(nc, es)
        xin_t = [k.dram(x_in[i * 128:(i + 1) * 128, :], f"xin{i}") for i in range(NT)]
        out_t = [k.dram(out_d[i * 128:(i + 1) * 128, :], f"out{i}") for i in range(NT)]
        wts = k.dram(w_mod, "wts")
        modrow_t = k.dram(modrow_d, "modrow")
        yT_t = [k.dram(yT_d[i], f"yT{i}") for i in range(NT)]
        nl_t = [k.dram(nl_d[p], f"nl{p}") for p in range(3)]
        dbg_t = k.dram(dbg_d, "dbg") if debug else None

        def RO(ap):
            return wts.v(ap)

        hT = k.sb("hT", [128, KC, T], BF16)
        hT_b = [Tile(k, hT.t, f"hTb{b}") for b in range(NB)]
        cst = k.sb("cst", [128, 8, 128], F32)
        identb = k.sb("identb", [128, 128], BF16)
        maskCb = k.sb("maskCb", [128, 128], BF16)
        maskPb = k.sb("maskPb", [128, 128], BF16)
        epsc = k.sb("epsc", [128, 1], F32)
        k.dma('sp', cst[:], RO(cst_in))
        ident = cst[:, 0, :]
        Um = cst[:, 1, :]
        SLm = cst[:, 2, :]
        ones = cst[:, 3, :]
        maskS = cst[:, 4, :]
        maskUi = cst[:, 5, :]
        maskP = cst[:, 6, :]
        k.copy('dve', identb[:], ident)
        k.copy('dve', maskCb[:], maskUi)
        k.copy('dve', maskPb[:], maskP)
        k.memset('dve', epsc[:], EPS)

        a1 = k.sb("a1", [128, KC], F32)
        b1 = k.sb("b1", [128, KC], F32)
        a2 = k.sb("a2", [128, KC], F32)
        b2 = k.sb("b2", [128, KC], F32)
        g1bc = k.sb("g1bc", [128, D], F32)
        g2bc = k.sb("g2bc", [128, D], F32)
        nw1 = k.sb("nw1", [128, KC], F32)
        nw2 = k.sb("nw2", [128, KC], F32)
        mcol = k.sb("mcol", [128, 4, KC], F32)

        with ExitStack() as pes:
            k.begin_phase()
            c_sb = k.sb("c_sb", [128, KC], F32, pes)
            cact = k.sb("cact", [128, KC], F32, pes)
            wm = [k.sb(f"wm{i}", [128, KC, 512], F32, pes) for i in range(4)]
            bm = k.sb("bm", [1, 6 * D], F32, pes)
            mr = k.sb("mr", [1, 6 * D], F32, pes)
            pm = [k.ps(f"pm{i}", [128, 512], F32, pes) for i in range(2)]
            k.dma('sp', c_sb[:], RO(c_in))
            k.act(cact[:], c_sb[:], AF.Silu)
            it = 0
            for l in range(depth):
                k.dma('sp', bm[:], RO(b_mod[l:l + 1, :]))
                for ec in range(12):
                    w = wm[it % 4]
                    p = pm[it % 2]
                    q = 'sp' if it % 2 == 0 else 'pool'
                    k.dma(q, w[:], RO(w_mod[l, :, ec * 512:(ec + 1) * 512].rearrange("(kc p) e -> p kc e", p=128)))
                    for kc in range(KC):
                        k.mm(p[0:1, :], cact[:, kc:kc + 1], w[:, kc, :], start=(kc == 0), stop=(kc == KC - 1))
                    k.tt('dve', mr[0:1, ec * 512:(ec + 1) * 512], p[0:1, :], bm[0:1, ec * 512:(ec + 1) * 512], ALU.add)
                    it += 1
                k.dma('sp', modrow_t.v(modrow_d[l:l + 1, :]), mr[:])
            k.end_phase()

        def load_layer_params(l):
            for j, src in enumerate((0, 1, 3, 4)):
                k.dma('sp', mcol[:, j, :],
                      modrow_t.v(modrow_d[l, src * D:(src + 1) * D].rearrange("(kc p) -> p kc", p=128)),
                      allow_slow_non_contiguous=True)
            k.dma('sp', g1bc[:], modrow_t.v(modrow_d[l, 2 * D:3 * D].partition_broadcast(128)))
            k.dma('sp', g2bc[:], modrow_t.v(modrow_d[l, 5 * D:6 * D].partition_broadcast(128)))
            k.dma('sp', nw1[:], RO(mixw[l]))
            k.dma('sp', nw2[:], RO(ffnw[l]))
            k.stt('dve', a1[:], mcol[:, 1, :], 1.0, nw1[:], ALU.add, ALU.mult)
            k.copy('dve', b1[:], mcol[:, 0, :])
            k.stt('dve', a2[:], mcol[:, 3, :], 1.0, nw2[:], ALU.add, ALU.mult)
            k.copy('dve', b2[:], mcol[:, 2, :])

        def norm_gen(xt, tt, aa, bb, sq, ss, rs, xn, ptr):
            hb = hT_b[tt // 4]
            k.memset('dve', ss[:, 0:1], 0.0)
            k.act(sq[:], xt[:], AF.Square, accum=ss[:, 0:1])
            yield
            k.act(rs[:, 0:1], ss[:, 0:1], AF.Sqrt, bias=epsc[:, 0:1], scale=1.0 / D)
            k.recip(rs[:, 1:2], rs[:, 0:1])
            yield
            k.ts('dve', xn[:], xt[:], rs[:, 1:2], None, ALU.mult)
            yield
            for half in range(2):
                p = ptr[half]
                for j in range(4):
                    kc = half * 4 + j
                    k.tr(p[:, j, :], xn[:, kc * 128:(kc + 1) * 128], ident)
                yield
                for j in range(4):
                    kc = half * 4 + j
                    k.ts('dve', hb.v(hT.t[:, kc, tt * 128:(tt + 1) * 128]), p[:, j, :], aa[:, kc:kc + 1], bb[:, kc:kc + 1],
                         ALU.mult, ALU.add)
                    if j % 2 == 1:
                        yield

        def norm_to_hT(*a):
            for _ in norm_gen(*a):
                pass

        def rr(gens):
            gens = list(gens)
            while gens:
                for g in list(gens):
                    try:
                        next(g)
                    except StopIteration:
                        gens.remove(g)

        for l in range(depth):
            load_layer_params(l)
            xsrc = xin_t if l == 0 else out_t

            with ExitStack() as pes:
                k.begin_phase()
                xt = [k.sb(f"n1x{i}", [128, D], F32, pes) for i in range(2)]
                sqn = [k.sb(f"n1sq{i}", [128, D], F32, pes) for i in range(2)]
                xn = [k.sb(f"n1xn{i}", [128, D], F32, pes) for i in range(2)]
                ss = [k.sb(f"n1ss{i}", [128, 1], F32, pes) for i in range(2)]
                rs = [k.sb(f"n1rs{i}", [128, 2], F32, pes) for i in range(2)]
                ptr = [[k.ps(f"n1p{i}{h}", [128, 4, 128], F32, pes) for h in range(2)] for i in range(2)]
                for tt in range(0, NT, 2):
                    for i in range(2):
                        k.dma('sp', xt[i][:], xsrc[tt + i][:])
                    rr([norm_gen(xt[i], tt + i, a1, b1, sqn[i], ss[i], rs[i], xn[i], ptr[i]) for i in range(2)])
                k.end_phase()

            if 'A' in phases:
                phase_A(k, l, T, hT, hT_b, RO, w_in, convw, a_log, dt_b, onw_in, cst, identb, epsc, yT_t, yT_d)
            if 'B' in phases:
                phase_B(k, l, T, hT, hT_b, RO, w_in, lng_in, lnb_in, wsT_in, bs_in, cst, identb, epsc, yT_t, yT_d)
            if 'C' in phases:
                phase_C(k, l, T, hT, hT_b, RO, w_in, qnw_in, knw_in, rot_in, identb, maskCb, maskPb, epsc, nl_t, nl_d)

            with ExitStack() as pes:
                if 'D' not in phases:
                    break
                k.begin_phase()
                wo = k.sb("wo", [128, KC, D], BF16, pes)
                k.load_w(wo, w_out[l], KC, D, RO)
                k.tt('dve', wo[:], wo[:], g1bc.v(g1bc.t[:, :].unsqueeze(1).to_broadcast([128, KC, D])), ALU.mult)
                nl = [[k.sb(f"dnl{i}{p}", [128, 4, 65], F32, pes) for p in range(3)] for i in range(2)]
                nsum = k.sb("dnsum", [128, 4, 65], F32, pes)
                rl = k.sb("drl", [128, 4], F32, pes)
                ycb = k.sb("dycb", [128, 4, 64], BF16, pes)
                yT = [k.sb(f"dyT{i}", [128, 8, 128], BF16, pes) for i in range(2)]
                xt = [k.sb(f"dx{i}", [128, D], F32, pes) for i in range(2)]
                xo = [k.sb(f"dxo{i}", [128, D], F32, pes) for i in range(2)]
                tmp = k.sb("dtmp", [128, D], F32, pes)
                sq = k.sb("dsq", [128, D], F32, pes)
                xn = k.sb("dxn", [128, D], F32, pes)
                ss = k.sb("dss", [128, 1], F32, pes)
                rs = k.sb("drs", [128, 2], F32, pes)
                ptc = k.ps("dptc", [128, 8, 128], BF16, pes)
                po = [k.ps(f"dpo{h}", [128, 512], F32, pes) for h in range(2)]
                ptr = [k.ps(f"dptr{h}", [128, 4, 128], F32, pes) for h in range(2)]
                def d1a(tt):
                    i = tt % 2
                    for p in range(3):
                        k.dma('sp', nl[i][p][:], nl_t[p].v(nl_d[p][tt * 128:(tt + 1) * 128, :].rearrange("p (h e) -> p h e", h=4)))
                    k.dma('sp', yT[i][:, 0:6, :], yT_t[tt].v(yT_d[tt].rearrange("p (c t) -> p c t", c=6)))
                    k.dma('sp', xt[i][:], xsrc[tt][:])
                    k.tt('dve', nsum[:], nl[i][0][:], nl[i][1][:], ALU.add)
                    yield
                    k.tt('dve', nsum[:], nsum[:], nl[i][2][:], ALU.add)
                    yield
                    k.recip(rl[:], nsum[:, :, 64])
                    yield
                    k.tt('dve', ycb[:], nsum[:, :, 0:64], rl.v(rl.t[:, :].unsqueeze(2).to_broadcast([128, 4, 64])), ALU.mult)
                    yield
                    for j in range(2):
                        k.tr(ptc[:, j, :], ycb.v(ycb.t[:, 2 * j:2 * j + 2, :].rearrange("p h e -> p (h e)")), identb[:])
                    yield
                    k.copy('act', yT[i][:, 6:8, :], ptc[:, 0:2, :])
                    yield

                def d1b(tt):
                    i = tt % 2
                    for half in range(2):
                        for c in range(8):
                            k.mm(po[half][:], yT[i][:, c, :], wo[:, c, half * 512:(half + 1) * 512],
                                 start=(c == 0), stop=(c == 7))
                        yield
                    for half in range(2):
                        sl = slice(half * 512, (half + 1) * 512)
                        k.tt('dve', xo[i][:, sl], po[half][:], xt[i][:, sl], ALU.add)
                        yield
                    k.dma('sp', out_t[tt][:], xo[i][:])
                    yield

                for it_ in range(NT + 2):
                    g = []
                    if 0 <= it_ - 2 < NT:
                        t2 = it_ - 2
                        g.append(norm_gen(xo[t2 % 2], t2, a2, b2, sq, ss, rs, xn, ptr))
                    if 0 <= it_ - 1 < NT:
                        g.append(d1b(it_ - 1))
                    if it_ < NT:
                        g.append(d1a(it_))
                    rr(g)
                k.end_phase()

            groups = [(0, 6), (6, 6), (12, 5), (17, 5)] if 'F' in phases else []
            if groups:
                with ExitStack() as pes:
                    k.begin_phase()
                    wg = [k.sb(f"fwg{i}", [128, KC, 768], BF16, pes) for i in range(2)]
                    wu = [k.sb(f"fwu{i}", [128, KC, 768], BF16, pes) for i in range(2)]
                    wo2 = [k.sb(f"fwo{i}", [128, 6, D], BF16, pes) for i in range(2)]
                    actT = [k.sb(f"fact{i}", [128, 6, 512], BF16, pes) for i in range(2)]
                    sg = [k.sb(f"fsg{i}", [128, 512], F32, pes) for i in range(2)]
                    xt = [k.sb(f"fx{i}", [128, D], F32, pes) for i in range(2)]
                    xo = [k.sb(f"fxo{i}", [128, D], F32, pes) for i in range(2)]
                    tmp = k.sb("ftmp", [128, D], F32, pes)
                    pg = [k.ps(f"fpg{i}", [128, 512], F32, pes) for i in range(2)]
                    pu = [k.ps(f"fpu{i}", [128, 512], F32, pes) for i in range(2)]
                    po = [[k.ps(f"fpo{i}{h}", [128, 512], F32, pes) for h in range(2)] for i in range(2)]

                    def loadw(gi):
                        f0, nf = groups[gi]
                        j = gi % 2
                        for kc in range(KC):
                            k.dma('pool', wg[j][:, kc, 0:nf * 128], RO(wf_in[l, kc * 128:(kc + 1) * 128, f0 * 128:(f0 + nf) * 128]))
                            k.dma('pool', wu[j][:, kc, 0:nf * 128], RO(wf_in[l, kc * 128:(kc + 1) * 128, FH + f0 * 128:FH + (f0 + nf) * 128]))
                        for c in range(nf):
                            k.dma('pool', wo2[j][:, c, :], RO(wf_out[l, (f0 + c) * 128:(f0 + c + 1) * 128, :]))

                    def scalew(gi):
                        f0, nf = groups[gi]
                        j = gi % 2
                        k.tt('dve', wo2[j][:, 0:nf, :], wo2[j][:, 0:nf, :],
                             g2bc.v(g2bc.t[:, :].unsqueeze(1).to_broadcast([128, nf, D])), ALU.mult)

                    loadw(0)
                    it = 0
                    xi = 0
                    ab = 0
                    for gi, (f0, nf) in enumerate(groups):
                        if gi + 1 < len(groups):
                            loadw(gi + 1)
                        scalew(gi)
                        wj = gi % 2
                        for blk in range(NB):
                            hb = hT_b[blk]
                            a = actT[ab % 2]
                            ab += 1
                            for fi in range(nf):
                                j = it % 2
                                it += 1
                                for kc in range(KC):
                                    k.mm(pg[j][:], wg[wj][:, kc, fi * 128:(fi + 1) * 128], hb.v(hT.t[:, kc, blk * 512:(blk + 1) * 512]),
                                         start=(kc == 0), stop=(kc == KC - 1))
                                for kc in range(KC):
                                    k.mm(pu[j][:], wu[wj][:, kc, fi * 128:(fi + 1) * 128], hb.v(hT.t[:, kc, blk * 512:(blk + 1) * 512]),
                                         start=(kc == 0), stop=(kc == KC - 1))
                                k.act(sg[j][:], pg[j][:], AF.Silu)
                                k.tt('dve', a[:, fi, :], sg[j][:], pu[j][:], ALU.mult)
                            for t4 in range(4):
                                tt = blk * 4 + t4
                                i = xi % 2
                                xi += 1
                                k.dma('sp', xt[i][:], out_t[tt][:])
                                for half in range(2):
                                    for fi in range(nf):
                                        k.mm(po[i][half][:], a[:, fi, t4 * 128:(t4 + 1) * 128], wo2[wj][:, fi, half * 512:(half + 1) * 512],
                                             start=(fi == 0), stop=(fi == nf - 1))
                                for half in range(2):
                                    sl = slice(half * 512, (half + 1) * 512)
                                    k.tt('dve', xo[i][:, sl], po[i][half][:], xt[i][:, sl], ALU.add)
                                k.dma('sp', out_t[tt][:], xo[i][:])
                    k.end_phase()
        k.barrier()
        build.ninst = dict(k.ninst)
        build.retired = (getattr(k, 'retired', 0), getattr(k, 'nds', 0), len(k.own['pe']) + len(k.own['act']) + len(k.own['dve']) + len(k.own['pool']) + len(k.own['sp']))
        build.maxd = (max([c for _, c in k.dpool['sp']] + [0]) * 16, max([c for _, c in k.dpool['pool']] + [0]) * 16, k.nsem)
    return nc


def phase_B(k, l, T, hT, hT_b, RO, w_in, lng_in, lnb_in, wsT_in, bs_in, cst, identb, epsc, yT_t, yT_d):
    NT = T // 128
    c0 = 2056
    maskUi = cst[:, 5, :]
    with ExitStack() as pes:
        k.begin_phase()
        wb = k.sb("bw", [128, KC, 512], BF16, pes)
        k.load_w(wb, w_in[l, :, c0:c0 + 512], KC, 512, RO)
        lng = k.sb("blng", [128, 256], F32, pes)
        lnb = k.sb("blnb", [128, 256], F32, pes)
        wsf = k.sb("bwsf", [128, 4, 128], F32, pes)
        wsb = k.sb("bwsb", [128, 4, 128], BF16, pes)
        bs = k.sb("bbs", [128, 4], F32, pes)
        k.dma('sp', lng[:], RO(lng_in[l].partition_broadcast(128)))
        k.dma('sp', lnb[:], RO(lnb_in[l].partition_broadcast(128)))
        k.dma('sp', wsf[:], RO(wsT_in[l]))
        k.dma('sp', bs[:], RO(bs_in[l]))
        k.tt('dve', wsb[:], wsf[:], maskUi.tile.v(cst.t[:, 5:6, :].to_broadcast([128, 4, 128])), ALU.mult)
        gl = [k.sb(f"bgl{i}", [128, 512], F32, pes) for i in range(2)]
        st = k.sb("bst", [128, 6], F32, pes)
        mv = k.sb("bmv", [128, 2], F32, pes)
        rs = k.sb("brs", [128, 2], F32, pes)
        vn = k.sb("bvn", [128, 256], F32, pes)
        vn2 = k.sb("bvn2", [128, 256], F32, pes)
        vnb = [k.sb(f"bvnb{i}", [128, 256], BF16, pes) for i in range(2)]
        yb = [k.sb(f"byb{i}", [128, 256], BF16, pes) for i in range(2)]
        ybT = [k.sb(f"bybT{i}", [128, 2, 128], BF16, pes) for i in range(2)]
        pp = [k.ps(f"bpp{i}", [128, 512], F32, pes) for i in range(2)]
        psv = [k.ps(f"bpsv{i}", [128, 512], F32, pes) for i in range(2)]
        ptr = [k.ps(f"bptr{i}", [128, 8, 128], BF16, pes) for i in range(2)]
        def b1(tt):
            i = tt % 2
            hb = hT_b[tt // 4]
            for kc in range(KC):
                k.mm(pp[i][:], hb.v(hT.t[:, kc, tt * 128:(tt + 1) * 128]), wb[:, kc, :], start=(kc == 0), stop=(kc == KC - 1))
            yield
            k.act(gl[i][:], pp[i][:], AF.Gelu)
            yield
            k.op('dve', lambda g: g.bn_stats(out=st.t[:, 0:6], in_=gl[i].t[:, 256:512]), [gl[i][:]], [st[:]])
            k.op('dve', lambda g: g.bn_aggr(out=mv.t[:, 0:2], in_=st.t[:, 0:6]), [st[:]], [mv[:]])
            yield
            k.act(rs[:, 0:1], mv[:, 1:2], AF.Sqrt, bias=epsc[:, 0:1], scale=1.0)
            k.recip(rs[:, 1:2], rs[:, 0:1])
            yield
            k.ts('dve', vn[:], gl[i][:, 256:512], mv[:, 0:1], rs[:, 1:2], ALU.subtract, ALU.mult)
            yield
            k.tt('dve', vn2[:], vn[:], lng[:], ALU.mult)
            yield
            k.tt('dve', vnb[i][:], vn2[:], lnb[:], ALU.add)
            yield

        def b2(tt):
            i = tt % 2
            for g in range(4):
                k.mm(psv[i][:, g * 64:(g + 1) * 64], wsb[:, g, :], vnb[i][:, g * 64:(g + 1) * 64])
            yield
            for g in range(4):
                k.stt('dve', yb[i][:, g * 64:(g + 1) * 64], psv[i][:, g * 64:(g + 1) * 64], bs[:, g:g + 1],
                      gl[i][:, g * 64:(g + 1) * 64], ALU.add, ALU.mult)
                if g % 2 == 1:
                    yield
            for j in range(2):
                k.tr(ptr[i][:, j, :], yb[i][:, j * 128:(j + 1) * 128], identb[:])
            yield
            k.copy('act', ybT[i][:], ptr[i][:, 0:2, :])
            yield
            k.dma('sp', yT_t[tt].v(yT_d[tt][:, 512:768].rearrange("p (c t) -> p c t", c=2)), ybT[i][:])

        def rr(gens):
            gens = list(gens)
            while gens:
                for g in list(gens):
                    try:
                        next(g)
                    except StopIteration:
                        gens.remove(g)

        rr([b1(0)])
        for tt in range(NT):
            g = [b2(tt)]
            if tt + 1 < NT:
                g.append(b1(tt + 1))
            rr(g)
        k.end_phase()


def phase_C(k, l, T, hT, hT_b, RO, w_in, qnw_in, knw_in, rot_in, identb, maskCb, maskPb, epsc, nl_t, nl_d):
    c0 = 2568
    with ExitStack() as pes:
        k.begin_phase()
        wc = k.sb("cw", [128, KC, 2304], BF16, pes)
        k.load_w(wc, w_in[l, :, c0:c0 + 2304], KC, 2304, RO)
        nwq = k.sb("cnwq", [128, 64], F32, pes)
        nwk = k.sb("cnwk", [128, 64], F32, pes)
        nw = k.sb("cnw", [128, 8, 64], F32, pes)
        k.dma('sp', nwq[:], RO(qnw_in[l].partition_broadcast(128)))
        k.dma('sp', nwk[:], RO(knw_in[l].partition_broadcast(128)))
        for h in range(4):
            k.copy('dve', nw[:, h, :], nwq[:])
            k.copy('dve', nw[:, 4 + h, :], nwk[:])
        rot = [k.sb(f"crot{i}", [128, 32], F32, pes) for i in range(3)]
        sqt = [k.sb(f"csq{i}", [128, 512], F32, pes) for i in range(2)]
        ss = [k.sb(f"css{i}", [128, 8], F32, pes) for i in range(2)]
        rs = [k.sb(f"crs{i}", [128, 8], F32, pes) for i in range(2)]
        ri = [k.sb(f"cri{i}", [128, 8], F32, pes) for i in range(2)]
        qkf = k.sb("cqkf", [128, 8, 64], F32, pes)
        qkg = k.sb("cqkg", [128, 8, 64], F32, pes)
        ra = k.sb("cra", [128, 8, 16], F32, pes)
        rb = k.sb("crb", [128, 8, 16], F32, pes)
        qkb = k.sb("cqkb", [128, 8, 64], BF16, pes)
        v1 = [k.sb(f"cv1{i}", [128, 4, 80], BF16, pes) for i in range(4)]
        qT = [k.sb(f"cqT{i}", [64, 4, 128], BF16, pes) for i in range(2)]
        kT = [k.sb(f"ckT{i}", [64, 4, 128], BF16, pes) for i in range(3)]
        P0 = k.sb("cP0", [128, 4, 128], BF16, pes)
        P1 = k.sb("cP1", [128, 4, 128], BF16, pes)
        P0m = k.sb("cP0m", [128, 4, 128], BF16, pes)
        P1m = k.sb("cP1m", [128, 4, 128], BF16, pes)
        nlo = [k.sb(f"cnlo{i}", [128, 4, 65], F32, pes) for i in range(2)]
        pc0b = [k.ps(f"cpc0{i}", [128, 512], F32, pes) for i in range(2)]
        pc1 = k.ps("cpc1", [128, 512], F32, pes)
        ptr = k.ps("cptr", [128, 8, 128], BF16, pes)
        pS0 = k.ps("cpS0", [128, 4, 128], F32, pes)
        pS1 = k.ps("cpS1", [128, 4, 128], F32, pes)
        pO = k.ps("cpO", [128, 4, 128], F32, pes)
        for i in range(4):
            k.memset('pool', v1[i][:], 1.0)
        mC = maskCb.v(maskCb.t[:, :].unsqueeze(1).to_broadcast([128, 4, 128]))
        mP = maskPb.v(maskPb.t[:, :].unsqueeze(1).to_broadcast([128, 4, 128]))
        tiles = []
        for p, (window, d) in enumerate(PATTERNS):
            nb = (T // d) // 128
            for r in range(d):
                for n in range(nb):
                    start = 128 * n * d + r
                    tiles.append((p, d, n, start, start + 127 * d + 1))

        def s1a(i):
            p, d, n, start, stop_ = tiles[i]
            b0 = start // 512
            b1 = (stop_ - 1) // 512
            hbs = [hT_b[b] for b in range(b0, b1 + 1)]
            rt = rot[i % 3]
            pc0 = pc0b[i % 2]
            k.dma('sp', rt[:], RO(rot_in[start:stop_:d, :]))
            for kc in range(KC):
                lhs = TV(hbs[0], hT.t[:, kc, start:stop_:d], tuple(hbs[1:]))
                k.mm(pc0[:], lhs, wc[:, kc, p * 768:p * 768 + 512], start=(kc == 0), stop=(kc == KC - 1))
            yield
            for kc in range(KC):
                lhs = TV(hbs[0], hT.t[:, kc, start:stop_:d], tuple(hbs[1:]))
                k.mm(pc1[:, 0:256], lhs, wc[:, kc, p * 768 + 512:p * 768 + 768], start=(kc == 0), stop=(kc == KC - 1))
            k.act(sqt[i % 2][:], pc0[:], AF.Square)
            yield
            k.reduce('dve', ss[i % 2][:], sqt[i % 2].v(sqt[i % 2].t[:, :].rearrange("p (g e) -> p g e", g=8)), ALU.add)
            k.copy('act', v1[i % 4][:, :, 0:64], pc1.v(pc1.t[:, 0:256].rearrange("p (h e) -> p h e", h=4)))
            yield
            k.act(rs[i % 2][:], ss[i % 2][:], AF.Ln, bias=epsc[:, 0:1], scale=1.0 / 64)
            k.act(ri[i % 2][:], rs[i % 2][:], AF.Exp, scale=-0.5)
            yield

        def s1b(i):
            p, d, n, start, stop_ = tiles[i]
            rt = rot[i % 3]
            pc0 = pc0b[i % 2]
            k.tt('dve', qkf[:], pc0.v(pc0.t[:, :].rearrange("p (g e) -> p g e", g=8)),
                 ri[i % 2].v(ri[i % 2].t[:, :].unsqueeze(2).to_broadcast([128, 8, 64])), ALU.mult)
            yield
            k.tt('dve', qkg[:], qkf[:], nw[:], ALU.mult)
            yield
            C2 = rt.v(rt.t[:, 0:16].unsqueeze(1).to_broadcast([128, 8, 16]))
            Sa = rt.v(rt.t[:, 16:24].unsqueeze(1).to_broadcast([128, 8, 8]))
            Sb = rt.v(rt.t[:, 24:32].unsqueeze(1).to_broadcast([128, 8, 8]))
            k.tt('dve', ra[:], qkg[:, :, 0:16], C2, ALU.mult)
            k.tt('dve', rb[:, :, 0:8], qkg[:, :, 8:16], Sa, ALU.mult)
            k.copy('pool', qkb[:, :, 16:64], qkg[:, :, 16:64])
            yield
            k.tt('dve', rb[:, :, 8:16], qkg[:, :, 0:8], Sb, ALU.mult)
            k.tt('dve', qkb[:, :, 0:16], ra[:], rb[:], ALU.add)
            yield
            for j in range(8):
                k.tr(ptr[0:64, j, :], qkb[:, j, :], identb[:])
            yield
            k.copy('act', qT[i % 2][:], ptr[0:64, 0:4, :])
            k.copy('act', kT[i % 3][:], ptr[0:64, 4:8, :])
            yield

        def s2(i):
            p, d, n, start, stop_ = tiles[i]
            cur = i % 3
            prev = (i - 1) % 3
            vc = i % 4
            vp = (i - 1) % 4
            q_ = qT[i % 2]
            for h in range(4):
                k.mm(pS0[:, h, :], kT[cur][:, h, :], q_[:, h, :])
            yield
            k.act(P0[:], pS0[:], AF.Exp, scale=0.125)
            if n > 0:
                for h in range(4):
                    k.mm(pS1[:, h, :], kT[prev][:, h, :], q_[:, h, :])
            yield
            k.tt('dve', P0m[:], P0[:], mC, ALU.mult)
            if n > 0:
                k.act(P1[:], pS1[:], AF.Exp, scale=0.125)
                yield
                k.tt('dve', P1m[:], P1[:], mP, ALU.mult)
            yield
            for h in range(4):
                if n > 0:
                    k.mm(pO[:, h, 0:65], P1m[:, h, :], v1[vp][:, h, 0:65], start=True, stop=False)
                k.mm(pO[:, h, 0:65], P0m[:, h, :], v1[vc][:, h, 0:65], start=(n == 0), stop=True)
            yield
            o_ = nlo[i % 2]
            k.copy('act', o_[:], pO[:, :, 0:65])
            yield
            k.dma('sp', nl_t[p].v(nl_d[p][start:stop_:d, :].rearrange("p (h e) -> p h e", h=4)), o_[:])
            yield

        def rr(gens):
            gens = list(gens)
            while gens:
                for g in list(gens):
                    try:
                        next(g)
                    except StopIteration:
                        gens.remove(g)

        NTL = len(tiles)
        for i in range(NTL + 2):
            g = []
            if 0 <= i - 2 < NTL:
                g.append(s2(i - 2))
            if 0 <= i - 1 < NTL:
                g.append(s1b(i - 1))
            if i < NTL:
                g.append(s1a(i))
            rr(g)
        k.end_phase()


def phase_A(k, l, T, hT, hT_b, RO, w_in, convw_in, a_log, dt_b, onw_in, cst, identb, epsc, yT_t, yT_d):
    NT = T // 128
    NB = T // 512
    ident = cst[:, 0, :]
    Um = cst[:, 1, :]
    SLm = cst[:, 2, :]
    ones = cst[:, 3, :]
    maskS = cst[:, 4, :]
    maskUi = cst[:, 5, :]
    SCALE = 128.0 ** -0.5
    with ExitStack() as pes:
        k.begin_phase()
        wa = k.sb("aw", [128, KC, 2056], BF16, pes)
        k.load_w(wa, w_in[l, :, 0:2056], KC, 2056, RO)
        cw = k.sb("acw", [128, 12, 4], F32, pes)
        k.dma('sp', cw[:], RO(convw_in[l]))
        alog = k.sb("aalog", [128, 4], F32, pes)
        dtb = k.sb("adtb", [128, 4], F32, pes)
        nea = k.sb("anea", [128, 4], F32, pes)
        onw = k.sb("aonw", [128, 128], F32, pes)
        k.dma('sp', alog[:], RO(a_log[l].partition_broadcast(128)))
        k.dma('sp', dtb[:], RO(dt_b[l].partition_broadcast(128)))
        k.dma('sp', onw[:], RO(onw_in[l].partition_broadcast(128)))
        k.act(nea[:], alog[:], AF.Exp)
        k.ts('dve', nea[:], nea[:], -1.0, None, ALU.mult)
        halo = k.sb("ahalo", [128, 12, 3], F32, pes)
        k.memset('dve', halo[:], 0.0)
        raw3 = [k.sb(f"araw{i}", [128, 515], F32, pes) for i in range(3)]
        acc3 = [k.sb(f"aacc{i}", [128, 512], F32, pes) for i in range(3)]
        qkv = k.sb("aqkv", [128, 12, 512], F32, pes)
        S = [[k.sb(f"aS{h}{i}", [128, 128], F32, pes) for i in range(2)] for h in range(4)]
        for h in range(4):
            k.memset('dve', S[h][0][:], 0.0)
        szb = [k.sb(f"asz{i}", [128, 512], F32, pes) for i in range(3)]
        smb = [k.sb(f"asm{i}", [128, 64], F32, pes) for i in range(2)]
        oallb = [k.sb(f"aoall{i}", [128, 4, 128], F32, pes) for i in range(2)]
        osq = k.sb("aosq", [128, 128], F32, pes)
        oss = k.sb("aoss", [128, 4], F32, pes)
        ors = k.sb("aors", [128, 8], F32, pes)
        ot = k.sb("aot", [128, 128], F32, pes)
        ya = k.sb("aya", [128, 4, 128], F32, pes)
        yaT = [k.sb(f"ayaT{i}", [128, 4, 128], BF16, pes) for i in range(2)]
        names = ["kbg", "kdec", "vb", "GU", "eD", "eDT", "tA", "N", "NT", "tQ", "aqk", "TTa", "TTb",
                 "Pa", "PTa", "Pb", "PTb", "u", "wT", "vnew", "o2"]
        W = [{n: k.sb(f"a{n}{s}", [128, 128], F32, pes) for n in names} for s in range(4)]
        BK = [k.ps(f"abk{i}", [128, 512], F32, pes) for i in range(8)]
        pq = [BK[0], BK[1]]
        pz = BK[6]
        pm = BK[7]
        pty = BK[7]

        def sc(j):
            return sm[:, j:j + 1]

        pwi = [0]

        def nxt():
            t = pw[pwi[0] % 3]
            s = (pwi[0] // 3) % 4
            pwi[0] += 1
            return t[:, s, :]

        it = 0
        for blk in range(NB):
            hb = hT_b[blk]
            tok = slice(blk * 512, (blk + 1) * 512)
            def conv_gen(fc, n_):
                j = n_ % 2
                rw = raw3[n_ % 3]
                ac = acc3[n_ % 3]
                for kc in range(KC):
                    k.mm(pq[j][:], wa[:, kc, fc * 128:(fc + 1) * 128], hb.v(hT.t[:, kc, tok]), start=(kc == 0), stop=(kc == KC - 1))
                k.copy('pool', rw[:, 0:3], halo[:, fc, :])
                k.copy('act', rw[:, 3:515], pq[j][:])
                k.copy('pool', halo[:, fc, :], rw[:, 512:515])
                yield
                k.ts('dve', ac[:], rw[:, 0:512], cw[:, fc, 0:1], None, ALU.mult)
                k.stt('dve', ac[:], rw[:, 1:513], cw[:, fc, 1:2], ac[:], ALU.mult, ALU.add)
                k.stt('dve', ac[:], rw[:, 2:514], cw[:, fc, 2:3], ac[:], ALU.mult, ALU.add)
                k.stt('dve', ac[:], rw[:, 3:515], cw[:, fc, 3:4], ac[:], ALU.mult, ALU.add)
                yield
                k.act(qkv[:, fc, :], ac[:], AF.Silu)
                yield

            def l2_gen(fc, n_):
                j = n_ % 2
                ac = acc3[n_ % 3]
                rw = raw3[n_ % 3]
                k.act(ac[:], qkv[:, fc, :], AF.Square)
                yield
                k.mm(pq[j][:], ones, ac[:])
                yield
                k.act(rw[:, 0:512], pq[j][:], AF.Ln, bias=epsc[:, 0:1], scale=1.0)
                k.act(ac[:], rw[:, 0:512], AF.Exp, scale=-0.5)
                yield
                k.tt('dve', qkv[:, fc, :], qkv[:, fc, :], ac[:], ALU.mult)
                yield

            def stagger(mk, n):
                act_ = []
                i_ = 0
                while act_ or i_ < n:
                    for g in list(act_):
                        try:
                            next(g)
                        except StopIteration:
                            act_.remove(g)
                    if i_ < n:
                        g = mk(i_)
                        next(g)
                        act_.append(g)
                        i_ += 1

            stagger(lambda i_: conv_gen(i_, i_), 12)
            stagger(lambda i_: l2_gen(i_, i_), 8)
            def pre_gen(tt):
                sm = smb[tt % 2]
                sz = szb[tt % 3]
                hts = slice(tt * 128, (tt + 1) * 128)
                for kc in range(KC):
                    k.mm(pz[:], hb.v(hT.t[:, kc, hts]), wa[:, kc, 1536:2048], start=(kc == 0), stop=(kc == KC - 1))
                yield
                for kc in range(KC):
                    k.mm(pm[:, 0:8], hb.v(hT.t[:, kc, hts]), wa[:, kc, 2048:2056], start=(kc == 0), stop=(kc == KC - 1))
                k.act(sz[:], pz[:], AF.Silu)
                yield
                k.tt('dve', sm[:, 0:4], pm[:, 0:4], dtb[:], ALU.add)
                k.act(sm[:, 16:20], pm[:, 4:8], AF.Exp, scale=-1.0)
                yield
                k.act(sm[:, 4:8], sm[:, 0:4], AF.Exp)
                k.ts('dve', sm[:, 16:20], sm[:, 16:20], 1.0, None, ALU.add)
                yield
                k.act(sm[:, 8:12], sm[:, 4:8], AF.Ln, bias=1.0)
                k.recip(sm[:, 20:24], sm[:, 16:20])
                yield
                k.tt('dve', sm[:, 12:16], sm[:, 8:12], nea[:], ALU.mult)
                k.ts('dve', sm[:, 24:28], sm[:, 20:24], -1.0, None, ALU.mult)
                yield
                k.mm(pm[:, 8:12], Um, sm[:, 12:16])
                k.mm(pm[:, 12:16], ones, sm[:, 12:16])
                yield
                k.copy('dve', sm[:, 28:32], pm[:, 8:12])
                k.copy('dve', sm[:, 36:40], pm[:, 12:16])
                yield
                k.act(sm[:, 32:36], sm[:, 28:32], AF.Exp)
                k.act(sm[:, 40:44], sm[:, 36:40], AF.Exp)
                k.tt('dve', sm[:, 44:48], sm[:, 36:40], sm[:, 28:32], ALU.subtract)
                yield
                k.act(sm[:, 48:52], sm[:, 44:48], AF.Exp)
                k.tt('dve', sm[:, 52:56], sm[:, 20:24], sm[:, 32:36], ALU.mult)
                k.ts('dve', sm[:, 56:60], sm[:, 32:36], SCALE, None, ALU.mult)
                yield

            def epi_gen(tt):
                oall = oallb[tt % 2]
                sz = szb[tt % 3]
                k.memset('dve', oss[:], 0.0)
                for h in range(4):
                    k.act(osq[:], oall[:, h, :], AF.Square, accum=oss[:, h:h + 1])
                    yield
                k.act(ors[:, 0:4], oss[:], AF.Ln, bias=epsc[:, 0:1], scale=1.0 / 128)
                k.act(ors[:, 4:8], ors[:, 0:4], AF.Exp, scale=-0.5)
                yield
                for h in range(4):
                    k.stt('dve', ot[:], oall[:, h, :], ors[:, 4 + h:5 + h], onw[:], ALU.mult, ALU.mult)
                    k.tt('dve', ya[:, h, :], ot[:], sz[:, h * 128:(h + 1) * 128], ALU.mult)
                    yield
                for h in range(4):
                    k.tr(pty[:, h * 128:(h + 1) * 128], ya[:, h, :], ident)
                yield
                i = tt % 2
                k.copy('act', yaT[i][:], pty.v(pty.t[:, :].rearrange("p (h t) -> p h t", h=4)))
                yield
                k.dma('sp', yT_t[tt].v(yT_d[tt][:, 0:512].rearrange("p (c t) -> p c t", c=4)), yaT[i][:])

            def head_gen(h, tt, cs, sm, oall):
                w = W[h]

                def sc(j):
                    return sm[:, j:j + 1]
                qT_ = qkv[:, h, cs]
                kT_ = qkv[:, 4 + h, cs]
                vT_ = qkv[:, 8 + h, cs]
                Sold = S[h][tt % 2]
                Snew = S[h][(tt + 1) % 2]
                cnt = [0]

                def nx():
                    c = cnt[0]
                    cnt[0] += 1
                    bk = BK[2 * h + (c % 2)]
                    o = ((c // 2) % 4) * 128
                    return bk[:, o:o + 128]
                p = nx()

                def post(p=p):
                    k.ts('dve', w["kbg"][:], p, sc(52 + h), None, ALU.mult)
                    k.amul(w["kdec"][:], p, sc(48 + h))
                yield [('tr', p, kT_, ident)], post
                p = nx()

                def post(p=p):
                    k.amul(w["vb"][:], p, sc(20 + h))
                    k.ts('dve', w["GU"][:], Um, sc(12 + h), None, ALU.mult)
                yield [('tr', p, vT_, ident)], post
                p = nx()

                def post(p=p):
                    k.act(w["eD"][:], p, AF.Exp)
                    k.tt('pool', w["tA"][:], w["eD"][:], maskS, ALU.mult)
                yield [('mm', p, w["GU"][:], SLm)], post
                p = nx()

                def post(p=p):
                    k.act(w["eDT"][:], p, AF.Exp)
                    k.tt('pool', w["tQ"][:], w["eDT"][:], maskUi, ALU.mult)
                yield [('mm', p, SLm, w["GU"][:])], post
                p = nx()

                def post(p=p):
                    k.stt('dve', w["N"][:], p, sc(24 + h), w["tA"][:], ALU.mult, ALU.mult)
                yield [('mm', p, kT_, kT_)], post
                p = nx()

                def post(p=p):
                    k.copy('act', w["NT"][:], p)
                    k.tt('pool', w["TTa"][:], w["NT"][:], ident, ALU.add)
                yield [('tr', p, w["N"][:], ident)], post
                p = nx()

                def post(p=p):
                    k.stt('dve', w["aqk"][:], p, SCALE, w["tQ"][:], ALU.mult, ALU.mult)
                yield [('mm', p, kT_, qT_)], post
                TT, TTo = w["TTa"], w["TTb"]
                P_, PT_ = w["N"], w["NT"]
                Pn, PTn = w["Pa"], w["PTa"]
                for s_ in range(6):
                    p = nx()
                    ops = [('mm', p, PT_[:], P_[:])]
                    p2 = None
                    if s_ < 5:
                        p2 = nx()
                        ops.append(('mm', p2, P_[:], PT_[:]))

                    def post(p=p, p2=p2, Pn=Pn, PTn=PTn, s_=s_):
                        k.copy('act', Pn[:], p)
                        if p2 is not None:
                            k.copy('dve', PTn[:], p2)
                    yield ops, post
                    p3 = nx()

                    def post(p3=p3, TT=TT, TTo=TTo):
                        k.tt('dve', TTo[:], p3, TT[:], ALU.add)
                    yield [('mm', p3, Pn[:], TT[:])], post
                    TT, TTo = TTo, TT
                    P_, PT_ = Pn, PTn
                    if s_ % 2 == 0:
                        Pn, PTn = w["Pb"], w["PTb"]
                    else:
                        Pn, PTn = w["Pa"], w["PTa"]
                p = nx()
                p2 = nx()

                def post(p=p, p2=p2):
                    k.copy('act', w["u"][:], p)
                    k.copy('dve', w["wT"][:], p2)
                yield [('mm', p, TT[:], w["vb"][:]), ('mm', p2, w["kbg"][:], TT[:])], post
                p = nx()
                p1 = nx()

                def post(p=p):
                    k.stt('dve', w["vnew"][:], p, -1.0, w["u"][:], ALU.mult, ALU.add)
                yield [('mm', p, w["wT"][:], Sold[:]), ('mm', p1, qT_, Sold[:])], post
                p2 = nx()
                p3 = nx()

                def post(p1=p1, p2=p2, p3=p3):
                    k.copy('act', w["o2"][:], p2)
                    k.stt('dve', Snew[:], Sold[:], sc(40 + h), p3, ALU.mult, ALU.add)
                    k.stt('dve', oall[:, h, :], p1, sc(56 + h), w["o2"][:], ALU.mult, ALU.add)
                yield [('mm', p2, w["aqk"][:], w["vnew"][:]), ('mm', p3, w["kdec"][:], w["vnew"][:])], post


            def run_tile(tt, side):
                cs = slice((tt % 4) * 128, (tt % 4 + 1) * 128)
                groups = [[head_gen(h, tt, cs, smb[tt % 2], oallb[tt % 2]) for h in (0, 1)],
                          [head_gen(h, tt, cs, smb[tt % 2], oallb[tt % 2]) for h in (2, 3)]]
                side = list(side)
                alive = True
                while alive:
                    alive = False
                    for grp in groups:
                        items = []
                        for g in list(grp):
                            try:
                                items.append(next(g))
                            except StopIteration:
                                grp.remove(g)
                        if not items:
                            continue
                        alive = True
                        k.pe_batch([d_ for ops, _ in items for d_ in ops])
                        for _, post in items:
                            post()
                    for g in list(side):
                        try:
                            next(g)
                        except StopIteration:
                            side.remove(g)
                for g in side:
                    for _ in g:
                        pass

            t0_ = blk * 4
            for _ in pre_gen(t0_):
                pass
            for t4 in range(4):
                tt = t0_ + t4
                side = []
                if t4 + 1 < 4:
                    side.append(pre_gen(tt + 1))
                if t4 > 0:
                    side.append(epi_gen(tt - 1))
                run_tile(tt, side)
            for _ in epi_gen(t0_ + 3):
                pass
        k.end_phase()


def _consts():
    i = np.arange(128)
    c = np.zeros((128, 8, 128), np.float32)
    c[:, 0, :] = np.eye(128)
    c[:, 1, :] = (i[:, None] <= i[None, :])
    c[:, 2, :] = (i[:, None] > i[None, :])
    c[:, 3, :] = 1.0
    c[:, 4, :] = (i[:, None] > i[None, :])
    c[:, 5, :] = (i[:, None] <= i[None, :])
    c[:, 6, :] = (i[:, None] >= i[None, :])
    return c


def _rot(T):
    inv = (np.float32(ROPE_THETA) ** (-np.arange(0, 16, 2, dtype=np.float32) / np.float32(16))).astype(np.float32)
    ang = (np.arange(T, dtype=np.float32)[:, None] * inv[None, :]).astype(np.float32)
    cos, sin = np.cos(ang).astype(np.float32), np.sin(ang).astype(np.float32)
    return np.concatenate([cos, cos, -sin, sin], axis=1).astype(np.float32)


def make_in_maps(inputs, T, depth, ncores):
    f = lambda a: np.ascontiguousarray(np.asarray(a, dtype=np.float32))
    shared = {
        "w_mod": f(inputs["w_mod"]), "b_mod": f(inputs["b_mod"]),
        "mix_norm_w": f(np.asarray(inputs["mix_norm_w"]).reshape(depth, KC, 128).transpose(0, 2, 1)),
        "ffn_norm_w": f(np.asarray(inputs["ffn_norm_w"]).reshape(depth, KC, 128).transpose(0, 2, 1)),
        "w_in": f(inputs["w_in"]), "w_out": f(inputs["w_out"]),
        "dn_conv_w": f(np.asarray(inputs["dn_conv_w"]).transpose(0, 2, 1).reshape(depth, 12, 128, 4).transpose(0, 2, 1, 3)),
        "dn_a_log": f(inputs["dn_a_log"]), "dn_dt_bias": f(inputs["dn_dt_bias"]),
        "dn_out_norm_w": f(inputs["dn_out_norm_w"]),
        "gm_ln_g": f(inputs["gm_ln_g"]), "gm_ln_b": f(inputs["gm_ln_b"]),
        "gm_w_sT": f(np.asarray(inputs["gm_w_s"]).transpose(0, 3, 1, 2)),
        "gm_b_s": f(np.asarray(inputs["gm_b_s"]).transpose(0, 2, 1)),
        "sw_q_norm_w": f(inputs["sw_q_norm_w"]), "sw_k_norm_w": f(inputs["sw_k_norm_w"]),
        "w_ffn_in": f(inputs["w_ffn_in"]), "w_ffn_out": f(inputs["w_ffn_out"]),
        "consts": _consts(), "rot": _rot(T),
    }
    x = np.asarray(inputs["x"], dtype=np.float32)
    c = np.asarray(inputs["c"], dtype=np.float32)
    maps = []
    for b in range(ncores):
        m = dict(shared)
        m["x"] = np.ascontiguousarray(x[b])
        m["c"] = np.ascontiguousarray(c[b].reshape(KC, 128).T)
        maps.append(m)
    return maps


def kernel(**inputs):
    nc = build(SEQ, DEPTH, False)
    maps = make_in_maps(inputs, SEQ, DEPTH, NCORES)
    res = run_bass_kernel_spmd(nc, maps, core_ids=list(range(NCORES)))
    return np.stack([np.asarray(r["out"], dtype=np.float32) for r in res.results], axis=0)
```
